# Optimizing a Trainium2 kernel written in Bass

```python
import functools
import jax, jax.numpy as jnp
from jax import lax
import numpy as np

D_MODEL = 1024
BATCH = 8
SEQ = 2048
DEPTH = 2
DEC_BATCH = 128
DEC_SEQ = 8
PAST_LEN = 16384
PAGE_SIZE = 128

N_POOL_LAYERS = (DEPTH + 1) // 2
N_GDN_LAYERS = DEPTH // 2

POOL_WINDOWS = (2, 4, 8, 16)
N_POOL_GROUPS = len(POOL_WINDOWS)
POOL_GROUP_DIM = D_MODEL // N_POOL_GROUPS
POOL_BUF = max(POOL_WINDOWS) - 1

GDN_K_HEADS = 8
GDN_V_HEADS = 16
GDN_HEAD_K = 128
GDN_HEAD_V = 128
GDN_QK_DIM = GDN_K_HEADS * GDN_HEAD_K
GDN_V_DIM = GDN_V_HEADS * GDN_HEAD_V
GDN_CONV_DIM = 2 * GDN_QK_DIM + GDN_V_DIM
GDN_IN_DIM = GDN_CONV_DIM + GDN_V_DIM + 2 * GDN_V_HEADS
CONV_WIDTH = 4
CHUNK = 64

D_FF = -(-8 * D_MODEL // (3 * 256)) * 256

EPS = 1e-6

kernel_name = 'hybrid_pool_gdn_decoder_step'


def rms_norm(x, gain):
    xf = x.astype(jnp.float32)
    y = xf * lax.rsqrt(jnp.mean(xf * xf, axis=-1, keepdims=True) + EPS)
    return (y * gain.astype(jnp.float32)).astype(x.dtype)


def l2norm(x):
    xf = x.astype(jnp.float32)
    return xf * lax.rsqrt(jnp.sum(xf * xf, axis=-1, keepdims=True) + EPS)


def swiglu(h, w_in, w_out):
    gu = h @ w_in
    return (jax.nn.silu(gu[..., :D_FF]) * gu[..., D_FF:]) @ w_out


def pool_mixer(h, buf, w_grp, scale, n_past):
    B, L, _ = h.shape
    hp = jnp.concatenate([buf.astype(h.dtype), h], axis=1)
    hf = hp.astype(jnp.float32)
    cs = jnp.cumsum(hf, axis=1)
    cs = jnp.concatenate([jnp.zeros_like(cs[:, :1]), cs], axis=1)
    cur = hf[:, POOL_BUF:]
    t = jnp.arange(L)
    diffs = []
    for gi, win in enumerate(POOL_WINDOWS):
        c0, c1 = gi * POOL_GROUP_DIM, (gi + 1) * POOL_GROUP_DIM
        total = (cs[:, POOL_BUF + 1:POOL_BUF + 1 + L, c0:c1]
                 - cs[:, POOL_BUF + 1 - win:POOL_BUF + 1 - win + L, c0:c1])
        count = jnp.minimum(win, t + 1 + n_past).astype(jnp.float32)[None, :, None]
        diffs.append(total / count - cur[..., c0:c1])
    d = jnp.stack(diffs, axis=2)
    y = jnp.einsum('blgc,gce->blge', d, w_grp.astype(jnp.float32)).reshape(B, L, D_MODEL)
    y = y * scale.astype(jnp.float32)
    return y.astype(h.dtype), hp[:, L:]


def short_conv(u, buf, w):
    L = u.shape[1]
    up = jnp.concatenate([buf.astype(u.dtype), u], axis=1)
    out = up[:, 0:L] * w[0]
    for tap in range(1, CONV_WIDTH):
        out = out + up[:, tap:tap + L] * w[tap]
    return jax.nn.silu(out), up[:, L:]


def gated_delta_chunked(q, k, v, g, beta, s0):
    B, L, H, _ = q.shape
    n = -(-L // CHUNK)
    pad = n * CHUNK - L

    def blocks(x):
        x = jnp.pad(x, [(0, 0), (0, pad)] + [(0, 0)] * (x.ndim - 2))
        x = x.reshape((B, n, CHUNK) + x.shape[2:])
        return jnp.moveaxis(x, 3, 1)

    q, k, v, g, beta = blocks(q), blocks(k), blocks(v), blocks(g), blocks(beta)
    gc = jnp.cumsum(g, axis=-1)
    idx = jnp.arange(CHUNK)
    causal = idx[:, None] >= idx[None, :]
    strict = idx[:, None] > idx[None, :]
    decay = jnp.exp(jnp.where(causal, gc[..., :, None] - gc[..., None, :], -jnp.inf))
    kb = k * beta[..., None]
    a_mat = jnp.where(strict, jnp.einsum('bhnid,bhnjd->bhnij', kb, k) * decay, 0.0)
    t_mat = a_mat + jnp.eye(CHUNK, dtype=jnp.float32)
    solve = functools.partial(lax.linalg.triangular_solve, left_side=True, lower=True,
                              unit_diagonal=True)
    u = solve(t_mat, v * beta[..., None])
    w = solve(t_mat, kb * jnp.exp(gc)[..., None])
    qk = jnp.einsum('bhnid,bhnjd->bhnij', q, k) * decay
    q_dec = q * jnp.exp(gc)[..., None]
    k_dec = k * jnp.exp(gc[..., -1:] - gc)[..., None]
    g_last = jnp.exp(gc[..., -1])
    xs = tuple(jnp.moveaxis(a, 2, 0) for a in (u, w, qk, q_dec, k_dec, g_last))

    def step(S, inp):
        u_i, w_i, qk_i, qd_i, kd_i, gl_i = inp
        v_new = u_i - jnp.einsum('bhck,bhkv->bhcv', w_i, S)
        o_i = jnp.einsum('bhck,bhkv->bhcv', qd_i, S) + jnp.einsum('bhcs,bhsv->bhcv', qk_i, v_new)
        S = S * gl_i[..., None, None] + jnp.einsum('bhck,bhcv->bhkv', kd_i, v_new)
        return S, o_i

    s_fin, o = lax.scan(step, s0, xs)
    o = jnp.transpose(o, (1, 0, 3, 2, 4)).reshape(B, n * CHUNK, H, o.shape[-1])[:, :L]
    return o, s_fin


def gdn_mixer(h, conv_buf, s0, w_in, conv_w, a_log, dt_bias, o_norm, w_out):
    B, L, _ = h.shape
    proj = h @ w_in
    qkv = proj[..., :GDN_CONV_DIM]
    z = proj[..., GDN_CONV_DIM:GDN_CONV_DIM + GDN_V_DIM]
    b = proj[..., GDN_CONV_DIM + GDN_V_DIM:GDN_CONV_DIM + GDN_V_DIM + GDN_V_HEADS]
    a = proj[..., GDN_CONV_DIM + GDN_V_DIM + GDN_V_HEADS:]
    qkv_c, new_conv = short_conv(qkv, conv_buf, conv_w)
    rep = GDN_V_HEADS // GDN_K_HEADS
    q = l2norm(qkv_c[..., :GDN_QK_DIM].reshape(B, L, GDN_K_HEADS, GDN_HEAD_K)) * (GDN_HEAD_K ** -0.5)
    k = l2norm(qkv_c[..., GDN_QK_DIM:2 * GDN_QK_DIM].reshape(B, L, GDN_K_HEADS, GDN_HEAD_K))
    q = jnp.repeat(q, rep, axis=2)
    k = jnp.repeat(k, rep, axis=2)
    v = qkv_c[..., 2 * GDN_QK_DIM:].reshape(B, L, GDN_V_HEADS, GDN_HEAD_V).astype(jnp.float32)
    beta = jax.nn.sigmoid(b.astype(jnp.float32))
    g = -jnp.exp(a_log.astype(jnp.float32)) * jax.nn.softplus(a.astype(jnp.float32) + dt_bias.astype(jnp.float32))
    o, s_new = gated_delta_chunked(q, k, v, g, beta, s0.astype(jnp.float32))
    zf = z.reshape(B, L, GDN_V_HEADS, GDN_HEAD_V).astype(jnp.float32)
    o = rms_norm(o, o_norm) * jax.nn.silu(zf)
    y = o.reshape(B, L, GDN_V_DIM).astype(h.dtype) @ w_out
    return y, new_conv, s_new.astype(s0.dtype)


def trunk(x, pool_bufs, conv_bufs, rec_states, n_past_pool, norm_mix_pre, norm_mix_post,
          norm_ffn_pre, norm_ffn_post, pool_w, pool_scale, gdn_w_in, gdn_conv_w, gdn_a_log,
          gdn_dt_bias, gdn_o_norm, gdn_w_out, ffn_w_in, ffn_w_out):
    new_pool, new_conv, new_rec = [], [], []
    for i in range(DEPTH):
        j = i // 2
        h = rms_norm(x, norm_mix_pre[i])
        if i % 2 == 0:
            m, nb = pool_mixer(h, pool_bufs[j], pool_w[j], pool_scale[j], n_past_pool)
            new_pool.append(nb)
        else:
            m, nc, ns = gdn_mixer(h, conv_bufs[j], rec_states[j], gdn_w_in[j], gdn_conv_w[j],
                                  gdn_a_log[j], gdn_dt_bias[j], gdn_o_norm[j], gdn_w_out[j])
            new_conv.append(nc)
            new_rec.append(ns)
        x = x + rms_norm(m, norm_mix_post[i])
        h = rms_norm(x, norm_ffn_pre[i])
        x = x + rms_norm(swiglu(h, ffn_w_in[i], ffn_w_out[i]), norm_ffn_post[i])
    return x, jnp.stack(new_pool), jnp.stack(new_conv), jnp.stack(new_rec)


def setup_inputs(seed: int = 0) -> dict:
    key = jax.random.key(seed)
    ks = jax.random.split(key, 20)
    f32 = jnp.float32
    nrm = lambda k, s: jax.random.normal(k, s, f32)
    dt = jnp.exp(jax.random.uniform(ks[13], (N_GDN_LAYERS, GDN_V_HEADS), f32,
                                    jnp.log(0.001), jnp.log(0.1)))
    return {
        'x_prompt': nrm(ks[0], (BATCH, SEQ, D_MODEL)),
        'x_sample': nrm(ks[1], (DEC_BATCH, DEC_SEQ, D_MODEL)),
        'state_pool': nrm(ks[2], (N_POOL_LAYERS, DEC_BATCH, POOL_BUF, D_MODEL)),
        'state_gdn_conv': nrm(ks[3], (N_GDN_LAYERS, DEC_BATCH, CONV_WIDTH - 1, GDN_CONV_DIM)),
        'state_gdn_rec': 0.1 * nrm(ks[4], (N_GDN_LAYERS, DEC_BATCH, GDN_V_HEADS, GDN_HEAD_K, GDN_HEAD_V)),
        'norm_mix_pre': 1.0 + 0.05 * nrm(ks[5], (DEPTH, D_MODEL)),
        'norm_mix_post': 1.0 + 0.05 * nrm(ks[6], (DEPTH, D_MODEL)),
        'norm_ffn_pre': 1.0 + 0.05 * nrm(ks[7], (DEPTH, D_MODEL)),
        'norm_ffn_post': 1.0 + 0.05 * nrm(ks[8], (DEPTH, D_MODEL)),
        'pool_w': nrm(ks[9], (N_POOL_LAYERS, N_POOL_GROUPS, POOL_GROUP_DIM, POOL_GROUP_DIM)) * POOL_GROUP_DIM ** -0.5,
        'pool_scale': 1.0 + 0.1 * nrm(ks[10], (N_POOL_LAYERS, D_MODEL)),
        'gdn_w_in': nrm(ks[11], (N_GDN_LAYERS, D_MODEL, GDN_IN_DIM)) * D_MODEL ** -0.5,
        'gdn_conv_w': nrm(ks[12], (N_GDN_LAYERS, CONV_WIDTH, GDN_CONV_DIM)) * CONV_WIDTH ** -0.5,
        'gdn_a_log': jnp.log(jax.random.uniform(ks[14], (N_GDN_LAYERS, GDN_V_HEADS), f32, 1.0, 16.0)),
        'gdn_dt_bias': dt + jnp.log(-jnp.expm1(-dt)),
        'gdn_o_norm': 1.0 + 0.05 * nrm(ks[15], (N_GDN_LAYERS, GDN_HEAD_V)),
        'gdn_w_out': nrm(ks[16], (N_GDN_LAYERS, GDN_V_DIM, D_MODEL)) * GDN_V_DIM ** -0.5,
        'ffn_w_in': nrm(ks[17], (DEPTH, D_MODEL, 2 * D_FF)) * D_MODEL ** -0.5,
        'ffn_w_out': nrm(ks[18], (DEPTH, D_FF, D_MODEL)) * D_FF ** -0.5,
    }


def reference(x_prompt, x_sample, state_pool, state_gdn_conv, state_gdn_rec, norm_mix_pre,
              norm_mix_post, norm_ffn_pre, norm_ffn_post, pool_w, pool_scale, gdn_w_in,
              gdn_conv_w, gdn_a_log, gdn_dt_bias, gdn_o_norm, gdn_w_out, ffn_w_in, ffn_w_out):
    bp = x_prompt.shape[0]
    zero_pool = jnp.zeros((N_POOL_LAYERS, bp, POOL_BUF, D_MODEL), x_prompt.dtype)
    zero_conv = jnp.zeros((N_GDN_LAYERS, bp, CONV_WIDTH - 1, GDN_CONV_DIM), x_prompt.dtype)
    zero_rec = jnp.zeros((N_GDN_LAYERS, bp, GDN_V_HEADS, GDN_HEAD_K, GDN_HEAD_V), state_gdn_rec.dtype)
    y_prompt, pool_p, conv_p, rec_p = trunk(
        x_prompt, zero_pool, zero_conv, zero_rec, 0, norm_mix_pre, norm_mix_post, norm_ffn_pre,
        norm_ffn_post, pool_w, pool_scale, gdn_w_in, gdn_conv_w, gdn_a_log, gdn_dt_bias,
        gdn_o_norm, gdn_w_out, ffn_w_in, ffn_w_out)
    y_sample, pool_s, conv_s, rec_s = trunk(
        x_sample, state_pool, state_gdn_conv, state_gdn_rec, min(PAST_LEN, POOL_BUF), norm_mix_pre,
        norm_mix_post, norm_ffn_pre, norm_ffn_post, pool_w, pool_scale, gdn_w_in, gdn_conv_w,
        gdn_a_log, gdn_dt_bias, gdn_o_norm, gdn_w_out, ffn_w_in, ffn_w_out)
    return (y_prompt, y_sample, pool_p, pool_s, conv_p, conv_s, rec_p, rec_s)
```

```python
import numpy as np
import concourse.bass as bass
import concourse.mybir as mybir
from concourse.bass_utils import run_bass_kernel_spmd

F32 = mybir.dt.float32
BF16 = mybir.dt.bfloat16
AF = mybir.ActivationFunctionType
ALU = mybir.AluOpType

D = 1024
DFF = 2816
NCORE = 8
NEG = -30000.0


class Buf:
    __slots__ = ("w", "r")

    def __init__(self):
        self.w = None
        self.r = []


class T:
    def __init__(self, ap, bufs=None):
        self.ap = ap
        self.b = bufs if bufs is not None else [Buf()]


class Ring:
    def __init__(self, items):
        self.items = items
        self.i = 0

    def next(self):
        t = self.items[self.i % len(self.items)]
        self.i += 1
        return t


def _bufs(lst):
    out = []
    for t in lst:
        if isinstance(t, Buf):
            out.append(t)
        else:
            out.extend(t.b)
    return out


class _Rec:
    def __getattr__(self, name):
        def f(*a, **k):
            self.__dict__["call"] = (name, a, k)
            return self
        return f


class Sched:
    ENG = ["pe", "act", "dve", "pool", "sp"]
    NDSEM = 12

    def __init__(self, nc):
        self.nc = nc
        self.ops = []
        self.by_eng = {e: [] for e in self.ENG}
        self.ndma = {e: 0 for e in self.ENG}

    def op(self, eng, fn, reads=(), writes=(), dma=False):
        import os
        if len(self.ops) >= int(os.environ.get("MAXOPS", "100000000")):
            return None
        rec_ = _Rec()
        fn(rec_)
        call = rec_.call
        fn = lambda e, call=call: getattr(e, call[0])(*call[1], **call[2])
        reads = _bufs(reads)
        writes = _bufs(writes)
        oid = len(self.ops)
        deps = {}
        for b in reads:
            if b.w is not None:
                deps[(b.w, "raw")] = 1
        for b in writes:
            if b.w is not None:
                deps[(b.w, "waw")] = 1
            for r in b.r:
                deps[(r, "war")] = 1
        for b in reads:
            b.r.append(oid)
        for b in writes:
            b.w = oid
            b.r = []
        real = {}
        for (p, kind) in deps:
            if p == oid:
                continue
            po = self.ops[p]
            if not po[3] and po[0] == eng and eng == "pe" and not dma:
                continue
            real[p] = 1
        rec = [eng, fn, list(real.keys()), dma, False, None, None]
        if dma:
            rec[5] = self.ndma[eng]
            self.ndma[eng] += 1
        self.ops.append(rec)
        self.by_eng[eng].append(oid)
        for p in real:
            self.ops[p][4] = True
        return oid

    def emit(self):
        nc = self.nc
        engsem = {e: nc.alloc_semaphore(name=f"es_{e}") for e in self.ENG}
        dsem = {e: [nc.alloc_semaphore(name=f"ds_{e}{i}") for i in range(self.NDSEM)]
                for e in self.ENG if self.ndma[e] > 0}
        for e in self.ENG:
            c = 0
            for oid in self.by_eng[e]:
                o = self.ops[oid]
                if not o[3] and o[4]:
                    c += 1
                    o[6] = c
        K = self.NDSEM
        ops = self.ops
        allsems = list(engsem.values()) + [x for v in dsem.values() for x in v]
        for sm in allsems:
            nc.gpsimd.sem_clear(sm)
        nc.all_engine_barrier()

        def run(e, eng):
            waited = {}

            def wait(sem, val):
                key = id(sem)
                if waited.get(key, 0) >= val:
                    return
                waited[key] = val
                eng.wait_ge(sem, val)

            for oid in self.by_eng[e]:
                o = ops[oid]
                for p in o[2]:
                    po = ops[p]
                    if po[3]:
                        wait(dsem[po[0]][po[5] % K], 16 * (po[5] // K + 1))
                    else:
                        wait(engsem[po[0]], po[6])
                if o[3]:
                    j = o[5]
                    if j >= K:
                        wait(dsem[e][j % K], 16 * (j // K))
                inst = o[1](eng)
                if o[3]:
                    inst.then_inc(dsem[e][o[5] % K], 16)
                elif o[4]:
                    inst.then_inc(engsem[e], 1)
            if e == "sp":
                for q in dsem:
                    n = self.ndma[q]
                    for i in range(min(K, n)):
                        wait(dsem[q][i], 16 * ((n - 1 - i) // K + 1))

        with nc.Block() as block:
            @block.tensor
            def _(eng):
                run("pe", eng)

            @block.scalar
            def _(eng):
                run("act", eng)

            @block.vector
            def _(eng):
                run("dve", eng)

            @block.gpsimd
            def _(eng):
                run("pool", eng)

            @block.sync
            def _(eng):
                run("sp", eng)

        nc.all_engine_barrier()
        nc.clear_and_free_semaphores(allsems)
        nc.all_engine_barrier()


def _consts():
    idx = np.arange(128)
    j = idx[:, None]
    i = idx[None, :]
    same = (j // 8) == (i // 8)
    U = (j <= i).astype(np.float32)
    Ub = ((j <= i) & same).astype(np.float32)
    NM = np.where(i >= j, 0.0, NEG).astype(np.float32)
    NMb = np.where((i >= j) & same, 0.0, NEG).astype(np.float32)
    ST = (i > j).astype(np.float32)
    STb = ((i > j) & same).astype(np.float32)
    BO = same.astype(np.float32)
    ONES = np.ones((128, 128), np.float32)
    ID = np.eye(128, dtype=np.float32)
    BM = ((idx[:, None] // 8) == np.arange(16)[None, :]).astype(np.float32)
    c32 = np.concatenate([U, Ub, NM, NMb, ST, STb, BO, ONES, ID, BM], axis=1)
    wins = (2, 4, 8, 16)
    pm = np.zeros((128, 6, 4, 128), np.float32)
    s = idx[:, None]
    t = idx[None, :]
    for g, w in enumerate(wins):
        pm[:, 0, g] = ((s > t - w) & (s <= t)) / w - (s == t)
        cnt = np.minimum(w, t + 1)
        pm[:, 1, g] = ((s > t - w) & (s <= t)) / cnt - (s == t)
        pm[:, 2, g] = ((s - 128) > (t - w)) / w
        sq, sp_ = s // 8, s % 8
        tq, tp = t // 8, t % 8
        pm[:, 3, g] = ((sq == tq) & (sp_ > tp - w) & (sp_ <= tp)) / w - (s == t)
        r = np.arange(120)[:, None]
        rq, rb = r // 15, r % 15
        for half in range(2):
            pm[:120, 4 + half, g] = ((rq + 8 * half == tq) & ((rb - 15) > (tp - w))) / w
    return c32, pm.reshape(128, 6 * 512)


def build(stop_after=None):
    nc = bass.Bass("TRN2", target_bir_lowering=False)

    def din(name, shape):
        return nc.dram_tensor(name, list(shape), F32, kind="ExternalInput").ap()

    def dout(name, shape):
        return nc.dram_tensor(name, list(shape), F32, kind="ExternalOutput").ap()

    xp_d = din("xp", [2048, D])
    xs_d = din("xs", [128, D])
    pst_d = din("pst", [240, D])
    cst_d = din("cst", [4096, 16, 3])
    rst_d = din("rst", [16, 16, 128, 128])
    nmp_d = din("nmp", [2, D])
    nmo_d = din("nmo", [2, D])
    nfp_d = din("nfp", [2, D])
    nfo_d = din("nfo", [2, D])
    pw_d = din("pw", [4, 256, 256])
    psc_d = din("psc", [1, D])
    gwi_d = din("gwi", [D, 6176])
    gcw_d = din("gcw", [4096, 4])
    galog_d = din("galog", [1, 16])
    gdtb_d = din("gdtb", [1, 16])
    gon_d = din("gon", [1, 128])
    gwo_d = din("gwo", [2048, D])
    fwi_d = din("fwi", [2, D, 2 * DFF])
    fwo_d = din("fwo", [2, DFF, D])
    c32_d = din("c32", [128, 9 * 128 + 16])
    pm_d = din("pm", [128, 6 * 512])

    yp_d = dout("yp", [2048, D])
    ys_d = dout("ys", [128, D])
    pp_d = dout("pp", [15, D])
    pso_d = dout("pso", [16, 15, D])
    cp_d = dout("cp", [4096, 3])
    cso_d = dout("cso", [4096, 16, 3])
    rp_d = dout("rp", [16, 128, 128])
    rso_d = dout("rso", [16, 16, 128, 128])
    dbg_d = dout("dbg", [128, 9 * D]) if stop_after == "gdn" else None
    dbg2_d = dout("dbg2", [128, 9 * D]) if stop_after == "gdn" else None
    dbg3_d = dout("dbg3", [16, 128, 128]) if stop_after == "gdn" else None
    dbgn = [0]

    def dump(t, ap):
        if dbg3_d is None or dbgn[0] >= 16:
            return
        dma("pool", dbg3_d[dbgn[0]][:, 0:ap.shape[-1]] if len(ap.shape) == 2 else dbg3_d[dbgn[0]], ap, reads=[t], writes=[Buf()])
        dbgn[0] += 1

    S = Sched(nc)
    cnt = [0]

    def sb(shape, dt, name=None):
        cnt[0] += 1
        return nc.alloc_sbuf_tensor(f"s_{name}" if name else f"sb{cnt[0]}", list(shape), dt).ap()

    def tsb(shape, dt, name=None):
        return T(sb(shape, dt, name))

    dram_out = Buf()

    def dma(q, out, in_, reads=(), writes=()):
        S.op(q, lambda e: e.dma_start(out=out, in_=in_), reads=reads, writes=writes, dma=True)

    ARENA = 82 * 1024
    arena = sb([128, ARENA // 4], F32, "arena")
    ablk = [Buf() for _ in range(ARENA // 512)]
    apos = [0]

    def areset():
        apos[0] = 0

    def aalloc(shape, dt):
        esz = 4 if dt == F32 else 2
        n = esz
        for d_ in shape[1:]:
            n *= d_
        off = apos[0]
        nb = (n + 511) // 512 * 512
        assert off + nb <= ARENA, ("arena overflow", off, nb)
        apos[0] = off + nb
        v = arena[:, off // 4:(off + nb) // 4]
        if dt != F32:
            v = v.bitcast(dt)
        v = v[:, :n // esz]
        if len(shape) == 3:
            v = v.rearrange("p (a b) -> p a b", a=shape[1])
        elif len(shape) == 4:
            v = v.rearrange("p (a b c) -> p a b c", a=shape[1], b=shape[2])
        if shape[0] < 128:
            v = v[:shape[0]]
        return T(v, ablk[off // 512:(off + nb) // 512])

    c32 = tsb([128, 9 * 128 + 16], F32, "c32")
    dma("sp", c32.ap, c32_d, writes=[c32])
    cU, cUb, cNM, cNMb, cST, cSTb, cBO, cONES, cID = [c32.ap[:, k * 128:(k + 1) * 128] for k in range(9)]
    cBM = c32.ap[:, 9 * 128:9 * 128 + 16]
    cbf = tsb([128, 2, 128], BF16, "cbf")
    S.op("dve", lambda e: e.tensor_copy(out=cbf.ap[:, 0, :], in_=cID), reads=[c32], writes=[cbf])
    S.op("dve", lambda e: e.tensor_copy(out=cbf.ap[:, 1, :], in_=cONES), reads=[c32], writes=[cbf])
    identb = cbf.ap[:, 0, :]
    onesb = cbf.ap[:, 1, :]

    banks = [nc.alloc_psum_tensor(f"pb{k}", [128, 512], F32).ap() for k in range(6)]
    PF = [T(banks[k]) for k in range(6)]
    PH = [T(banks[3 + k // 2][:, (k % 2) * 256:(k % 2) * 256 + 256], PF[3 + k // 2].b) for k in range(6)]
    ptr_ap = nc.alloc_psum_tensor("ptr", [128, 8, 128], BF16).ap()
    ptrB = [Buf()]
    PTq = [T(ptr_ap[:, k, :], ptrB) for k in range(8)]
    PTh = [T(ptr_ap[:, 4 * k:4 * k + 4, :], ptrB) for k in range(2)]
    pm_ap = nc.alloc_psum_tensor("pmisc", [128, 512], F32).ap()
    pmB = [Buf()]
    PM = [T(pm_ap[:, k * 128:(k + 1) * 128], pmB) for k in range(4)]
    rPF = Ring(PF)
    rPB = Ring(PF[:3])
    rPH = Ring(PH)
    rPTq = Ring(PTq)
    rPTh = Ring(PTh)
    rPM = Ring(PM)

    NTM = 9
    xbig = sb([128, NTM, D], F32, "x")
    x_t = [T(xbig[:, j, :]) for j in range(NTM)]
    hTbig = sb([128, 8, NTM * 128], BF16, "hT")
    hT_t = [T(hTbig[:, :, j * 128:(j + 1) * 128]) for j in range(NTM)]
    ybig = sb([128, NTM * D], F32, "yacc")
    y_t = [T(ybig[:, j * D:(j + 1) * D]) for j in range(NTM)]
    junk = tsb([128, D], BF16, "junk")
    rSmall = Ring([tsb([128, 2], F32) for _ in range(6)])
    rHb = Ring([tsb([128, D], BF16) for _ in range(2)])
    rHbPool = Ring([tsb([128, D], BF16) for _ in range(2)])
    rTmp = Ring([tsb([128, D], F32) for _ in range(2)])

    def load_gain(src_row):
        g = aalloc([128, D], F32)
        dma("sp", g.ap, src_row.partition_broadcast(128), writes=[g])
        return g

    def rstd_of(src_ap, srcT, rows, n):
        st = rSmall.next()
        sc = float(n) ** -0.5
        S.op("act", lambda e: e.activation(out=junk.ap[:rows, :n], in_=src_ap, func=AF.Square, scale=sc,
                                           accum_out=st.ap[:rows, 0:1]), reads=[srcT], writes=[junk, st])
        S.op("act", lambda e: e.activation(out=st.ap[:rows, 1:2], in_=st.ap[:rows, 0:1], func=AF.Ln,
                                           bias=1e-6, scale=1.0), reads=[st], writes=[st])
        S.op("act", lambda e: e.activation(out=st.ap[:rows, 0:1], in_=st.ap[:rows, 1:2], func=AF.Exp,
                                           scale=-0.5), reads=[st], writes=[st])
        return st

    def pre_norm(j, gain, ring, out32=None):
        st = rstd_of(x_t[j].ap, x_t[j], 128, D)
        hb = ring.next()
        if out32 is not None:
            S.op("dve", lambda e: e.scalar_tensor_tensor(out=out32.ap, in0=x_t[j].ap, scalar=st.ap[:, 0:1],
                                                         in1=gain.ap, op0=ALU.mult, op1=ALU.mult),
                 reads=[x_t[j], st, gain], writes=[out32])
            S.op("act", lambda e: e.activation(out=hb.ap, in_=out32.ap, func=AF.Copy), reads=[out32], writes=[hb])
        else:
            S.op("dve", lambda e: e.scalar_tensor_tensor(out=hb.ap, in0=x_t[j].ap, scalar=st.ap[:, 0:1],
                                                         in1=gain.ap, op0=ALU.mult, op1=ALU.mult),
                 reads=[x_t[j], st, gain], writes=[hb])
        return hb

    def to_hT(j, hb):
        for half in range(2):
            pt = rPTh.next()
            for k in range(4):
                kk = half * 4 + k
                S.op("pe", lambda e, kk=kk, k=k, pt=pt: e.transpose(out=pt.ap[:, k, :], in_=hb.ap[:, kk * 128:(kk + 1) * 128],
                                                                     identity=identb), reads=[hb, cbf], writes=[pt])
            S.op("act", lambda e, pt=pt, half=half: e.activation(out=hT_t[j].ap[:, half * 4:half * 4 + 4, :], in_=pt.ap, func=AF.Copy),
                 reads=[pt], writes=[hT_t[j]])

    def post_norm_add(j, src_ap, srcT, gain):
        st = rstd_of(src_ap, srcT, 128, D)
        tmp = rTmp.next()
        S.op("dve", lambda e: e.scalar_tensor_tensor(out=tmp.ap, in0=src_ap, scalar=st.ap[:, 0:1], in1=gain.ap,
                                                     op0=ALU.mult, op1=ALU.mult), reads=[srcT, st, gain], writes=[tmp])
        S.op("dve", lambda e: e.tensor_tensor(out=x_t[j].ap, in0=x_t[j].ap, in1=tmp.ap, op=ALU.add),
             reads=[x_t[j], tmp], writes=[x_t[j]])

    def ffn(layer, tiles, blocks):
        areset()
        rWg = Ring([aalloc([128, 8, 512], BF16) for _ in range(2)])
        rWu = Ring([aalloc([128, 8, 512], BF16) for _ in range(2)])
        rWo = Ring([aalloc([128, 4, D], BF16) for _ in range(2)])
        rAct = Ring([aalloc([128, 4, 512], BF16) for _ in range(2)])
        rSg = Ring([aalloc([128, 512], F32) for _ in range(2)])
        gpre = load_gain(nfp_d[layer:layer + 1, :])
        gpost = load_gain(nfo_d[layer:layer + 1, :])
        groups = [(f0, min(4, 22 - f0)) for f0 in range(0, 22, 4)]

        def load_group(gi):
            f0, nf = groups[gi]
            wg, wu, wo = rWg.next(), rWu.next(), rWo.next()
            dma("pool", wg.ap[:, :, :nf * 128],
                fwi_d[layer, :, f0 * 128:(f0 + nf) * 128].rearrange("(k p) n -> p k n", p=128), writes=[wg])
            dma("pool", wu.ap[:, :, :nf * 128],
                fwi_d[layer, :, DFF + f0 * 128:DFF + (f0 + nf) * 128].rearrange("(k p) n -> p k n", p=128), writes=[wu])
            dma("pool", wo.ap[:, :nf, :],
                fwo_d[layer, f0 * 128:(f0 + nf) * 128, :].rearrange("(c p) n -> p c n", p=128), writes=[wo])
            return wg, wu, wo

        nxt = load_group(0)
        for j in tiles:
            hb = pre_norm(j, gpre, rHb)
            to_hT(j, hb)
        for gi, (f0, nf) in enumerate(groups):
            wg, wu, wo = nxt
            if gi + 1 < len(groups):
                nxt = load_group(gi + 1)
            for (c0, wd) in blocks:
                tl = list(range(c0 // 128, (c0 + wd) // 128))
                hts = [hT_t[j] for j in tl]
                actT = rAct.next()
                for fc in range(nf):
                    pg = rPF.next()
                    for k in range(8):
                        S.op("pe", lambda e, pg=pg, k=k, fc=fc: e.matmul(pg.ap[:, :wd], lhsT=wg.ap[:, k, fc * 128:(fc + 1) * 128],
                                                                         rhs=hTbig[:, k, c0:c0 + wd], start=(k == 0), stop=(k == 7)),
                             reads=[wg] + hts, writes=[pg])
                    pu = rPF.next()
                    for k in range(8):
                        S.op("pe", lambda e, pu=pu, k=k, fc=fc: e.matmul(pu.ap[:, :wd], lhsT=wu.ap[:, k, fc * 128:(fc + 1) * 128],
                                                                         rhs=hTbig[:, k, c0:c0 + wd], start=(k == 0), stop=(k == 7)),
                             reads=[wu] + hts, writes=[pu])
                    sg = rSg.next()
                    S.op("act", lambda e, pg=pg, sg=sg: e.activation(out=sg.ap[:, :wd], in_=pg.ap[:, :wd], func=AF.Silu),
                         reads=[pg], writes=[sg])
                    S.op("dve", lambda e, pu=pu, sg=sg, fc=fc: e.tensor_tensor(out=actT.ap[:, fc, :wd], in0=pu.ap[:, :wd], in1=sg.ap[:, :wd],
                                                                               op=ALU.mult), reads=[pu, sg], writes=[actT])
                for ti, j in enumerate(tl):
                    for dh in range(2):
                        py = rPF.next()
                        for fc in range(nf):
                            S.op("pe", lambda e, py=py, fc=fc, ti=ti, dh=dh: e.matmul(py.ap, lhsT=actT.ap[:, fc, ti * 128:(ti + 1) * 128],
                                                                                      rhs=wo.ap[:, fc, dh * 512:(dh + 1) * 512],
                                                                                      start=(fc == 0), stop=(fc == nf - 1)),
                                 reads=[actT, wo], writes=[py])
                        ysl = y_t[j].ap[:, dh * 512:(dh + 1) * 512]
                        if gi == 0:
                            S.op("act", lambda e, py=py, ysl=ysl: e.activation(out=ysl, in_=py.ap, func=AF.Copy), reads=[py], writes=[y_t[j]])
                        else:
                            S.op("dve", lambda e, py=py, ysl=ysl: e.tensor_tensor(out=ysl, in0=ysl, in1=py.ap, op=ALU.add),
                                 reads=[py, y_t[j]], writes=[y_t[j]])
        for j in tiles:
            post_norm_add(j, y_t[j].ap, y_t[j], gpost)

    pool_prev = [None]

    def pool_phase(sbi, tiles, kinds):
        areset()
        pmb = aalloc([128, 6, 4, 128], BF16)
        dma("pool", pmb.ap, pm_d.rearrange("p (a g t) -> p a g t", a=6, g=4), writes=[pmb])
        pwb = aalloc([128, 4, 2, 256], BF16)
        pstb = aalloc([128, 2, D], BF16)
        dT = Ring([aalloc([128, 8, 128], BF16) for _ in range(2)])
        gpre = load_gain(nmp_d[0:1, :])
        gpost = load_gain(nmo_d[0:1, :])
        gsc = load_gain(psc_d[0:1, :])
        dma("pool", pwb.ap, pw_d.rearrange("g (c p) e -> p g c e", p=128), writes=[pwb])
        if sbi == 0:
            S.op("dve", lambda e: e.memset(pstb.ap, 0.0), writes=[pstb])
            dma("pool", pstb.ap[:120, :, :], pst_d.rearrange("(a r) d -> r a d", a=2), writes=[pstb])
            if stop_after != "nod2d":
                dma("sp", pso_d[:, 0:7, :], pst_d.rearrange("(s b) d -> s b d", b=15)[:, 8:15, :], writes=[Buf()])
        for j, (kind, pi) in zip(tiles, kinds):
            h32 = y_t[j]
            hb = pre_norm(j, gpre, rHbPool, out32=h32)
            if kind == "s":
                for sq in range(16):
                    dma("sp", pso_d[sq, 7:15, :], h32.ap[sq * 8:(sq + 1) * 8, :], reads=[h32], writes=[Buf()])
            elif pi == 15:
                dma("sp", pp_d, h32.ap[113:128, :], reads=[h32], writes=[Buf()])
            d = dT.next()
            phs = [rPH.next() for _ in range(4)]
            for cc in range(8):
                g = cc // 2
                ph = phs[cc // 2]
                o_ap = ph.ap[:, (cc % 2) * 128:(cc % 2) * 128 + 128]
                lhs_cur = hb.ap[:, cc * 128:(cc + 1) * 128]
                if kind == "s":
                    S.op("pe", lambda e, o_ap=o_ap, lhs_cur=lhs_cur, g=g: e.matmul(o_ap, lhsT=lhs_cur, rhs=pmb.ap[:, 3, g, :], start=True, stop=False),
                         reads=[hb, pmb], writes=[ph])
                    for half in range(2):
                        S.op("pe", lambda e, o_ap=o_ap, cc=cc, g=g, half=half: e.matmul(o_ap, lhsT=pstb.ap[:, half, cc * 128:(cc + 1) * 128],
                                                                                        rhs=pmb.ap[:, 4 + half, g, :], start=False, stop=(half == 1)),
                             reads=[pstb, pmb], writes=[ph])
                elif pi == 0:
                    S.op("pe", lambda e, o_ap=o_ap, lhs_cur=lhs_cur, g=g: e.matmul(o_ap, lhsT=lhs_cur, rhs=pmb.ap[:, 1, g, :], start=True, stop=True),
                         reads=[hb, pmb], writes=[ph])
                else:
                    prev = pool_prev[0]
                    S.op("pe", lambda e, o_ap=o_ap, lhs_cur=lhs_cur, g=g: e.matmul(o_ap, lhsT=lhs_cur, rhs=pmb.ap[:, 0, g, :], start=True, stop=False),
                         reads=[hb, pmb], writes=[ph])
                    S.op("pe", lambda e, o_ap=o_ap, cc=cc, g=g, prev=prev: e.matmul(o_ap, lhsT=prev.ap[:, cc * 128:(cc + 1) * 128], rhs=pmb.ap[:, 2, g, :],
                                                                                    start=False, stop=True), reads=[prev, pmb], writes=[ph])
            for q in range(4):
                S.op("act", lambda e, q=q: e.activation(out=d.ap[:, 2 * q:2 * q + 2, :], in_=phs[q].ap.rearrange("p (a t) -> p a t", a=2), func=AF.Copy),
                     reads=[phs[q]], writes=[d])
            if kind == "p":
                pool_prev[0] = hb
            pys = [rPB.next(), rPB.next()]
            for g in range(4):
                py = pys[g // 2]
                o_ap = py.ap[:, (g % 2) * 256:(g % 2) * 256 + 256]
                for c in range(2):
                    S.op("pe", lambda e, o_ap=o_ap, g=g, c=c: e.matmul(o_ap, lhsT=d.ap[:, 2 * g + c, :], rhs=pwb.ap[:, g, c, :],
                                                                       start=(c == 0), stop=(c == 1)), reads=[d, pwb], writes=[py])
            m32 = rTmp.next()
            for hh in range(2):
                S.op("dve", lambda e, hh=hh: e.tensor_tensor(out=m32.ap[:, hh * 512:(hh + 1) * 512], in0=pys[hh].ap,
                                                             in1=gsc.ap[:, hh * 512:(hh + 1) * 512], op=ALU.mult),
                     reads=[pys[hh], gsc], writes=[m32])
            post_norm_add(j, m32.ap, m32, gpost)


    S32 = [tsb([128, 128], F32, f"S32_{h}") for h in range(16)]
    Sbf = [tsb([128, 128], BF16, f"Sbf_{h}") for h in range(16)]
    ctail = tsb([128, 32, 3], F32, "ctail")

    def gdn_phase(sbi, tiles, kinds, blocks):
        NTl = len(tiles)
        Ncol = NTl * 128
        has_s = kinds[0][0] == "s"
        p0 = 128 if has_s else 0
        Np = Ncol - p0
        areset()
        if dbg2_d is not None and sbi == 0:
            dma("sp", dbg2_d.rearrange("p (a b) -> p a b", a=9), xbig, reads=x_t, writes=[Buf()])
        gpre = load_gain(nmp_d[1:2, :])
        for j in tiles:
            hb = pre_norm(j, gpre, rHb)
            to_hT(j, hb)
        areset()
        gs = aalloc([128, NTl, 6, 16], F32)
        glb = aalloc([128, NTl, 16], F32)
        gls = aalloc([128, 16, 16], F32)
        gblk = aalloc([128, 16, 16], F32)
        wba = aalloc([128, 8, 128], BF16)
        dtb = aalloc([128, 16], F32)
        nea = aalloc([128, 16], F32)
        cw = aalloc([128, 32, 4], F32)
        onb = aalloc([128, 128], F32)
        t16 = Ring([aalloc([128, 16], F32) for _ in range(3)])
        S.op("dve", lambda e: e.memset(wba.ap, 0.0), writes=[wba])
        S.op("dve", lambda e: e.memset(gs.ap, 0.0), writes=[gs])
        dma("pool", wba.ap[:, :, 0:32], gwi_d[:, 6144:6176].rearrange("(k p) n -> p k n", p=128), writes=[wba])
        dma("sp", dtb.ap, gdtb_d.partition_broadcast(128), writes=[dtb])
        dma("sp", nea.ap, galog_d.partition_broadcast(128), writes=[nea])
        dma("sp", cw.ap, gcw_d.rearrange("(c p) t -> p c t", p=128), writes=[cw])
        dma("sp", onb.ap, gon_d.partition_broadcast(128), writes=[onb])
        S.op("act", lambda e: e.activation(out=nea.ap, in_=nea.ap, func=AF.Exp), reads=[nea], writes=[nea])
        S.op("dve", lambda e: e.tensor_scalar(out=nea.ap, in0=nea.ap, scalar1=-1.0, scalar2=None, op0=ALU.mult), reads=[nea], writes=[nea])
        for j, (kind, pi) in zip(tiles, kinds):
            cols = slice(j * 128, (j + 1) * 128)
            pm = rPM.next()
            for k in range(8):
                S.op("pe", lambda e, k=k: e.matmul(pm.ap, lhsT=hTbig[:, k, cols], rhs=wba.ap[:, k, :], start=(k == 0), stop=(k == 7)),
                     reads=[hT_t[j], wba], writes=[pm])
            G = lambda q: gs.ap[:, j, q, :]
            ta, tb = t16.next(), t16.next()
            S.op("act", lambda e: e.activation(out=ta.ap, in_=pm.ap[:, 0:16], func=AF.Exp, scale=-1.0), reads=[pm], writes=[ta])
            S.op("dve", lambda e: e.tensor_tensor(out=tb.ap, in0=pm.ap[:, 16:32], in1=dtb.ap, op=ALU.add), reads=[pm, dtb], writes=[tb])
            S.op("dve", lambda e: e.tensor_scalar(out=ta.ap, in0=ta.ap, scalar1=1.0, scalar2=None, op0=ALU.add), reads=[ta], writes=[ta])
            S.op("dve", lambda e: e.reciprocal(out=G(1), in_=ta.ap), reads=[ta], writes=[gs])
            S.op("dve", lambda e: e.tensor_scalar(out=G(2), in0=G(1), scalar1=-1.0, scalar2=None, op0=ALU.mult), reads=[gs], writes=[gs])
            S.op("act", lambda e: e.activation(out=tb.ap, in_=tb.ap, func=AF.Exp), reads=[tb], writes=[tb])
            S.op("act", lambda e: e.activation(out=tb.ap, in_=tb.ap, func=AF.Ln, bias=1.0, scale=1.0), reads=[tb], writes=[tb])
            S.op("dve", lambda e: e.tensor_tensor(out=G(0), in0=tb.ap, in1=nea.ap, op=ALU.mult), reads=[tb, nea], writes=[gs])
            pm2 = rPM.next()
            pm3 = rPM.next()
            grow = gs.ap[:, j, :, :].rearrange("p a b -> p (a b)")
            S.op("pe", lambda e: e.matmul(pm2.ap[:, 0:96], lhsT=(cUb if kind == "s" else cU), rhs=grow, start=True, stop=True), reads=[c32, gs], writes=[pm2])
            S.op("pe", lambda e: e.matmul(pm3.ap[:, 0:96], lhsT=(cBO if kind == "s" else cONES), rhs=grow, start=True, stop=True), reads=[c32, gs], writes=[pm3])
            S.op("dve", lambda e: e.tensor_copy(out=G(3), in_=pm2.ap[:, 0:16]), reads=[pm2], writes=[gs])
            S.op("act", lambda e: e.activation(out=G(4), in_=pm2.ap[:, 0:16], func=AF.Exp), reads=[pm2], writes=[gs])
            S.op("act", lambda e: e.activation(out=glb.ap[:, j, :], in_=pm3.ap[:, 0:16], func=AF.Exp), reads=[pm3], writes=[glb])
            tc_ = t16.next()
            S.op("dve", lambda e: e.tensor_tensor(out=tc_.ap, in0=pm3.ap[:, 0:16], in1=G(3), op=ALU.subtract), reads=[pm3, gs], writes=[tc_])
            S.op("act", lambda e: e.activation(out=G(5), in_=tc_.ap, func=AF.Exp), reads=[tc_], writes=[gs])
            if kind == "s":
                S.op("dve", lambda e: e.tensor_tensor(out=gblk.ap, in0=G(0).unsqueeze(1).to_broadcast([128, 16, 16]),
                                                      in1=cBM.unsqueeze(2).to_broadcast([128, 16, 16]), op=ALU.mult), reads=[gs, c32], writes=[gblk])
                pb = rPB.next()
                S.op("pe", lambda e: e.matmul(pb.ap[:, 0:256], lhsT=cONES, rhs=gblk.ap.rearrange("p a b -> p (a b)"), start=True, stop=True),
                     reads=[c32, gblk], writes=[pb])
                S.op("act", lambda e: e.activation(out=gls.ap.rearrange("p a b -> p (a b)"), in_=pb.ap[:, 0:256], func=AF.Exp), reads=[pb], writes=[gls])

        wq = aalloc([128, 8, 128], BF16)
        wk = aalloc([128, 8, 128], BF16)
        wv = aalloc([128, 8, 128], BF16)
        wz = aalloc([128, 8, 128], BF16)
        pre = aalloc([128, 3 + 1024], F32)
        pres = aalloc([128, 16, 11], F32)
        co = aalloc([128, Ncol], F32)
        sq = aalloc([128, 512], BF16)
        rn = aalloc([128, 512], F32)
        qT = aalloc([128, Ncol], BF16)
        kT = aalloc([128, Ncol], BF16)
        vT = aalloc([128, Ncol], BF16)
        ktok = aalloc([128, NTl, 128], BF16)
        vtok = aalloc([128, NTl, 128], BF16)
        zs = aalloc([128, NTl, 128], BF16)
        f128 = Ring([aalloc([128, 128], F32) for _ in range(4)])
        b128 = Ring([aalloc([128, 128], BF16) for _ in range(10)])
        xx = Ring([aalloc([128, 2, 128], F32) for _ in range(2)])
        rR = Ring([aalloc([128, 128], F32) for _ in range(3)])
        padw = aalloc([128, 16 * 136], BF16)
        padq = aalloc([128, 16 * 136], BF16)
        kpad = aalloc([128, 16, 128], BF16)
        s0f = aalloc([128, 8, 128], F32)
        s0b = aalloc([128, 8, 128], BF16)
        snw = aalloc([128, 8, 128], F32)
        onT = T(ybig.bitcast(BF16)[:, :16 * Ncol].rearrange("p (a b) -> p a b", a=16), [b for t in y_t for b in t.b])
        if has_s:
            S.op("dve", lambda e: e.memset(padw.ap, 0.0), writes=[padw])
            S.op("dve", lambda e: e.memset(padq.ap, 0.0), writes=[padq])

        def proj_conv(w, cidx, silu=True):
            for (c0, wd) in blocks:
                ps = rPB.next()
                tl = [hT_t[t] for t in range(c0 // 128, (c0 + wd) // 128)]
                for k in range(8):
                    S.op("pe", lambda e, k=k: e.matmul(ps.ap[:, :wd], lhsT=w.ap[:, k, :], rhs=hTbig[:, k, c0:c0 + wd], start=(k == 0), stop=(k == 7)),
                         reads=[w] + tl, writes=[ps])
                if has_s and c0 == 0:
                    S.op("act", lambda e: e.activation(out=pres.ap[:, :, 3:11], in_=ps.ap[:, 0:128].rearrange("p (a b) -> p a b", a=16), func=AF.Copy),
                         reads=[ps], writes=[pres])
                else:
                    S.op("act", lambda e: e.activation(out=pre.ap[:, 3 + c0 - p0:3 + c0 - p0 + wd], in_=ps.ap[:, :wd], func=AF.Copy), reads=[ps], writes=[pre])
            if sbi == 0:
                S.op("dve", lambda e: e.memset(pre.ap[:, 0:3], 0.0), writes=[pre])
            else:
                S.op("dve", lambda e: e.tensor_copy(out=pre.ap[:, 0:3], in_=ctail.ap[:, cidx, :]), reads=[ctail], writes=[pre])
            if has_s:
                dma("sp", pres.ap[:, :, 0:3], cst_d[cidx * 128:(cidx + 1) * 128, :, :], writes=[pres])
            if sbi == 0:
                S.op("dve", lambda e: e.tensor_copy(out=ctail.ap[:, cidx, :], in_=pre.ap[:, Np:Np + 3]), reads=[pre], writes=[ctail])
            else:
                dma("sp", cp_d[cidx * 128:(cidx + 1) * 128, :], pre.ap[:, Np:Np + 3], reads=[pre], writes=[Buf()])
            if has_s:
                dma("sp", cso_d[cidx * 128:(cidx + 1) * 128, :, :], pres.ap[:, :, 8:11], reads=[pres], writes=[Buf()])
            views = [(co.ap[:, p0:Ncol], lambda tap: pre.ap[:, tap:tap + Np], pre)]
            if has_s:
                views.append((co.ap[:, 0:128].rearrange("p (a b) -> p a b", a=16), lambda tap: pres.ap[:, :, tap:tap + 8], pres))
            for (o_ap, src, srcT) in views:
                S.op("dve", lambda e: e.tensor_scalar(out=o_ap, in0=src(0), scalar1=cw.ap[:, cidx, 0:1], scalar2=None, op0=ALU.mult),
                     reads=[srcT, cw], writes=[co])
                for tap in range(1, 4):
                    S.op("dve", lambda e, tap=tap: e.scalar_tensor_tensor(out=o_ap, in0=src(tap), scalar=cw.ap[:, cidx, tap:tap + 1], in1=o_ap,
                                                                           op0=ALU.mult, op1=ALU.add), reads=[srcT, cw, co], writes=[co])
            S.op("act", lambda e: e.activation(out=co.ap, in_=co.ap, func=AF.Silu), reads=[co], writes=[co])

        def l2n(dst, scale):
            for (c0, wd) in blocks:
                S.op("dve", lambda e: e.tensor_tensor(out=sq.ap[:, :wd], in0=co.ap[:, c0:c0 + wd], in1=co.ap[:, c0:c0 + wd], op=ALU.mult), reads=[co], writes=[sq])
                ps = rPB.next()
                S.op("pe", lambda e: e.matmul(ps.ap[:, :wd], lhsT=onesb, rhs=sq.ap[:, :wd], start=True, stop=True), reads=[cbf, sq], writes=[ps])
                S.op("act", lambda e: e.activation(out=rn.ap[:, :wd], in_=ps.ap[:, :wd], func=AF.Ln, bias=1e-6, scale=1.0), reads=[ps], writes=[rn])
                S.op("act", lambda e: e.activation(out=rn.ap[:, :wd], in_=rn.ap[:, :wd], func=AF.Exp, scale=-0.5), reads=[rn], writes=[rn])
                S.op("dve", lambda e: e.scalar_tensor_tensor(out=dst.ap[:, c0:c0 + wd], in0=co.ap[:, c0:c0 + wd], scalar=scale, in1=rn.ap[:, :wd],
                                                             op0=ALU.mult, op1=ALU.mult), reads=[co, rn], writes=[dst])

        def to_tok(srcT, dst):
            for j in tiles:
                pt = rPTq.next()
                S.op("pe", lambda e: e.transpose(out=pt.ap, in_=srcT.ap[:, j * 128:(j + 1) * 128], identity=identb), reads=[srcT, cbf], writes=[pt])
                S.op("act", lambda e: e.activation(out=dst.ap[:, j, :], in_=pt.ap, func=AF.Copy), reads=[pt], writes=[dst])

        def wload(w, col0):
            dma("pool", w.ap, gwi_d[:, col0:col0 + 128].rearrange("(k p) n -> p k n", p=128), writes=[w])

        def mm(out_t, out_ap, lhsT, rhs, R, start=True, stop=True):
            S.op("pe", lambda e: e.matmul(out_ap, lhsT=lhsT, rhs=rhs, start=start, stop=stop), reads=R, writes=[out_t])

        for kh in range(8):
            wload(wq, kh * 128)
            wload(wk, 1024 + kh * 128)
            proj_conv(wq, kh)
            l2n(qT, 128.0 ** -0.5)
            proj_conv(wk, 8 + kh)
            l2n(kT, 1.0)
            to_tok(kT, ktok)
            for h in (2 * kh, 2 * kh + 1):
                wload(wv, 2048 + h * 128)
                wload(wz, 4096 + h * 128)
                proj_conv(wv, 16 + h)
                S.op("act", lambda e: e.activation(out=vT.ap, in_=co.ap, func=AF.Copy), reads=[co], writes=[vT])
                to_tok(vT, vtok)
                for j in tiles:
                    pm = rPM.next()
                    for k in range(8):
                        mm(pm, pm.ap, hTbig[:, k, j * 128:(j + 1) * 128], wz.ap[:, k, :], [hT_t[j], wz], start=(k == 0), stop=(k == 7))
                    S.op("act", lambda e: e.activation(out=zs.ap[:, j, :], in_=pm.ap, func=AF.Silu), reads=[pm], writes=[zs])
                if sbi == 0:
                    S.op("dve", lambda e: e.memset(S32[h].ap, 0.0), writes=[S32[h]])
                    S.op("dve", lambda e: e.memset(Sbf[h].ap, 0.0), writes=[Sbf[h]])
                for j, (kind, pi) in zip(tiles, kinds):
                    smp = kind == "s"
                    cols = slice(j * 128, (j + 1) * 128)
                    G = lambda q: gs.ap[:, j, q, h:h + 1]
                    pkq = rPH.next()
                    mm(pkq, pkq.ap[:, 0:128], kT.ap[:, cols], qT.ap[:, cols], [kT, qT])
                    mm(pkq, pkq.ap[:, 128:256], kT.ap[:, cols], kT.ap[:, cols], [kT])
                    prow = rPM.next()
                    mm(prow, prow.ap, gs.ap[:, j, 0, h:h + 1].to_broadcast([128, 128]), (cUb if smp else cU), [gs, c32])
                    E = f128.next()
                    S.op("dve", lambda e: e.scalar_tensor_tensor(out=E.ap, in0=prow.ap, scalar=G(3), in1=(cNMb if smp else cNM), op0=ALU.subtract, op1=ALU.add),
                         reads=[prow, gs, c32], writes=[E])
                    S.op("act", lambda e: e.activation(out=E.ap, in_=E.ap, func=AF.Exp), reads=[E], writes=[E])
                    egb = f128.next()
                    S.op("act", lambda e: e.activation(out=egb.ap, in_=prow.ap, func=AF.Exp), reads=[prow], writes=[egb])
                    qkT = b128.next()
                    S.op("dve", lambda e: e.tensor_tensor(out=qkT.ap, in0=pkq.ap[:, 0:128], in1=E.ap, op=ALU.mult), reads=[pkq, E], writes=[qkT])
                    dS = f128.next()
                    S.op("dve", lambda e: e.tensor_tensor(out=dS.ap, in0=E.ap, in1=(cSTb if smp else cST), op=ALU.mult), reads=[E, c32], writes=[dS])
                    X = xx.next()
                    S.op("dve", lambda e: e.scalar_tensor_tensor(out=X.ap[:, 0, :], in0=pkq.ap[:, 128:256], scalar=G(2), in1=dS.ap, op0=ALU.mult, op1=ALU.mult),
                         reads=[pkq, gs, dS], writes=[X])
                    pt = rPM.next()
                    mm(pt, pt.ap, X.ap[:, 0, :], cID, [X, c32])
                    S.op("act", lambda e: e.activation(out=X.ap[:, 1, :], in_=pt.ap, func=AF.Copy), reads=[pt], writes=[X])
                    Rm = rR.next()
                    S.op("dve", lambda e: e.tensor_tensor(out=Rm.ap, in0=X.ap[:, 0, :], in1=cID, op=ALU.add), reads=[X, c32], writes=[Rm])
                    nlev = 2 if smp else 6
                    for lev in range(nlev):
                        px = rPH.next()
                        last = lev == nlev - 1
                        mm(px, px.ap[:, 128:256], X.ap[:, 0, :], X.ap[:, 1, :], [X])
                        if not last:
                            mm(px, px.ap[:, 0:128], X.ap[:, 1, :], X.ap[:, 0, :], [X])
                        X2 = xx.next()
                        if last:
                            S.op("act", lambda e: e.activation(out=X2.ap[:, 1, :], in_=px.ap[:, 128:256], func=AF.Copy), reads=[px], writes=[X2])
                        else:
                            S.op("act", lambda e: e.activation(out=X2.ap, in_=px.ap.rearrange("p (a b) -> p a b", a=2), func=AF.Copy), reads=[px], writes=[X2])
                        pr = rPM.next()
                        mm(pr, pr.ap, X2.ap[:, 1, :], Rm.ap, [X2, Rm])
                        R2 = rR.next()
                        S.op("dve", lambda e: e.tensor_tensor(out=R2.ap, in0=pr.ap, in1=Rm.ap, op=ALU.add), reads=[pr, Rm], writes=[R2])
                        Rm, X = R2, X2
                    Rb = b128.next()
                    S.op("act", lambda e: e.activation(out=Rb.ap, in_=Rm.ap, func=AF.Copy), reads=[Rm], writes=[Rb])
                    Rm = Rb
                    kg = b128.next()
                    S.op("act", lambda e: e.mul(out=kg.ap, in_=ktok.ap[:, j, :], mul=G(4)), reads=[ktok, gs], writes=[kg])
                    kd = b128.next()
                    S.op("act", lambda e: e.mul(out=kd.ap, in_=ktok.ap[:, j, :], mul=G(5)), reads=[ktok, gs], writes=[kd])
                    pw_ = rPM.next()
                    mm(pw_, pw_.ap, kg.ap, Rm.ap, [kg, Rm])
                    qd = b128.next()
                    S.op("dve", lambda e: e.tensor_tensor(out=qd.ap, in0=qT.ap[:, cols], in1=egb.ap, op=ALU.mult), reads=[qT, egb], writes=[qd])
                    vn = b128.next()
                    if not smp:
                        nw = b128.next()
                        S.op("dve", lambda e: e.tensor_scalar(out=nw.ap, in0=pw_.ap, scalar1=-1.0, scalar2=None, op0=ALU.mult), reads=[pw_], writes=[nw])
                        pv = rPM.next()
                        mm(pv, pv.ap, Rm.ap, vtok.ap[:, j, :], [Rm, vtok], start=True, stop=False)
                        mm(pv, pv.ap, nw.ap, Sbf[h].ap, [nw, Sbf[h]], start=False, stop=True)
                        S.op("act", lambda e: e.mul(out=vn.ap, in_=pv.ap, mul=G(1)), reads=[pv, gs], writes=[vn])
                        po = rPM.next()
                        po_ap = po.ap
                        mm(po, po_ap, qd.ap, Sbf[h].ap, [qd, Sbf[h]], start=True, stop=False)
                        mm(po, po_ap, qkT.ap, vn.ap, [qkT, vn], start=False, stop=True)
                        ps_ = rPH.next()
                        mm(ps_, ps_.ap[:, 0:128], kd.ap, vn.ap, [kd, vn])
                        S.op("dve", lambda e: e.scalar_tensor_tensor(out=S32[h].ap, in0=S32[h].ap, scalar=glb.ap[:, j, h:h + 1], in1=ps_.ap[:, 0:128],
                                                                     op0=ALU.mult, op1=ALU.add), reads=[S32[h], glb, ps_], writes=[S32[h]])
                        S.op("act", lambda e: e.activation(out=Sbf[h].ap, in_=S32[h].ap, func=AF.Copy), reads=[S32[h]], writes=[Sbf[h]])
                        if pi == 15:
                            dma("sp", rp_d[h], S32[h].ap, reads=[S32[h]], writes=[Buf()])
                    else:
                        pw3 = padw.ap[:, :].rearrange("p (a b) -> p a b", b=136)[:, :, 0:8]
                        pq3 = padq.ap[:, :].rearrange("p (a b) -> p a b", b=136)[:, :, 0:8]
                        S.op("dve", lambda e: e.tensor_scalar(out=pw3, in0=pw_.ap.rearrange("p (a b) -> p a b", a=16), scalar1=-1.0, scalar2=None, op0=ALU.mult),
                             reads=[pw_], writes=[padw])
                        S.op("dve", lambda e: e.tensor_copy(out=pq3, in_=qd.ap.rearrange("p (a b) -> p a b", a=16)), reads=[qd], writes=[padq])
                        S.op("dve", lambda e: e.tensor_tensor(out=kpad.ap, in0=kd.ap.unsqueeze(1).to_broadcast([128, 16, 128]),
                                                              in1=cBM.unsqueeze(2).to_broadcast([128, 16, 128]), op=ALU.mult), reads=[kd, c32], writes=[kpad])
                        pv = rPM.next()
                        po = rPH.next()
                        po_ap = po.ap[:, 0:128]
                        for half in range(2):
                            dma("sp", s0f.ap, rst_d[half * 8:(half + 1) * 8, h].rearrange("s k v -> k s v"), writes=[s0f])
                            dma("pool", s0b.ap, rst_d[half * 8:(half + 1) * 8, h].rearrange("s k v -> k s v"), writes=[s0b])
                            if half == 0:
                                mm(pv, pv.ap, Rm.ap, vtok.ap[:, j, :], [Rm, vtok], start=True, stop=False)
                            for sl in range(8):
                                sq_ = half * 8 + sl
                                mm(pv, pv.ap, padw.ap[:, sq_ * 128:(sq_ + 1) * 128], s0b.ap[:, sl, :], [padw, s0b], start=False, stop=(sq_ == 15))
                            for sl in range(8):
                                sq_ = half * 8 + sl
                                mm(po, po_ap, padq.ap[:, sq_ * 128:(sq_ + 1) * 128], s0b.ap[:, sl, :], [padq, s0b], start=(sq_ == 0), stop=False)
                            if half == 1:
                                S.op("act", lambda e: e.mul(out=vn.ap, in_=pv.ap, mul=G(1)), reads=[pv, gs], writes=[vn])
                                mm(po, po_ap, qkT.ap, vn.ap, [qkT, vn], start=False, stop=True)
                        for half in range(2):
                            dma("sp", s0f.ap, rst_d[half * 8:(half + 1) * 8, h].rearrange("s k v -> k s v"), writes=[s0f])
                            for qd4 in range(2):
                                pb = rPB.next()
                                for s4 in range(4):
                                    sq_ = half * 8 + qd4 * 4 + s4
                                    mm(pb, pb.ap[:, s4 * 128:(s4 + 1) * 128], kpad.ap[:, sq_, :], vn.ap, [kpad, vn])
                                sl0 = qd4 * 4
                                S.op("dve", lambda e: e.tensor_tensor(out=snw.ap[:, sl0:sl0 + 4, :], in0=s0f.ap[:, sl0:sl0 + 4, :],
                                                                      in1=gls.ap[:, half * 8 + sl0:half * 8 + sl0 + 4, h:h + 1].to_broadcast([128, 4, 128]), op=ALU.mult),
                                     reads=[s0f, gls], writes=[snw])
                                S.op("dve", lambda e: e.tensor_tensor(out=snw.ap[:, sl0:sl0 + 4, :], in0=snw.ap[:, sl0:sl0 + 4, :],
                                                                      in1=pb.ap.rearrange("p (a b) -> p a b", a=4), op=ALU.add), reads=[snw, pb], writes=[snw])
                            dma("sp", rso_d[half * 8:(half + 1) * 8, h].rearrange("s k v -> k s v"), snw.ap, reads=[snw], writes=[Buf()])
                    if sbi == 0 and j == 1 and h == 2:
                        dump(vtok, vtok.ap[:, j, :])
                        dump(Rm, Rm.ap)
                        dump(vn, vn.ap)
                        dump(kd, kd.ap)
                        dump(qkT, qkT.ap)
                        dump(E, E.ap)
                        dump(ktok, ktok.ap[:, j, :])
                        dump(zs, zs.ap[:, j, :])
                        dump(qd, qd.ap)
                        dump(gs, gs.ap[:, j, :, :].rearrange("p a b -> p (a b)")[:, 0:96])
                    st = rstd_of(po_ap, po, 128, 128)
                    on = f128.next()
                    S.op("dve", lambda e: e.scalar_tensor_tensor(out=on.ap, in0=po_ap, scalar=st.ap[:, 0:1], in1=onb.ap, op0=ALU.mult, op1=ALU.mult),
                         reads=[po, st, onb], writes=[on])
                    ob = b128.next()
                    S.op("dve", lambda e: e.tensor_tensor(out=ob.ap, in0=on.ap, in1=zs.ap[:, j, :], op=ALU.mult), reads=[on, zs], writes=[ob])
                    pt2 = rPTq.next()
                    S.op("pe", lambda e: e.transpose(out=pt2.ap, in_=ob.ap, identity=identb), reads=[ob, cbf], writes=[pt2])
                    S.op("act", lambda e: e.activation(out=onT.ap[:, h, cols], in_=pt2.ap, func=AF.Copy), reads=[pt2], writes=[onT])

        if dbg_d is not None and sbi == 0:
            dma("sp", dbg_d, ybig, reads=[onT], writes=[Buf()])
        areset()
        gpost = aalloc([128, D], F32)
        dma("sp", gpost.ap, nmo_d[1:2, :].partition_broadcast(128), writes=[gpost])
        wo_r = Ring([aalloc([128, 4, D], BF16) for _ in range(2)])
        wos = []
        for j in tiles:
            pys = [rPF.next(), rPF.next()]
            for hg in range(4):
                wo = wo_r.next()
                dma("pool", wo.ap, gwo_d[hg * 512:(hg + 1) * 512, :].rearrange("(c p) n -> p c n", p=128), writes=[wo])
                for hh in range(4):
                    h = hg * 4 + hh
                    for dh in range(2):
                        mm(pys[dh], pys[dh].ap, onT.ap[:, h, j * 128:(j + 1) * 128], wo.ap[:, hh, dh * 512:(dh + 1) * 512], [onT, wo],
                           start=(h == 0), stop=(h == 15))
            m32 = rTmp.next()
            for dh in range(2):
                S.op("act", lambda e, dh=dh: e.activation(out=m32.ap[:, dh * 512:(dh + 1) * 512], in_=pys[dh].ap, func=AF.Copy), reads=[pys[dh]], writes=[m32])
            post_norm_add(j, m32.ap, m32, gpost)

    for sbi in range(2):
        if sbi == 0:
            kinds = [("s", None)] + [("p", i) for i in range(8)]
            blocks = [(0, 128), (128, 512), (640, 512)]
        else:
            kinds = [("p", 8 + i) for i in range(8)]
            blocks = [(0, 512), (512, 512)]
        tiles = list(range(len(kinds)))
        for j, (kind, pi) in zip(tiles, kinds):
            src = xs_d if kind == "s" else xp_d[pi * 128:(pi + 1) * 128, :]
            dma("sp", x_t[j].ap, src, writes=[x_t[j]])
        pool_phase(sbi, tiles, kinds)
        if stop_after not in ("pool", "nod2d"):
            ffn(0, tiles, blocks)
        if stop_after not in ("pool", "nod2d", "l0"):
            gdn_phase(sbi, tiles, kinds, blocks)
            if stop_after != "gdn":
                ffn(1, tiles, blocks)
        for j, (kind, pi) in zip(tiles, kinds):
            dst = ys_d if kind == "s" else yp_d[pi * 128:(pi + 1) * 128, :]
            dma("sp", dst, x_t[j].ap, reads=[x_t[j]], writes=[Buf()])

    with nc.allow_low_precision("bf16 matmul operands, fp32 accumulation"):
        S.emit()
    return nc


_NC_CACHE = {}


def make_in_maps(inp):
    c32, pm = _consts()
    f = lambda a: np.ascontiguousarray(np.asarray(a, dtype=np.float32))
    shared = {
        "nmp": f(inp["norm_mix_pre"]), "nmo": f(inp["norm_mix_post"]),
        "nfp": f(inp["norm_ffn_pre"]), "nfo": f(inp["norm_ffn_post"]),
        "pw": f(inp["pool_w"][0]), "psc": f(inp["pool_scale"]),
        "gwi": f(inp["gdn_w_in"][0]), "gcw": f(np.asarray(inp["gdn_conv_w"][0]).T),
        "galog": f(inp["gdn_a_log"]), "gdtb": f(inp["gdn_dt_bias"]), "gon": f(inp["gdn_o_norm"]),
        "gwo": f(inp["gdn_w_out"][0]), "fwi": f(inp["ffn_w_in"]), "fwo": f(inp["ffn_w_out"]),
        "c32": c32, "pm": pm,
    }
    maps = []
    for c in range(NCORE):
        sl = slice(16 * c, 16 * (c + 1))
        m = dict(shared)
        m["xp"] = f(inp["x_prompt"][c])
        m["xs"] = f(np.asarray(inp["x_sample"][sl]).reshape(128, D))
        m["pst"] = f(np.asarray(inp["state_pool"][0, sl]).reshape(240, D))
        m["cst"] = f(np.asarray(inp["state_gdn_conv"][0, sl]).transpose(2, 0, 1))
        m["rst"] = f(inp["state_gdn_rec"][0, sl])
        maps.append(m)
    return maps


def kernel(**inp):
    if "nc" not in _NC_CACHE:
        _NC_CACHE["nc"] = build()
    nc = _NC_CACHE["nc"]
    maps = make_in_maps(inp)
    res = run_bass_kernel_spmd(nc, maps, core_ids=list(range(NCORE)))
    R = res.results
    y_p = np.stack([R[c]["yp"] for c in range(NCORE)]).reshape(8, 2048, D)
    y_s = np.concatenate([R[c]["ys"].reshape(16, 8, D) for c in range(NCORE)])
    pool_p = np.stack([R[c]["pp"] for c in range(NCORE)])[None]
    pool_s = np.concatenate([R[c]["pso"] for c in range(NCORE)])[None]
    conv_p = np.stack([R[c]["cp"].T for c in range(NCORE)])[None]
    conv_s = np.concatenate([R[c]["cso"].transpose(1, 2, 0) for c in range(NCORE)])[None]
    rec_p = np.stack([R[c]["rp"] for c in range(NCORE)])[None]
    rec_s = np.concatenate([R[c]["rso"] for c in range(NCORE)])[None]
    return (y_p, y_s, pool_p, pool_s, conv_p, conv_s, rec_p, rec_s)
```

```python
import numpy as np
import concourse.bass as bass
import concourse.mybir as mybir
from concourse.bass_utils import run_bass_kernel_spmd

F32 = mybir.dt.float32
BF16 = mybir.dt.bfloat16
AF = mybir.ActivationFunctionType
ALU = mybir.AluOpType

D = 1024
DFF = 2816
NCORE = 8
NEG = -30000.0


class Buf:
    __slots__ = ("w", "r", "excl")

    def __init__(self):
        self.w = None
        self.r = []
        self.excl = False


class T:
    def __init__(self, ap, bufs=None):
        self.ap = ap
        self.b = bufs if bufs is not None else [Buf()]


class Ring:
    def __init__(self, items):
        self.items = items
        self.i = 0

    def next(self):
        t = self.items[self.i % len(self.items)]
        self.i += 1
        return t


def _bufs(lst):
    out = []
    for t in lst:
        if isinstance(t, Buf):
            out.append(t)
        else:
            out.extend(t.b)
    return out


class _Rec:
    def __getattr__(self, name):
        def f(*a, **k):
            self.__dict__["call"] = (name, a, k)
            return self
        return f


class Sched:
    ENG = ["pe", "act", "dve", "pool", "sp"]
    NDSEM = 12

    def __init__(self, nc):
        self.nc = nc
        self.ops = []
        self.by_eng = {e: [] for e in self.ENG}
        self.ndma = {e: 0 for e in self.ENG}

    def op(self, eng, fn, reads=(), writes=(), dma=False):
        import os
        if len(self.ops) >= int(os.environ.get("MAXOPS", "100000000")):
            return None
        rec_ = _Rec()
        fn(rec_)
        call = rec_.call
        fn = lambda e, call=call: getattr(e, call[0])(*call[1], **call[2])
        reads = _bufs(reads)
        writes = _bufs(writes)
        writes = writes + [b for b in reads if b.excl and b not in writes]
        reads = [b for b in reads if not b.excl]
        oid = len(self.ops)
        deps = {}
        for b in reads:
            if b.w is not None:
                deps[(b.w, "raw")] = 1
        for b in writes:
            if b.w is not None:
                deps[(b.w, "waw")] = 1
            for r in b.r:
                deps[(r, "war")] = 1
        for b in reads:
            b.r.append(oid)
        for b in writes:
            b.w = oid
            b.r = []
        real = {}
        for (p, kind) in deps:
            if p == oid:
                continue
            po = self.ops[p]
            if not po[3] and po[0] == eng and eng == "pe" and not dma:
                continue
            real[p] = 1
        rec = [eng, fn, list(real.keys()), dma, False, None, None]
        if dma:
            rec[5] = self.ndma[eng]
            self.ndma[eng] += 1
        self.ops.append(rec)
        self.by_eng[eng].append(oid)
        for p in real:
            self.ops[p][4] = True
        return oid

    def emit(self):
        nc = self.nc
        engsem = {e: nc.alloc_semaphore(name=f"es_{e}") for e in self.ENG}
        dsem = {e: [nc.alloc_semaphore(name=f"ds_{e}{i}") for i in range(self.NDSEM)]
                for e in self.ENG if self.ndma[e] > 0}
        for e in self.ENG:
            c = 0
            for oid in self.by_eng[e]:
                o = self.ops[oid]
                if not o[3] and o[4]:
                    c += 1
                    o[6] = c
        K = self.NDSEM
        ops = self.ops
        allsems = list(engsem.values()) + [x for v in dsem.values() for x in v]
        for sm in allsems:
            nc.gpsimd.sem_clear(sm)
        nc.all_engine_barrier()

        def run(e, eng):
            waited = {}

            def wait(sem, val):
                key = id(sem)
                if waited.get(key, 0) >= val:
                    return
                waited[key] = val
                eng.wait_ge(sem, val)

            for oid in self.by_eng[e]:
                o = ops[oid]
                for p in o[2]:
                    po = ops[p]
                    if po[3]:
                        wait(dsem[po[0]][po[5] % K], 16 * (po[5] // K + 1))
                    else:
                        wait(engsem[po[0]], po[6])
                if o[3]:
                    j = o[5]
                    if j >= K:
                        wait(dsem[e][j % K], 16 * (j // K))
                inst = o[1](eng)
                if o[3]:
                    inst.then_inc(dsem[e][o[5] % K], 16)
                elif o[4]:
                    inst.then_inc(engsem[e], 1)
            if e == "sp":
                for q in dsem:
                    n = self.ndma[q]
                    for i in range(min(K, n)):
                        wait(dsem[q][i], 16 * ((n - 1 - i) // K + 1))

        with nc.Block() as block:
            @block.tensor
            def _(eng):
                run("pe", eng)

            @block.scalar
            def _(eng):
                run("act", eng)

            @block.vector
            def _(eng):
                run("dve", eng)

            @block.gpsimd
            def _(eng):
                run("pool", eng)

            @block.sync
            def _(eng):
                run("sp", eng)

        nc.all_engine_barrier()
        nc.clear_and_free_semaphores(allsems)
        nc.all_engine_barrier()


def _consts():
    idx = np.arange(128)
    j = idx[:, None]
    i = idx[None, :]
    same = (j // 8) == (i // 8)
    U = (j <= i).astype(np.float32)
    Ub = ((j <= i) & same).astype(np.float32)
    NM = np.where(i >= j, 0.0, NEG).astype(np.float32)
    NMb = np.where((i >= j) & same, 0.0, NEG).astype(np.float32)
    ST = (i > j).astype(np.float32)
    STb = ((i > j) & same).astype(np.float32)
    BO = same.astype(np.float32)
    ONES = np.ones((128, 128), np.float32)
    ID = np.eye(128, dtype=np.float32)
    BM = ((idx[:, None] // 8) == np.arange(16)[None, :]).astype(np.float32)
    c32 = np.concatenate([U, Ub, NM, NMb, ST, STb, BO, ONES, ID, BM], axis=1)
    wins = (2, 4, 8, 16)
    pm = np.zeros((128, 6, 4, 128), np.float32)
    s = idx[:, None]
    t = idx[None, :]
    for g, w in enumerate(wins):
        pm[:, 0, g] = ((s > t - w) & (s <= t)) / w - (s == t)
        cnt = np.minimum(w, t + 1)
        pm[:, 1, g] = ((s > t - w) & (s <= t)) / cnt - (s == t)
        pm[:, 2, g] = ((s - 128) > (t - w)) / w
        sq, sp_ = s // 8, s % 8
        tq, tp = t // 8, t % 8
        pm[:, 3, g] = ((sq == tq) & (sp_ > tp - w) & (sp_ <= tp)) / w - (s == t)
        r = np.arange(120)[:, None]
        rq, rb = r // 15, r % 15
        for half in range(2):
            pm[:120, 4 + half, g] = ((rq + 8 * half == tq) & ((rb - 15) > (tp - w))) / w
    return c32, pm.reshape(128, 6 * 512)


def build(stop_after=None):
    nc = bass.Bass("TRN2", target_bir_lowering=False)

    def din(name, shape):
        return nc.dram_tensor(name, list(shape), F32, kind="ExternalInput").ap()

    def dout(name, shape):
        return nc.dram_tensor(name, list(shape), F32, kind="ExternalOutput").ap()

    xp_d = din("xp", [2048, D])
    xs_d = din("xs", [128, D])
    pst_d = din("pst", [240, D])
    cst_d = din("cst", [4096, 16, 3])
    rst_d = din("rst", [16, 16, 128, 128])
    nmp_d = din("nmp", [2, D])
    nmo_d = din("nmo", [2, D])
    nfp_d = din("nfp", [2, D])
    nfo_d = din("nfo", [2, D])
    pw_d = din("pw", [4, 256, 256])
    psc_d = din("psc", [1, D])
    gwi_d = din("gwi", [D, 6176])
    gcw_d = din("gcw", [4096, 4])
    galog_d = din("galog", [1, 16])
    gdtb_d = din("gdtb", [1, 16])
    gon_d = din("gon", [1, 128])
    gwo_d = din("gwo", [2048, D])
    fwi_d = din("fwi", [2, D, 2 * DFF])
    fwo_d = din("fwo", [2, DFF, D])
    c32_d = din("c32", [128, 9 * 128 + 16])
    pm_d = din("pm", [128, 6 * 512])

    yp_d = dout("yp", [2048, D])
    ys_d = dout("ys", [128, D])
    pp_d = dout("pp", [15, D])
    pso_d = dout("pso", [16, 15, D])
    cp_d = dout("cp", [4096, 3])
    cso_d = dout("cso", [4096, 16, 3])
    rp_d = dout("rp", [16, 128, 128])
    rso_d = dout("rso", [16, 16, 128, 128])
    dbg_d = dout("dbg", [128, 9 * D]) if stop_after == "gdn" else None
    dbg2_d = dout("dbg2", [128, 9 * D]) if stop_after == "gdn" else None
    dbg3_d = dout("dbg3", [16, 128, 128]) if stop_after == "gdn" else None
    dbgn = [0]

    def dump(t, ap):
        if dbg3_d is None or dbgn[0] >= 16:
            return
        dma("pool", dbg3_d[dbgn[0]][:, 0:ap.shape[-1]] if len(ap.shape) == 2 else dbg3_d[dbgn[0]], ap, reads=[t], writes=[Buf()])
        dbgn[0] += 1

    S = Sched(nc)
    cnt = [0]

    def sb(shape, dt, name=None):
        cnt[0] += 1
        return nc.alloc_sbuf_tensor(f"s_{name}" if name else f"sb{cnt[0]}", list(shape), dt).ap()

    def tsb(shape, dt, name=None):
        return T(sb(shape, dt, name))

    dram_out = Buf()

    def dma(q, out, in_, reads=(), writes=()):
        S.op(q, lambda e: e.dma_start(out=out, in_=in_), reads=reads, writes=writes, dma=True)

    ARENA = 82 * 1024
    arena = sb([128, ARENA // 4], F32, "arena")
    ablk = [Buf() for _ in range(ARENA // 512)]
    apos = [0]

    def areset():
        apos[0] = 0

    def aalloc(shape, dt):
        esz = 4 if dt == F32 else 2
        n = esz
        for d_ in shape[1:]:
            n *= d_
        off = apos[0]
        nb = (n + 511) // 512 * 512
        assert off + nb <= ARENA, ("arena overflow", off, nb)
        apos[0] = off + nb
        v = arena[:, off // 4:(off + nb) // 4]
        if dt != F32:
            v = v.bitcast(dt)
        v = v[:, :n // esz]
        if len(shape) == 3:
            v = v.rearrange("p (a b) -> p a b", a=shape[1])
        elif len(shape) == 4:
            v = v.rearrange("p (a b c) -> p a b c", a=shape[1], b=shape[2])
        if shape[0] < 128:
            v = v[:shape[0]]
        return T(v, ablk[off // 512:(off + nb) // 512])

    c32 = tsb([128, 9 * 128 + 16], F32, "c32")
    dma("sp", c32.ap, c32_d, writes=[c32])
    cU, cUb, cNM, cNMb, cST, cSTb, cBO, cONES, cID = [c32.ap[:, k * 128:(k + 1) * 128] for k in range(9)]
    cBM = c32.ap[:, 9 * 128:9 * 128 + 16]
    cbf = tsb([128, 2, 128], BF16, "cbf")
    S.op("dve", lambda e: e.tensor_copy(out=cbf.ap[:, 0, :], in_=cID), reads=[c32], writes=[cbf])
    S.op("dve", lambda e: e.tensor_copy(out=cbf.ap[:, 1, :], in_=cONES), reads=[c32], writes=[cbf])
    identb = cbf.ap[:, 0, :]
    onesb = cbf.ap[:, 1, :]

    banks = [nc.alloc_psum_tensor(f"pb{k}", [128, 512], F32).ap() for k in range(6)]
    PF = [T(banks[k]) for k in range(6)]
    PH = [T(banks[3 + k // 2][:, (k % 2) * 256:(k % 2) * 256 + 256], PF[3 + k // 2].b) for k in range(6)]
    ptr_ap = nc.alloc_psum_tensor("ptr", [128, 8, 128], BF16).ap()
    ptrB = [Buf()]
    PTq = [T(ptr_ap[:, k, :], ptrB) for k in range(8)]
    PTh = [T(ptr_ap[:, 4 * k:4 * k + 4, :], ptrB) for k in range(2)]
    pm_ap = nc.alloc_psum_tensor("pmisc", [128, 512], F32).ap()
    pmB = [Buf()]
    PM = [T(pm_ap[:, k * 128:(k + 1) * 128], pmB) for k in range(4)]
    for t_ in PF:
        t_.b[0].excl = True
    ptrB[0].excl = True
    pmB[0].excl = True
    rPF = Ring(PF)
    rPB = Ring(PF[:3])
    rPH = Ring(PH)
    rPTq = Ring(PTq)
    rPTh = Ring(PTh)
    rPM = Ring(PM)

    NTM = 9
    xbig = sb([128, NTM, D], F32, "x")
    x_t = [T(xbig[:, j, :]) for j in range(NTM)]
    hTbig = sb([128, 8, NTM * 128], BF16, "hT")
    hT_t = [T(hTbig[:, :, j * 128:(j + 1) * 128]) for j in range(NTM)]
    ybig = sb([128, NTM * D], F32, "yacc")
    y_t = [T(ybig[:, j * D:(j + 1) * D]) for j in range(NTM)]
    junk = tsb([128, D], BF16, "junk")
    rSmall = Ring([tsb([128, 2], F32) for _ in range(6)])
    rHb = Ring([tsb([128, D], BF16) for _ in range(2)])
    rHbPool = Ring([tsb([128, D], BF16) for _ in range(2)])
    rTmp = Ring([tsb([128, D], F32) for _ in range(2)])

    def load_gain(src_row):
        g = aalloc([128, D], F32)
        dma("sp", g.ap, src_row.partition_broadcast(128), writes=[g])
        return g

    def rstd_of(src_ap, srcT, rows, n):
        st = rSmall.next()
        sc = float(n) ** -0.5
        S.op("act", lambda e: e.activation(out=junk.ap[:rows, :n], in_=src_ap, func=AF.Square, scale=sc,
                                           accum_out=st.ap[:rows, 0:1]), reads=[srcT], writes=[junk, st])
        S.op("act", lambda e: e.activation(out=st.ap[:rows, 1:2], in_=st.ap[:rows, 0:1], func=AF.Ln,
                                           bias=1e-6, scale=1.0), reads=[st], writes=[st])
        S.op("act", lambda e: e.activation(out=st.ap[:rows, 0:1], in_=st.ap[:rows, 1:2], func=AF.Exp,
                                           scale=-0.5), reads=[st], writes=[st])
        return st

    def pre_norm(j, gain, ring, out32=None):
        st = rstd_of(x_t[j].ap, x_t[j], 128, D)
        hb = ring.next()
        if out32 is not None:
            S.op("dve", lambda e: e.scalar_tensor_tensor(out=out32.ap, in0=x_t[j].ap, scalar=st.ap[:, 0:1],
                                                         in1=gain.ap, op0=ALU.mult, op1=ALU.mult),
                 reads=[x_t[j], st, gain], writes=[out32])
            S.op("act", lambda e: e.activation(out=hb.ap, in_=out32.ap, func=AF.Copy), reads=[out32], writes=[hb])
        else:
            S.op("dve", lambda e: e.scalar_tensor_tensor(out=hb.ap, in0=x_t[j].ap, scalar=st.ap[:, 0:1],
                                                         in1=gain.ap, op0=ALU.mult, op1=ALU.mult),
                 reads=[x_t[j], st, gain], writes=[hb])
        return hb

    def to_hT(j, hb):
        for half in range(2):
            pt = rPTh.next()
            for k in range(4):
                kk = half * 4 + k
                S.op("pe", lambda e, kk=kk, k=k, pt=pt: e.transpose(out=pt.ap[:, k, :], in_=hb.ap[:, kk * 128:(kk + 1) * 128],
                                                                     identity=identb), reads=[hb, cbf], writes=[pt])
            S.op("act", lambda e, pt=pt, half=half: e.activation(out=hT_t[j].ap[:, half * 4:half * 4 + 4, :], in_=pt.ap, func=AF.Copy),
                 reads=[pt], writes=[hT_t[j]])

    def post_norm_add(j, src_ap, srcT, gain):
        st = rstd_of(src_ap, srcT, 128, D)
        tmp = rTmp.next()
        S.op("dve", lambda e: e.scalar_tensor_tensor(out=tmp.ap, in0=src_ap, scalar=st.ap[:, 0:1], in1=gain.ap,
                                                     op0=ALU.mult, op1=ALU.mult), reads=[srcT, st, gain], writes=[tmp])
        S.op("dve", lambda e: e.tensor_tensor(out=x_t[j].ap, in0=x_t[j].ap, in1=tmp.ap, op=ALU.add),
             reads=[x_t[j], tmp], writes=[x_t[j]])

    def ffn(layer, tiles, blocks):
        areset()
        rWg = Ring([aalloc([128, 8, 512], BF16) for _ in range(2)])
        rWu = Ring([aalloc([128, 8, 512], BF16) for _ in range(2)])
        rWo = Ring([aalloc([128, 4, D], BF16) for _ in range(2)])
        rAct = Ring([aalloc([128, 4, 512], BF16) for _ in range(2)])
        rSg = Ring([aalloc([128, 512], F32) for _ in range(2)])
        gpre = load_gain(nfp_d[layer:layer + 1, :])
        gpost = load_gain(nfo_d[layer:layer + 1, :])
        groups = [(f0, min(4, 22 - f0)) for f0 in range(0, 22, 4)]

        def load_group(gi):
            f0, nf = groups[gi]
            wg, wu, wo = rWg.next(), rWu.next(), rWo.next()
            dma("pool", wg.ap[:, :, :nf * 128],
                fwi_d[layer, :, f0 * 128:(f0 + nf) * 128].rearrange("(k p) n -> p k n", p=128), writes=[wg])
            dma("pool", wu.ap[:, :, :nf * 128],
                fwi_d[layer, :, DFF + f0 * 128:DFF + (f0 + nf) * 128].rearrange("(k p) n -> p k n", p=128), writes=[wu])
            dma("pool", wo.ap[:, :nf, :],
                fwo_d[layer, f0 * 128:(f0 + nf) * 128, :].rearrange("(c p) n -> p c n", p=128), writes=[wo])
            return wg, wu, wo

        nxt = load_group(0)
        for j in tiles:
            hb = pre_norm(j, gpre, rHb)
            to_hT(j, hb)
        for gi, (f0, nf) in enumerate(groups):
            wg, wu, wo = nxt
            if gi + 1 < len(groups):
                nxt = load_group(gi + 1)
            for (c0, wd) in blocks:
                tl = list(range(c0 // 128, (c0 + wd) // 128))
                hts = [hT_t[j] for j in tl]
                actT = rAct.next()
                for fc in range(nf):
                    pg = rPF.next()
                    for k in range(8):
                        S.op("pe", lambda e, pg=pg, k=k, fc=fc: e.matmul(pg.ap[:, :wd], lhsT=wg.ap[:, k, fc * 128:(fc + 1) * 128],
                                                                         rhs=hTbig[:, k, c0:c0 + wd], start=(k == 0), stop=(k == 7)),
                             reads=[wg] + hts, writes=[pg])
                    pu = rPF.next()
                    for k in range(8):
                        S.op("pe", lambda e, pu=pu, k=k, fc=fc: e.matmul(pu.ap[:, :wd], lhsT=wu.ap[:, k, fc * 128:(fc + 1) * 128],
                                                                         rhs=hTbig[:, k, c0:c0 + wd], start=(k == 0), stop=(k == 7)),
                             reads=[wu] + hts, writes=[pu])
                    sg = rSg.next()
                    S.op("act", lambda e, pg=pg, sg=sg: e.activation(out=sg.ap[:, :wd], in_=pg.ap[:, :wd], func=AF.Silu),
                         reads=[pg], writes=[sg])
                    S.op("dve", lambda e, pu=pu, sg=sg, fc=fc: e.tensor_tensor(out=actT.ap[:, fc, :wd], in0=pu.ap[:, :wd], in1=sg.ap[:, :wd],
                                                                               op=ALU.mult), reads=[pu, sg], writes=[actT])
                for ti, j in enumerate(tl):
                    for dh in range(2):
                        py = rPF.next()
                        for fc in range(nf):
                            S.op("pe", lambda e, py=py, fc=fc, ti=ti, dh=dh: e.matmul(py.ap, lhsT=actT.ap[:, fc, ti * 128:(ti + 1) * 128],
                                                                                      rhs=wo.ap[:, fc, dh * 512:(dh + 1) * 512],
                                                                                      start=(fc == 0), stop=(fc == nf - 1)),
                                 reads=[actT, wo], writes=[py])
                        ysl = y_t[j].ap[:, dh * 512:(dh + 1) * 512]
                        if gi == 0:
                            S.op("act", lambda e, py=py, ysl=ysl: e.activation(out=ysl, in_=py.ap, func=AF.Copy), reads=[py], writes=[y_t[j]])
                        else:
                            S.op("dve", lambda e, py=py, ysl=ysl: e.tensor_tensor(out=ysl, in0=ysl, in1=py.ap, op=ALU.add),
                                 reads=[py, y_t[j]], writes=[y_t[j]])
        for j in tiles:
            post_norm_add(j, y_t[j].ap, y_t[j], gpost)

    pool_prev = [None]

    def pool_phase(sbi, tiles, kinds):
        areset()
        pmb = aalloc([128, 6, 4, 128], BF16)
        dma("pool", pmb.ap, pm_d.rearrange("p (a g t) -> p a g t", a=6, g=4), writes=[pmb])
        pwb = aalloc([128, 4, 2, 256], BF16)
        pstb = aalloc([128, 2, D], BF16)
        dT = Ring([aalloc([128, 8, 128], BF16) for _ in range(2)])
        gpre = load_gain(nmp_d[0:1, :])
        gpost = load_gain(nmo_d[0:1, :])
        gsc = load_gain(psc_d[0:1, :])
        dma("pool", pwb.ap, pw_d.rearrange("g (c p) e -> p g c e", p=128), writes=[pwb])
        if sbi == 0:
            S.op("dve", lambda e: e.memset(pstb.ap, 0.0), writes=[pstb])
            dma("pool", pstb.ap[:120, :, :], pst_d.rearrange("(a r) d -> r a d", a=2), writes=[pstb])
            if stop_after != "nod2d":
                dma("sp", pso_d[:, 0:7, :], pst_d.rearrange("(s b) d -> s b d", b=15)[:, 8:15, :], writes=[Buf()])
        for j, (kind, pi) in zip(tiles, kinds):
            h32 = y_t[j]
            hb = pre_norm(j, gpre, rHbPool, out32=h32)
            if kind == "s":
                for sq in range(16):
                    dma("sp", pso_d[sq, 7:15, :], h32.ap[sq * 8:(sq + 1) * 8, :], reads=[h32], writes=[Buf()])
            elif pi == 15:
                dma("sp", pp_d, h32.ap[113:128, :], reads=[h32], writes=[Buf()])
            d = dT.next()
            phs = [rPH.next() for _ in range(4)]
            for cc in range(8):
                g = cc // 2
                ph = phs[cc // 2]
                o_ap = ph.ap[:, (cc % 2) * 128:(cc % 2) * 128 + 128]
                lhs_cur = hb.ap[:, cc * 128:(cc + 1) * 128]
                if kind == "s":
                    S.op("pe", lambda e, o_ap=o_ap, lhs_cur=lhs_cur, g=g: e.matmul(o_ap, lhsT=lhs_cur, rhs=pmb.ap[:, 3, g, :], start=True, stop=False),
                         reads=[hb, pmb], writes=[ph])
                    for half in range(2):
                        S.op("pe", lambda e, o_ap=o_ap, cc=cc, g=g, half=half: e.matmul(o_ap, lhsT=pstb.ap[:, half, cc * 128:(cc + 1) * 128],
                                                                                        rhs=pmb.ap[:, 4 + half, g, :], start=False, stop=(half == 1)),
                             reads=[pstb, pmb], writes=[ph])
                elif pi == 0:
                    S.op("pe", lambda e, o_ap=o_ap, lhs_cur=lhs_cur, g=g: e.matmul(o_ap, lhsT=lhs_cur, rhs=pmb.ap[:, 1, g, :], start=True, stop=True),
                         reads=[hb, pmb], writes=[ph])
                else:
                    prev = pool_prev[0]
                    S.op("pe", lambda e, o_ap=o_ap, lhs_cur=lhs_cur, g=g: e.matmul(o_ap, lhsT=lhs_cur, rhs=pmb.ap[:, 0, g, :], start=True, stop=False),
                         reads=[hb, pmb], writes=[ph])
                    S.op("pe", lambda e, o_ap=o_ap, cc=cc, g=g, prev=prev: e.matmul(o_ap, lhsT=prev.ap[:, cc * 128:(cc + 1) * 128], rhs=pmb.ap[:, 2, g, :],
                                                                                    start=False, stop=True), reads=[prev, pmb], writes=[ph])
            for q in range(4):
                S.op("act", lambda e, q=q: e.activation(out=d.ap[:, 2 * q:2 * q + 2, :], in_=phs[q].ap.rearrange("p (a t) -> p a t", a=2), func=AF.Copy),
                     reads=[phs[q]], writes=[d])
            if kind == "p":
                pool_prev[0] = hb
            pys = [rPB.next(), rPB.next()]
            for g in range(4):
                py = pys[g // 2]
                o_ap = py.ap[:, (g % 2) * 256:(g % 2) * 256 + 256]
                for c in range(2):
                    S.op("pe", lambda e, o_ap=o_ap, g=g, c=c: e.matmul(o_ap, lhsT=d.ap[:, 2 * g + c, :], rhs=pwb.ap[:, g, c, :],
                                                                       start=(c == 0), stop=(c == 1)), reads=[d, pwb], writes=[py])
            m32 = rTmp.next()
            for hh in range(2):
                S.op("dve", lambda e, hh=hh: e.tensor_tensor(out=m32.ap[:, hh * 512:(hh + 1) * 512], in0=pys[hh].ap,
                                                             in1=gsc.ap[:, hh * 512:(hh + 1) * 512], op=ALU.mult),
                     reads=[pys[hh], gsc], writes=[m32])
            post_norm_add(j, m32.ap, m32, gpost)


    S32 = [tsb([128, 128], F32, f"S32_{h}") for h in range(16)]
    Sbf = [tsb([128, 128], BF16, f"Sbf_{h}") for h in range(16)]
    ctail = tsb([128, 32, 3], F32, "ctail")

    def gdn_phase(sbi, tiles, kinds, blocks):
        NTl = len(tiles)
        Ncol = NTl * 128
        has_s = kinds[0][0] == "s"
        p0 = 128 if has_s else 0
        Np = Ncol - p0
        areset()
        if dbg2_d is not None and sbi == 0:
            dma("sp", dbg2_d.rearrange("p (a b) -> p a b", a=9), xbig, reads=x_t, writes=[Buf()])
        gpre = load_gain(nmp_d[1:2, :])
        for j in tiles:
            hb = pre_norm(j, gpre, rHb)
            to_hT(j, hb)
        areset()
        gs = aalloc([128, NTl, 6, 16], F32)
        glb = aalloc([128, NTl, 16], F32)
        gls = aalloc([128, 16, 16], F32)
        gblk = aalloc([128, 16, 16], F32)
        wba = aalloc([128, 8, 128], BF16)
        dtb = aalloc([128, 16], F32)
        nea = aalloc([128, 16], F32)
        cw = aalloc([128, 32, 4], F32)
        onb = aalloc([128, 128], F32)
        t16 = Ring([aalloc([128, 16], F32) for _ in range(3)])
        S.op("dve", lambda e: e.memset(wba.ap, 0.0), writes=[wba])
        S.op("dve", lambda e: e.memset(gs.ap, 0.0), writes=[gs])
        dma("pool", wba.ap[:, :, 0:32], gwi_d[:, 6144:6176].rearrange("(k p) n -> p k n", p=128), writes=[wba])
        dma("sp", dtb.ap, gdtb_d.partition_broadcast(128), writes=[dtb])
        dma("sp", nea.ap, galog_d.partition_broadcast(128), writes=[nea])
        dma("sp", cw.ap, gcw_d.rearrange("(c p) t -> p c t", p=128), writes=[cw])
        dma("sp", onb.ap, gon_d.partition_broadcast(128), writes=[onb])
        S.op("act", lambda e: e.activation(out=nea.ap, in_=nea.ap, func=AF.Exp), reads=[nea], writes=[nea])
        S.op("dve", lambda e: e.tensor_scalar(out=nea.ap, in0=nea.ap, scalar1=-1.0, scalar2=None, op0=ALU.mult), reads=[nea], writes=[nea])
        for j, (kind, pi) in zip(tiles, kinds):
            cols = slice(j * 128, (j + 1) * 128)
            pm = rPM.next()
            for k in range(8):
                S.op("pe", lambda e, k=k: e.matmul(pm.ap, lhsT=hTbig[:, k, cols], rhs=wba.ap[:, k, :], start=(k == 0), stop=(k == 7)),
                     reads=[hT_t[j], wba], writes=[pm])
            G = lambda q: gs.ap[:, j, q, :]
            ta, tb = t16.next(), t16.next()
            S.op("act", lambda e: e.activation(out=ta.ap, in_=pm.ap[:, 0:16], func=AF.Exp, scale=-1.0), reads=[pm], writes=[ta])
            S.op("dve", lambda e: e.tensor_tensor(out=tb.ap, in0=pm.ap[:, 16:32], in1=dtb.ap, op=ALU.add), reads=[pm, dtb], writes=[tb])
            S.op("dve", lambda e: e.tensor_scalar(out=ta.ap, in0=ta.ap, scalar1=1.0, scalar2=None, op0=ALU.add), reads=[ta], writes=[ta])
            S.op("dve", lambda e: e.reciprocal(out=G(1), in_=ta.ap), reads=[ta], writes=[gs])
            S.op("dve", lambda e: e.tensor_scalar(out=G(2), in0=G(1), scalar1=-1.0, scalar2=None, op0=ALU.mult), reads=[gs], writes=[gs])
            S.op("act", lambda e: e.activation(out=tb.ap, in_=tb.ap, func=AF.Exp), reads=[tb], writes=[tb])
            S.op("act", lambda e: e.activation(out=tb.ap, in_=tb.ap, func=AF.Ln, bias=1.0, scale=1.0), reads=[tb], writes=[tb])
            S.op("dve", lambda e: e.tensor_tensor(out=G(0), in0=tb.ap, in1=nea.ap, op=ALU.mult), reads=[tb, nea], writes=[gs])
            pm2 = rPM.next()
            pm3 = rPM.next()
            grow = gs.ap[:, j, :, :].rearrange("p a b -> p (a b)")
            S.op("pe", lambda e: e.matmul(pm2.ap[:, 0:96], lhsT=(cUb if kind == "s" else cU), rhs=grow, start=True, stop=True), reads=[c32, gs], writes=[pm2])
            S.op("pe", lambda e: e.matmul(pm3.ap[:, 0:96], lhsT=(cBO if kind == "s" else cONES), rhs=grow, start=True, stop=True), reads=[c32, gs], writes=[pm3])
            S.op("dve", lambda e: e.tensor_copy(out=G(3), in_=pm2.ap[:, 0:16]), reads=[pm2], writes=[gs])
            S.op("act", lambda e: e.activation(out=G(4), in_=pm2.ap[:, 0:16], func=AF.Exp), reads=[pm2], writes=[gs])
            S.op("act", lambda e: e.activation(out=glb.ap[:, j, :], in_=pm3.ap[:, 0:16], func=AF.Exp), reads=[pm3], writes=[glb])
            tc_ = t16.next()
            S.op("dve", lambda e: e.tensor_tensor(out=tc_.ap, in0=pm3.ap[:, 0:16], in1=G(3), op=ALU.subtract), reads=[pm3, gs], writes=[tc_])
            S.op("act", lambda e: e.activation(out=G(5), in_=tc_.ap, func=AF.Exp), reads=[tc_], writes=[gs])
            if kind == "s":
                S.op("dve", lambda e: e.tensor_tensor(out=gblk.ap, in0=G(0).unsqueeze(1).to_broadcast([128, 16, 16]),
                                                      in1=cBM.unsqueeze(2).to_broadcast([128, 16, 16]), op=ALU.mult), reads=[gs, c32], writes=[gblk])
                pb = rPB.next()
                S.op("pe", lambda e: e.matmul(pb.ap[:, 0:256], lhsT=cONES, rhs=gblk.ap.rearrange("p a b -> p (a b)"), start=True, stop=True),
                     reads=[c32, gblk], writes=[pb])
                S.op("act", lambda e: e.activation(out=gls.ap.rearrange("p a b -> p (a b)"), in_=pb.ap[:, 0:256], func=AF.Exp), reads=[pb], writes=[gls])

        onT = T(ybig.bitcast(BF16)[:, :16 * Ncol].rearrange("p (a b) -> p a b", a=16), [b for t in y_t for b in t.b])
        rPBg = Ring([PF[4], PF[5]])
        qTs = [aalloc([128, Ncol], BF16) for _ in range(2)]
        kTs = [aalloc([128, Ncol], BF16) for _ in range(2)]
        ktoks = [aalloc([128, NTl, 128], BF16) for _ in range(2)]
        vtoks = [aalloc([128, NTl, 128], BF16) for _ in range(4)]
        zss = [aalloc([128, NTl, 128], BF16) for _ in range(4)]
        umark = apos[0]
        wq = aalloc([128, 8, 128], BF16)
        wk = aalloc([128, 8, 128], BF16)
        wv = aalloc([128, 8, 128], BF16)
        wz = aalloc([128, 8, 128], BF16)
        pre = aalloc([128, 3 + 1024], F32)
        pres = aalloc([128, 16, 11], F32)
        co = aalloc([128, Ncol], F32)
        sq = aalloc([128, 512], BF16)
        rn = aalloc([128, 512], F32)
        vT = aalloc([128, Ncol], BF16)
        uend = apos[0]
        apos[0] = umark
        chains = []
        cstart = []
        for c in range(4):
            cstart.append(apos[0])
            bb_ = aalloc([128, 8, 128], BF16)
            chains.append(dict(b=[T(bb_.ap[:, i_, :], bb_.b[i_ // 2:i_ // 2 + 1]) for i_ in range(8)],
                               f=[aalloc([128, 128], F32) for _ in range(4)],
                               x=[aalloc([128, 2, 128], F32) for _ in range(2)],
                               r=[aalloc([128, 128], F32) for _ in range(2)]))
        uend = max(uend, apos[0])
        if has_s:
            apos[0] = cstart[1]
            padw = aalloc([128, 16 * 136], BF16)
            padq = aalloc([128, 16 * 136], BF16)
            kpad = aalloc([128, 16, 128], BF16)
            s0f = aalloc([128, 8, 128], F32)
            s0b = aalloc([128, 8, 128], BF16)
            snw = aalloc([128, 8, 128], F32)
            uend = max(uend, apos[0])
        apos[0] = uend
        f128 = Ring(chains[0]["f"])
        b128 = Ring(chains[0]["b"])
        xx = Ring(chains[0]["x"])
        rR = Ring(chains[0]["r"])

        def proj_conv(w, cidx, silu=True):
            for (c0, wd) in blocks:
                ps = rPBg.next()
                tl = [hT_t[t] for t in range(c0 // 128, (c0 + wd) // 128)]
                for k in range(8):
                    S.op("pe", lambda e, k=k: e.matmul(ps.ap[:, :wd], lhsT=w.ap[:, k, :], rhs=hTbig[:, k, c0:c0 + wd], start=(k == 0), stop=(k == 7)),
                         reads=[w] + tl, writes=[ps])
                if has_s and c0 == 0:
                    S.op("act", lambda e: e.activation(out=pres.ap[:, :, 3:11], in_=ps.ap[:, 0:128].rearrange("p (a b) -> p a b", a=16), func=AF.Copy),
                         reads=[ps], writes=[pres])
                else:
                    S.op("act", lambda e: e.activation(out=pre.ap[:, 3 + c0 - p0:3 + c0 - p0 + wd], in_=ps.ap[:, :wd], func=AF.Copy), reads=[ps], writes=[pre])
            if sbi == 0:
                S.op("dve", lambda e: e.memset(pre.ap[:, 0:3], 0.0), writes=[pre])
            else:
                S.op("dve", lambda e: e.tensor_copy(out=pre.ap[:, 0:3], in_=ctail.ap[:, cidx, :]), reads=[ctail], writes=[pre])
            if has_s:
                dma("sp", pres.ap[:, :, 0:3], cst_d[cidx * 128:(cidx + 1) * 128, :, :], writes=[pres])
            if sbi == 0:
                S.op("dve", lambda e: e.tensor_copy(out=ctail.ap[:, cidx, :], in_=pre.ap[:, Np:Np + 3]), reads=[pre], writes=[ctail])
            else:
                dma("sp", cp_d[cidx * 128:(cidx + 1) * 128, :], pre.ap[:, Np:Np + 3], reads=[pre], writes=[Buf()])
            if has_s:
                dma("sp", cso_d[cidx * 128:(cidx + 1) * 128, :, :], pres.ap[:, :, 8:11], reads=[pres], writes=[Buf()])
            views = [(co.ap[:, p0:Ncol], lambda tap: pre.ap[:, tap:tap + Np], pre)]
            if has_s:
                views.append((co.ap[:, 0:128].rearrange("p (a b) -> p a b", a=16), lambda tap: pres.ap[:, :, tap:tap + 8], pres))
            for (o_ap, src, srcT) in views:
                S.op("dve", lambda e: e.tensor_scalar(out=o_ap, in0=src(0), scalar1=cw.ap[:, cidx, 0:1], scalar2=None, op0=ALU.mult),
                     reads=[srcT, cw], writes=[co])
                for tap in range(1, 4):
                    S.op("dve", lambda e, tap=tap: e.scalar_tensor_tensor(out=o_ap, in0=src(tap), scalar=cw.ap[:, cidx, tap:tap + 1], in1=o_ap,
                                                                           op0=ALU.mult, op1=ALU.add), reads=[srcT, cw, co], writes=[co])
            S.op("act", lambda e: e.activation(out=co.ap, in_=co.ap, func=AF.Silu), reads=[co], writes=[co])

        def l2n(dst, scale):
            for (c0, wd) in blocks:
                S.op("dve", lambda e: e.tensor_tensor(out=sq.ap[:, :wd], in0=co.ap[:, c0:c0 + wd], in1=co.ap[:, c0:c0 + wd], op=ALU.mult), reads=[co], writes=[sq])
                ps = rPBg.next()
                S.op("pe", lambda e: e.matmul(ps.ap[:, :wd], lhsT=onesb, rhs=sq.ap[:, :wd], start=True, stop=True), reads=[cbf, sq], writes=[ps])
                S.op("act", lambda e: e.activation(out=rn.ap[:, :wd], in_=ps.ap[:, :wd], func=AF.Ln, bias=1e-6, scale=1.0), reads=[ps], writes=[rn])
                S.op("act", lambda e: e.activation(out=rn.ap[:, :wd], in_=rn.ap[:, :wd], func=AF.Exp, scale=-0.5), reads=[rn], writes=[rn])
                S.op("dve", lambda e: e.scalar_tensor_tensor(out=dst.ap[:, c0:c0 + wd], in0=co.ap[:, c0:c0 + wd], scalar=scale, in1=rn.ap[:, :wd],
                                                             op0=ALU.mult, op1=ALU.mult), reads=[co, rn], writes=[dst])

        def to_tok(srcT, dst):
            for j in tiles:
                pt = rPTq.next()
                S.op("pe", lambda e: e.transpose(out=pt.ap, in_=srcT.ap[:, j * 128:(j + 1) * 128], identity=identb), reads=[srcT, cbf], writes=[pt])
                S.op("act", lambda e: e.activation(out=dst.ap[:, j, :], in_=pt.ap, func=AF.Copy), reads=[pt], writes=[dst])

        def wload(w, col0):
            dma("pool", w.ap, gwi_d[:, col0:col0 + 128].rearrange("(k p) n -> p k n", p=128), writes=[w])

        def mm(out_t, out_ap, lhsT, rhs, R, start=True, stop=True):
            S.op("pe", lambda e: e.matmul(out_ap, lhsT=lhsT, rhs=rhs, start=start, stop=stop), reads=R, writes=[out_t])


        def zproj(w, dst):
            for j in tiles:
                pm = rPM.next()
                for k in range(8):
                    mm(pm, pm.ap, hTbig[:, k, j * 128:(j + 1) * 128], w.ap[:, k, :], [hT_t[j], w], start=(k == 0), stop=(k == 7))
                S.op("act", lambda e: e.activation(out=dst.ap[:, j, :], in_=pm.ap, func=AF.Silu), reads=[pm], writes=[dst])

        def finish_o(po_ap, po, on, ob, zs, j, h):
            st = rstd_of(po_ap, po, 128, 128)
            S.op("dve", lambda e: e.scalar_tensor_tensor(out=on.ap, in0=po_ap, scalar=st.ap[:, 0:1], in1=onb.ap, op0=ALU.mult, op1=ALU.mult),
                 reads=[po, st, onb], writes=[on])
            S.op("dve", lambda e: e.tensor_tensor(out=ob.ap, in0=on.ap, in1=zs.ap[:, j, :], op=ALU.mult), reads=[on, zs], writes=[ob])

        def sample_state(h, qT, kT, ktok, vtok, zs):
            j = 0
            smp = True
            cols = slice(0, 128)
            G = lambda q: gs.ap[:, j, q, h:h + 1]
            pkq = rPH.next()
            mm(pkq, pkq.ap[:, 0:128], kT.ap[:, cols], qT.ap[:, cols], [kT, qT])
            mm(pkq, pkq.ap[:, 128:256], kT.ap[:, cols], kT.ap[:, cols], [kT])
            prow = rPM.next()
            mm(prow, prow.ap, gs.ap[:, j, 0, h:h + 1].to_broadcast([128, 128]), cUb, [gs, c32])
            E = f128.next()
            S.op("dve", lambda e: e.scalar_tensor_tensor(out=E.ap, in0=prow.ap, scalar=G(3), in1=cNMb, op0=ALU.subtract, op1=ALU.add),
                 reads=[prow, gs, c32], writes=[E])
            S.op("act", lambda e: e.activation(out=E.ap, in_=E.ap, func=AF.Exp), reads=[E], writes=[E])
            egb = f128.next()
            S.op("act", lambda e: e.activation(out=egb.ap, in_=prow.ap, func=AF.Exp), reads=[prow], writes=[egb])
            qkT = b128.next()
            S.op("dve", lambda e: e.tensor_tensor(out=qkT.ap, in0=pkq.ap[:, 0:128], in1=E.ap, op=ALU.mult), reads=[pkq, E], writes=[qkT])
            dS = f128.next()
            S.op("dve", lambda e: e.tensor_tensor(out=dS.ap, in0=E.ap, in1=cSTb, op=ALU.mult), reads=[E, c32], writes=[dS])
            X = xx.next()
            S.op("dve", lambda e: e.scalar_tensor_tensor(out=X.ap[:, 0, :], in0=pkq.ap[:, 128:256], scalar=G(2), in1=dS.ap, op0=ALU.mult, op1=ALU.mult),
                 reads=[pkq, gs, dS], writes=[X])
            pt = rPM.next()
            mm(pt, pt.ap, X.ap[:, 0, :], cID, [X, c32])
            S.op("act", lambda e: e.activation(out=X.ap[:, 1, :], in_=pt.ap, func=AF.Copy), reads=[pt], writes=[X])
            Rm = rR.next()
            S.op("dve", lambda e: e.tensor_tensor(out=Rm.ap, in0=X.ap[:, 0, :], in1=cID, op=ALU.add), reads=[X, c32], writes=[Rm])
            nlev = 2
            for lev in range(nlev):
                px = rPH.next()
                last = lev == nlev - 1
                mm(px, px.ap[:, 128:256], X.ap[:, 0, :], X.ap[:, 1, :], [X])
                if not last:
                    mm(px, px.ap[:, 0:128], X.ap[:, 1, :], X.ap[:, 0, :], [X])
                X2 = xx.next()
                if last:
                    S.op("act", lambda e: e.activation(out=X2.ap[:, 1, :], in_=px.ap[:, 128:256], func=AF.Copy), reads=[px], writes=[X2])
                else:
                    S.op("act", lambda e: e.activation(out=X2.ap, in_=px.ap.rearrange("p (a b) -> p a b", a=2), func=AF.Copy), reads=[px], writes=[X2])
                pr = rPM.next()
                mm(pr, pr.ap, X2.ap[:, 1, :], Rm.ap, [X2, Rm])
                R2 = rR.next()
                S.op("dve", lambda e: e.tensor_tensor(out=R2.ap, in0=pr.ap, in1=Rm.ap, op=ALU.add), reads=[pr, Rm], writes=[R2])
                Rm, X = R2, X2
            Rb = b128.next()
            S.op("act", lambda e: e.activation(out=Rb.ap, in_=Rm.ap, func=AF.Copy), reads=[Rm], writes=[Rb])
            Rm = Rb
            kg = b128.next()
            S.op("act", lambda e: e.mul(out=kg.ap, in_=ktok.ap[:, j, :], mul=G(4)), reads=[ktok, gs], writes=[kg])
            kd = b128.next()
            S.op("act", lambda e: e.mul(out=kd.ap, in_=ktok.ap[:, j, :], mul=G(5)), reads=[ktok, gs], writes=[kd])
            pw_ = rPM.next()
            mm(pw_, pw_.ap, kg.ap, Rm.ap, [kg, Rm])
            qd = b128.next()
            S.op("dve", lambda e: e.tensor_tensor(out=qd.ap, in0=qT.ap[:, cols], in1=egb.ap, op=ALU.mult), reads=[qT, egb], writes=[qd])
            vn = b128.next()
            if True:
                        pw3 = padw.ap[:, :].rearrange("p (a b) -> p a b", b=136)[:, :, 0:8]
                        pq3 = padq.ap[:, :].rearrange("p (a b) -> p a b", b=136)[:, :, 0:8]
                        S.op("dve", lambda e: e.tensor_scalar(out=pw3, in0=pw_.ap.rearrange("p (a b) -> p a b", a=16), scalar1=-1.0, scalar2=None, op0=ALU.mult),
                             reads=[pw_], writes=[padw])
                        S.op("dve", lambda e: e.tensor_copy(out=pq3, in_=qd.ap.rearrange("p (a b) -> p a b", a=16)), reads=[qd], writes=[padq])
                        S.op("dve", lambda e: e.tensor_tensor(out=kpad.ap, in0=kd.ap.unsqueeze(1).to_broadcast([128, 16, 128]),
                                                              in1=cBM.unsqueeze(2).to_broadcast([128, 16, 128]), op=ALU.mult), reads=[kd, c32], writes=[kpad])
                        pv = rPM.next()
                        po = rPH.next()
                        po_ap = po.ap[:, 0:128]
                        for half in range(2):
                            dma("sp", s0f.ap, rst_d[half * 8:(half + 1) * 8, h].rearrange("s k v -> k s v"), writes=[s0f])
                            dma("pool", s0b.ap, rst_d[half * 8:(half + 1) * 8, h].rearrange("s k v -> k s v"), writes=[s0b])
                            if half == 0:
                                mm(pv, pv.ap, Rm.ap, vtok.ap[:, j, :], [Rm, vtok], start=True, stop=False)
                            for sl in range(8):
                                sq_ = half * 8 + sl
                                mm(pv, pv.ap, padw.ap[:, sq_ * 128:(sq_ + 1) * 128], s0b.ap[:, sl, :], [padw, s0b], start=False, stop=(sq_ == 15))
                            for sl in range(8):
                                sq_ = half * 8 + sl
                                mm(po, po_ap, padq.ap[:, sq_ * 128:(sq_ + 1) * 128], s0b.ap[:, sl, :], [padq, s0b], start=(sq_ == 0), stop=False)
                            if half == 1:
                                S.op("act", lambda e: e.mul(out=vn.ap, in_=pv.ap, mul=G(1)), reads=[pv, gs], writes=[vn])
                                mm(po, po_ap, qkT.ap, vn.ap, [qkT, vn], start=False, stop=True)
                        for half in range(2):
                            dma("sp", s0f.ap, rst_d[half * 8:(half + 1) * 8, h].rearrange("s k v -> k s v"), writes=[s0f])
                            for qd4 in range(2):
                                pb = rPB.next()
                                for s4 in range(4):
                                    sq_ = half * 8 + qd4 * 4 + s4
                                    mm(pb, pb.ap[:, s4 * 128:(s4 + 1) * 128], kpad.ap[:, sq_, :], vn.ap, [kpad, vn])
                                sl0 = qd4 * 4
                                S.op("dve", lambda e: e.tensor_tensor(out=snw.ap[:, sl0:sl0 + 4, :], in0=s0f.ap[:, sl0:sl0 + 4, :],
                                                                      in1=gls.ap[:, half * 8 + sl0:half * 8 + sl0 + 4, h:h + 1].to_broadcast([128, 4, 128]), op=ALU.mult),
                                     reads=[s0f, gls], writes=[snw])
                                S.op("dve", lambda e: e.tensor_tensor(out=snw.ap[:, sl0:sl0 + 4, :], in0=snw.ap[:, sl0:sl0 + 4, :],
                                                                      in1=pb.ap.rearrange("p (a b) -> p a b", a=4), op=ALU.add), reads=[snw, pb], writes=[snw])
                            dma("sp", rso_d[half * 8:(half + 1) * 8, h].rearrange("s k v -> k s v"), snw.ap, reads=[snw], writes=[Buf()])

            on = f128.next()
            ob = b128.next()
            finish_o(po_ap, po, on, ob, zs, j, h)
            pt2 = rPTq.next()
            S.op("pe", lambda e: e.transpose(out=pt2.ap, in_=ob.ap, identity=identb), reads=[ob, cbf], writes=[pt2])
            S.op("act", lambda e: e.activation(out=onT.ap[:, h, cols], in_=pt2.ap, func=AF.Copy), reads=[pt2], writes=[onT])

        def chain(c, h, qT, kT, ktok, vtok, zs):
            cb = chains[c]
            QB = [T(banks[c][:, q * 128:(q + 1) * 128], PF[c].b) for q in range(4)]
            F0 = T(banks[c][:, 256:384], PF[c].b)
            F1 = T(banks[c][:, 384:512], PF[c].b)
            F01 = T(banks[c][:, 256:512].rearrange("p (a b) -> p a b", a=2), PF[c].b)
            qkT, Rb, kg, kd, qd, vn, nw, ob = cb["b"]
            E, egb, dS, on = cb["f"]
            for j, (kind, pi) in zip(tiles, kinds):
                if kind == "s":
                    continue
                cols = slice(j * 128, (j + 1) * 128)
                G = lambda q: gs.ap[:, j, q, h:h + 1]
                mm(QB[0], QB[0].ap, kT.ap[:, cols], qT.ap[:, cols], [kT, qT])
                mm(QB[1], QB[1].ap, kT.ap[:, cols], kT.ap[:, cols], [kT])
                mm(F0, F0.ap, gs.ap[:, j, 0, h:h + 1].to_broadcast([128, 128]), cU, [gs, c32])
                yield
                S.op("dve", lambda e: e.scalar_tensor_tensor(out=E.ap, in0=F0.ap, scalar=G(3), in1=cNM, op0=ALU.subtract, op1=ALU.add),
                     reads=[F0, gs, c32], writes=[E])
                S.op("act", lambda e: e.activation(out=egb.ap, in_=F0.ap, func=AF.Exp), reads=[F0], writes=[egb])
                yield
                S.op("act", lambda e: e.activation(out=E.ap, in_=E.ap, func=AF.Exp), reads=[E], writes=[E])
                yield
                S.op("dve", lambda e: e.tensor_tensor(out=qkT.ap, in0=QB[0].ap, in1=E.ap, op=ALU.mult), reads=[QB[0], E], writes=[qkT])
                S.op("dve", lambda e: e.tensor_tensor(out=dS.ap, in0=E.ap, in1=cST, op=ALU.mult), reads=[E, c32], writes=[dS])
                X = cb["x"][0]
                S.op("dve", lambda e: e.scalar_tensor_tensor(out=X.ap[:, 0, :], in0=QB[1].ap, scalar=G(2), in1=dS.ap, op0=ALU.mult, op1=ALU.mult),
                     reads=[QB[1], gs, dS], writes=[X])
                S.op("dve", lambda e: e.tensor_tensor(out=qd.ap, in0=qT.ap[:, cols], in1=egb.ap, op=ALU.mult), reads=[qT, egb], writes=[qd])
                yield
                mm(F1, F1.ap, X.ap[:, 0, :], cID, [X, c32])
                Rm = cb["r"][0]
                S.op("dve", lambda e: e.tensor_tensor(out=Rm.ap, in0=X.ap[:, 0, :], in1=cID, op=ALU.add), reads=[X, c32], writes=[Rm])
                S.op("act", lambda e: e.mul(out=kg.ap, in_=ktok.ap[:, j, :], mul=G(4)), reads=[ktok, gs], writes=[kg])
                S.op("act", lambda e: e.mul(out=kd.ap, in_=ktok.ap[:, j, :], mul=G(5)), reads=[ktok, gs], writes=[kd])
                yield
                S.op("act", lambda e: e.activation(out=X.ap[:, 1, :], in_=F1.ap, func=AF.Copy), reads=[F1], writes=[X])
                yield
                for lev in range(6):
                    last = lev == 5
                    Xn = cb["x"][(lev + 1) % 2]
                    Rn = cb["r"][(lev + 1) % 2]
                    mm(F1, F1.ap, X.ap[:, 0, :], X.ap[:, 1, :], [X])
                    if not last:
                        mm(F0, F0.ap, X.ap[:, 1, :], X.ap[:, 0, :], [X])
                    yield
                    if last:
                        S.op("act", lambda e: e.activation(out=Xn.ap[:, 1, :], in_=F1.ap, func=AF.Copy), reads=[F1], writes=[Xn])
                    else:
                        S.op("act", lambda e: e.activation(out=Xn.ap, in_=F01.ap, func=AF.Copy), reads=[F01], writes=[Xn])
                    yield
                    pr = F0
                    mm(pr, pr.ap, Xn.ap[:, 1, :], Rm.ap, [Xn, Rm])
                    yield
                    S.op("dve", lambda e: e.tensor_tensor(out=Rn.ap, in0=pr.ap, in1=Rm.ap, op=ALU.add), reads=[pr, Rm], writes=[Rn])
                    yield
                    X, Rm = Xn, Rn
                S.op("act", lambda e: e.activation(out=Rb.ap, in_=Rm.ap, func=AF.Copy), reads=[Rm], writes=[Rb])
                yield
                mm(QB[2], QB[2].ap, kg.ap, Rb.ap, [kg, Rb])
                yield
                S.op("dve", lambda e: e.tensor_scalar(out=nw.ap, in0=QB[2].ap, scalar1=-1.0, scalar2=None, op0=ALU.mult), reads=[QB[2]], writes=[nw])
                yield
                mm(QB[3], QB[3].ap, Rb.ap, vtok.ap[:, j, :], [Rb, vtok], start=True, stop=False)
                mm(QB[3], QB[3].ap, nw.ap, Sbf[h].ap, [nw, Sbf[h]], start=False, stop=True)
                yield
                S.op("act", lambda e: e.mul(out=vn.ap, in_=QB[3].ap, mul=G(1)), reads=[QB[3], gs], writes=[vn])
                yield
                mm(QB[0], QB[0].ap, qd.ap, Sbf[h].ap, [qd, Sbf[h]], start=True, stop=False)
                mm(QB[0], QB[0].ap, qkT.ap, vn.ap, [qkT, vn], start=False, stop=True)
                mm(QB[1], QB[1].ap, kd.ap, vn.ap, [kd, vn])
                yield
                S.op("dve", lambda e: e.scalar_tensor_tensor(out=S32[h].ap, in0=S32[h].ap, scalar=glb.ap[:, j, h:h + 1], in1=QB[1].ap,
                                                             op0=ALU.mult, op1=ALU.add), reads=[S32[h], glb, QB[1]], writes=[S32[h]])
                st = rstd_of(QB[0].ap, QB[0], 128, 128)
                yield
                S.op("act", lambda e: e.activation(out=Sbf[h].ap, in_=S32[h].ap, func=AF.Copy), reads=[S32[h]], writes=[Sbf[h]])
                S.op("dve", lambda e: e.scalar_tensor_tensor(out=on.ap, in0=QB[0].ap, scalar=st.ap[:, 0:1], in1=onb.ap, op0=ALU.mult, op1=ALU.mult),
                     reads=[QB[0], st, onb], writes=[on])
                S.op("dve", lambda e: e.tensor_tensor(out=ob.ap, in0=on.ap, in1=zs.ap[:, j, :], op=ALU.mult), reads=[on, zs], writes=[ob])
                if pi == 15:
                    dma("sp", rp_d[h], S32[h].ap, reads=[S32[h]], writes=[Buf()])
                yield
                pt2 = rPTq.next()
                S.op("pe", lambda e: e.transpose(out=pt2.ap, in_=ob.ap, identity=identb), reads=[ob, cbf], writes=[pt2])
                yield
                S.op("act", lambda e: e.activation(out=onT.ap[:, h, cols], in_=pt2.ap, func=AF.Copy), reads=[pt2], writes=[onT])
                yield

        for grp in range(4):
            heads = [4 * grp + i for i in range(4)]
            for kk in range(2):
                kh = 2 * grp + kk
                wload(wq, kh * 128)
                wload(wk, 1024 + kh * 128)
                proj_conv(wq, kh)
                l2n(qTs[kk], 128.0 ** -0.5)
                proj_conv(wk, 8 + kh)
                l2n(kTs[kk], 1.0)
                to_tok(kTs[kk], ktoks[kk])
            for c, h in enumerate(heads):
                wload(wv, 2048 + h * 128)
                wload(wz, 4096 + h * 128)
                proj_conv(wv, 16 + h)
                S.op("act", lambda e: e.activation(out=vT.ap, in_=co.ap, func=AF.Copy), reads=[co], writes=[vT])
                to_tok(vT, vtoks[c])
                zproj(wz, zss[c])
                if sbi == 0:
                    S.op("dve", lambda e: e.memset(S32[h].ap, 0.0), writes=[S32[h]])
                    S.op("dve", lambda e: e.memset(Sbf[h].ap, 0.0), writes=[Sbf[h]])
            if has_s:
                S.op("dve", lambda e: e.memset(padw.ap, 0.0), writes=[padw])
                S.op("dve", lambda e: e.memset(padq.ap, 0.0), writes=[padq])
                for c, h in enumerate(heads):
                    sample_state(h, qTs[c // 2], kTs[c // 2], ktoks[c // 2], vtoks[c], zss[c])
            allg = [chain(c, h, qTs[c // 2], kTs[c // 2], ktoks[c // 2], vtoks[c], zss[c]) for c, h in enumerate(heads)]
            gw = 4
            for g0 in range(0, 4, gw):
              gens = allg[g0:g0 + gw]
              while gens:
                alive = []
                for g_ in gens:
                    try:
                        next(g_)
                        alive.append(g_)
                    except StopIteration:
                        pass
                gens = alive

        if dbg_d is not None and sbi == 0:
            dma("sp", dbg_d, ybig, reads=[onT], writes=[Buf()])
        areset()
        gpost = aalloc([128, D], F32)
        dma("sp", gpost.ap, nmo_d[1:2, :].partition_broadcast(128), writes=[gpost])
        wo_r = Ring([aalloc([128, 4, D], BF16) for _ in range(2)])
        wos = []
        for j in tiles:
            pys = [rPF.next(), rPF.next()]
            for hg in range(4):
                wo = wo_r.next()
                dma("pool", wo.ap, gwo_d[hg * 512:(hg + 1) * 512, :].rearrange("(c p) n -> p c n", p=128), writes=[wo])
                for hh in range(4):
                    h = hg * 4 + hh
                    for dh in range(2):
                        mm(pys[dh], pys[dh].ap, onT.ap[:, h, j * 128:(j + 1) * 128], wo.ap[:, hh, dh * 512:(dh + 1) * 512], [onT, wo],
                           start=(h == 0), stop=(h == 15))
            m32 = rTmp.next()
            for dh in range(2):
                S.op("act", lambda e, dh=dh: e.activation(out=m32.ap[:, dh * 512:(dh + 1) * 512], in_=pys[dh].ap, func=AF.Copy), reads=[pys[dh]], writes=[m32])
            post_norm_add(j, m32.ap, m32, gpost)

    for sbi in range(2):
        if sbi == 0:
            kinds = [("s", None)] + [("p", i) for i in range(8)]
            blocks = [(0, 128), (128, 512), (640, 512)]
        else:
            kinds = [("p", 8 + i) for i in range(8)]
            blocks = [(0, 512), (512, 512)]
        tiles = list(range(len(kinds)))
        for j, (kind, pi) in zip(tiles, kinds):
            src = xs_d if kind == "s" else xp_d[pi * 128:(pi + 1) * 128, :]
            dma("sp", x_t[j].ap, src, writes=[x_t[j]])
        pool_phase(sbi, tiles, kinds)
        if stop_after not in ("pool", "nod2d"):
            ffn(0, tiles, blocks)
        if stop_after not in ("pool", "nod2d", "l0"):
            gdn_phase(sbi, tiles, kinds, blocks)
            if stop_after != "gdn":
                ffn(1, tiles, blocks)
        for j, (kind, pi) in zip(tiles, kinds):
            dst = ys_d if kind == "s" else yp_d[pi * 128:(pi + 1) * 128, :]
            dma("sp", dst, x_t[j].ap, reads=[x_t[j]], writes=[Buf()])

    with nc.allow_low_precision("bf16 matmul operands, fp32 accumulation"):
        S.emit()
    return nc


_NC_CACHE = {}


def make_in_maps(inp):
    c32, pm = _consts()
    f = lambda a: np.ascontiguousarray(np.asarray(a, dtype=np.float32))
    shared = {
        "nmp": f(inp["norm_mix_pre"]), "nmo": f(inp["norm_mix_post"]),
        "nfp": f(inp["norm_ffn_pre"]), "nfo": f(inp["norm_ffn_post"]),
        "pw": f(inp["pool_w"][0]), "psc": f(inp["pool_scale"]),
        "gwi": f(inp["gdn_w_in"][0]), "gcw": f(np.asarray(inp["gdn_conv_w"][0]).T),
        "galog": f(inp["gdn_a_log"]), "gdtb": f(inp["gdn_dt_bias"]), "gon": f(inp["gdn_o_norm"]),
        "gwo": f(inp["gdn_w_out"][0]), "fwi": f(inp["ffn_w_in"]), "fwo": f(inp["ffn_w_out"]),
        "c32": c32, "pm": pm,
    }
    maps = []
    for c in range(NCORE):
        sl = slice(16 * c, 16 * (c + 1))
        m = dict(shared)
        m["xp"] = f(inp["x_prompt"][c])
        m["xs"] = f(np.asarray(inp["x_sample"][sl]).reshape(128, D))
        m["pst"] = f(np.asarray(inp["state_pool"][0, sl]).reshape(240, D))
        m["cst"] = f(np.asarray(inp["state_gdn_conv"][0, sl]).transpose(2, 0, 1))
        m["rst"] = f(inp["state_gdn_rec"][0, sl])
        maps.append(m)
    return maps


def kernel(**inp):
    if "nc" not in _NC_CACHE:
        _NC_CACHE["nc"] = build()
    nc = _NC_CACHE["nc"]
    maps = make_in_maps(inp)
    res = run_bass_kernel_spmd(nc, maps, core_ids=list(range(NCORE)))
    R = res.results
    y_p = np.stack([R[c]["yp"] for c in range(NCORE)]).reshape(8, 2048, D)
    y_s = np.concatenate([R[c]["ys"].reshape(16, 8, D) for c in range(NCORE)])
    pool_p = np.stack([R[c]["pp"] for c in range(NCORE)])[None]
    pool_s = np.concatenate([R[c]["pso"] for c in range(NCORE)])[None]
    conv_p = np.stack([R[c]["cp"].T for c in range(NCORE)])[None]
    conv_s = np.concatenate([R[c]["cso"].transpose(1, 2, 0) for c in range(NCORE)])[None]
    rec_p = np.stack([R[c]["rp"] for c in range(NCORE)])[None]
    rec_s = np.concatenate([R[c]["rso"] for c in range(NCORE)])[None]
    return (y_p, y_s, pool_p, pool_s, conv_p, conv_s, rec_p, rec_s)
```

```python
import numpy as np
import concourse.bass as bass
import concourse.mybir as mybir
from concourse.bass_utils import run_bass_kernel_spmd

F32 = mybir.dt.float32
BF16 = mybir.dt.bfloat16
AF = mybir.ActivationFunctionType
ALU = mybir.AluOpType

D = 1024
DFF = 2816
NCORE = 8
NEG = -30000.0


class Buf:
    __slots__ = ("w", "r", "excl")

    def __init__(self):
        self.w = None
        self.r = []
        self.excl = False


class T:
    def __init__(self, ap, bufs=None):
        self.ap = ap
        self.b = bufs if bufs is not None else [Buf()]


class Ring:
    def __init__(self, items):
        self.items = items
        self.i = 0

    def next(self):
        t = self.items[self.i % len(self.items)]
        self.i += 1
        return t


def _bufs(lst):
    out = []
    for t in lst:
        if isinstance(t, Buf):
            out.append(t)
        else:
            out.extend(t.b)
    return out


class _Rec:
    def __getattr__(self, name):
        def f(*a, **k):
            self.__dict__["call"] = (name, a, k)
            return self
        return f


class Sched:
    ENG = ["pe", "act", "dve", "pool", "sp"]
    NDSEM = 12

    def __init__(self, nc):
        self.nc = nc
        self.ops = []
        self.by_eng = {e: [] for e in self.ENG}
        self.ndma = {e: 0 for e in self.ENG}

    def op(self, eng, fn, reads=(), writes=(), dma=False):
        import os
        if len(self.ops) >= int(os.environ.get("MAXOPS", "100000000")):
            return None
        rec_ = _Rec()
        fn(rec_)
        call = rec_.call
        fn = lambda e, call=call: getattr(e, call[0])(*call[1], **call[2])
        reads = _bufs(reads)
        writes = _bufs(writes)
        writes = writes + [b for b in reads if b.excl and b not in writes]
        reads = [b for b in reads if not b.excl]
        oid = len(self.ops)
        deps = {}
        for b in reads:
            if b.w is not None:
                deps[(b.w, "raw")] = 1
        for b in writes:
            if b.w is not None:
                deps[(b.w, "waw")] = 1
            for r in b.r:
                deps[(r, "war")] = 1
        for b in reads:
            b.r.append(oid)
        for b in writes:
            b.w = oid
            b.r = []
        real = {}
        for (p, kind) in deps:
            if p == oid:
                continue
            po = self.ops[p]
            if not po[3] and po[0] == eng and eng == "pe" and not dma:
                continue
            real[p] = 1
        rec = [eng, fn, list(real.keys()), dma, False, None, None]
        if dma:
            rec[5] = self.ndma[eng]
            self.ndma[eng] += 1
        self.ops.append(rec)
        self.by_eng[eng].append(oid)
        for p in real:
            self.ops[p][4] = True
        return oid

    def emit(self):
        nc = self.nc
        engsem = {e: nc.alloc_semaphore(name=f"es_{e}") for e in self.ENG}
        dsem = {e: [nc.alloc_semaphore(name=f"ds_{e}{i}") for i in range(self.NDSEM)]
                for e in self.ENG if self.ndma[e] > 0}
        for e in self.ENG:
            c = 0
            for oid in self.by_eng[e]:
                o = self.ops[oid]
                if not o[3] and o[4]:
                    c += 1
                    o[6] = c
        K = self.NDSEM
        ops = self.ops
        allsems = list(engsem.values()) + [x for v in dsem.values() for x in v]
        for sm in allsems:
            nc.gpsimd.sem_clear(sm)
        nc.all_engine_barrier()

        def run(e, eng):
            waited = {}

            def wait(sem, val):
                key = id(sem)
                if waited.get(key, 0) >= val:
                    return
                waited[key] = val
                eng.wait_ge(sem, val)

            for oid in self.by_eng[e]:
                o = ops[oid]
                for p in o[2]:
                    po = ops[p]
                    if po[3]:
                        wait(dsem[po[0]][po[5] % K], 16 * (po[5] // K + 1))
                    else:
                        wait(engsem[po[0]], po[6])
                if o[3]:
                    j = o[5]
                    if j >= K:
                        wait(dsem[e][j % K], 16 * (j // K))
                inst = o[1](eng)
                if o[3]:
                    inst.then_inc(dsem[e][o[5] % K], 16)
                elif o[4]:
                    inst.then_inc(engsem[e], 1)
            if e == "sp":
                for q in dsem:
                    n = self.ndma[q]
                    for i in range(min(K, n)):
                        wait(dsem[q][i], 16 * ((n - 1 - i) // K + 1))

        with nc.Block() as block:
            @block.tensor
            def _(eng):
                run("pe", eng)

            @block.scalar
            def _(eng):
                run("act", eng)

            @block.vector
            def _(eng):
                run("dve", eng)

            @block.gpsimd
            def _(eng):
                run("pool", eng)

            @block.sync
            def _(eng):
                run("sp", eng)

        nc.all_engine_barrier()
        nc.clear_and_free_semaphores(allsems)
        nc.all_engine_barrier()


def _consts():
    idx = np.arange(128)
    j = idx[:, None]
    i = idx[None, :]
    same = (j // 8) == (i // 8)
    U = (j <= i).astype(np.float32)
    Ub = ((j <= i) & same).astype(np.float32)
    NM = np.where(i >= j, 0.0, NEG).astype(np.float32)
    NMb = np.where((i >= j) & same, 0.0, NEG).astype(np.float32)
    ST = (i > j).astype(np.float32)
    STb = ((i > j) & same).astype(np.float32)
    BO = same.astype(np.float32)
    ONES = np.ones((128, 128), np.float32)
    ID = np.eye(128, dtype=np.float32)
    BM = ((idx[:, None] // 8) == np.arange(16)[None, :]).astype(np.float32)
    c32 = np.concatenate([U, Ub, NM, NMb, ST, STb, BO, ONES, ID, BM], axis=1)
    wins = (2, 4, 8, 16)
    pm = np.zeros((128, 6, 4, 128), np.float32)
    s = idx[:, None]
    t = idx[None, :]
    for g, w in enumerate(wins):
        pm[:, 0, g] = ((s > t - w) & (s <= t)) / w - (s == t)
        cnt = np.minimum(w, t + 1)
        pm[:, 1, g] = ((s > t - w) & (s <= t)) / cnt - (s == t)
        pm[:, 2, g] = ((s - 128) > (t - w)) / w
        sq, sp_ = s // 8, s % 8
        tq, tp = t // 8, t % 8
        pm[:, 3, g] = ((sq == tq) & (sp_ > tp - w) & (sp_ <= tp)) / w - (s == t)
        r = np.arange(120)[:, None]
        rq, rb = r // 15, r % 15
        for half in range(2):
            pm[:120, 4 + half, g] = ((rq + 8 * half == tq) & ((rb - 15) > (tp - w))) / w
    return c32, pm.reshape(128, 6 * 512)


def build(stop_after=None):
    nc = bass.Bass("TRN2", target_bir_lowering=False)

    def din(name, shape):
        return nc.dram_tensor(name, list(shape), F32, kind="ExternalInput").ap()

    def dout(name, shape):
        return nc.dram_tensor(name, list(shape), F32, kind="ExternalOutput").ap()

    xp_d = din("xp", [2048, D])
    xs_d = din("xs", [128, D])
    pst_d = din("pst", [240, D])
    cst_d = din("cst", [4096, 16, 3])
    rst_d = din("rst", [16, 16, 128, 128])
    nmp_d = din("nmp", [2, D])
    nmo_d = din("nmo", [2, D])
    nfp_d = din("nfp", [2, D])
    nfo_d = din("nfo", [2, D])
    pw_d = din("pw", [4, 256, 256])
    psc_d = din("psc", [1, D])
    gwi_d = din("gwi", [48, 128, 8 * 128])
    gwba_d = din("gwba", [D, 32])
    gcw_d = din("gcw", [4096, 4])
    galog_d = din("galog", [1, 16])
    gdtb_d = din("gdtb", [1, 16])
    gon_d = din("gon", [1, 128])
    gwo_d = din("gwo", [2048, D])
    fwi_d = din("fwi", [2, D, 2 * DFF])
    fwo_d = din("fwo", [2, DFF, D])
    c32_d = din("c32", [128, 9 * 128 + 16])
    pm_d = din("pm", [128, 6 * 512])

    yp_d = dout("yp", [2048, D])
    ys_d = dout("ys", [128, D])
    pp_d = dout("pp", [15, D])
    pso_d = dout("pso", [16, 15, D])
    cp_d = dout("cp", [4096, 3])
    cso_d = dout("cso", [4096, 16, 3])
    rp_d = dout("rp", [16, 128, 128])
    rso_d = dout("rso", [16, 16, 128, 128])
    dbg_d = dout("dbg", [128, 9 * D]) if stop_after == "gdn" else None
    dbg2_d = dout("dbg2", [128, 9 * D]) if stop_after == "gdn" else None
    dbg3_d = dout("dbg3", [16, 128, 128]) if stop_after == "gdn" else None
    dbgn = [0]

    def dump(t, ap):
        if dbg3_d is None or dbgn[0] >= 16:
            return
        dma("pool", dbg3_d[dbgn[0]][:, 0:ap.shape[-1]] if len(ap.shape) == 2 else dbg3_d[dbgn[0]], ap, reads=[t], writes=[Buf()])
        dbgn[0] += 1

    S = Sched(nc)
    cnt = [0]

    def sb(shape, dt, name=None):
        cnt[0] += 1
        return nc.alloc_sbuf_tensor(f"s_{name}" if name else f"sb{cnt[0]}", list(shape), dt).ap()

    def tsb(shape, dt, name=None):
        return T(sb(shape, dt, name))

    dram_out = Buf()

    def dma(q, out, in_, reads=(), writes=()):
        S.op(q, lambda e: e.dma_start(out=out, in_=in_), reads=reads, writes=writes, dma=True)

    ARENA = 82 * 1024
    arena = sb([128, ARENA // 4], F32, "arena")
    ablk = [Buf() for _ in range(ARENA // 512)]
    apos = [0]

    def areset():
        apos[0] = 0

    def aalloc(shape, dt):
        esz = 4 if dt == F32 else 2
        n = esz
        for d_ in shape[1:]:
            n *= d_
        off = apos[0]
        nb = (n + 511) // 512 * 512
        assert off + nb <= ARENA, ("arena overflow", off, nb)
        apos[0] = off + nb
        v = arena[:, off // 4:(off + nb) // 4]
        if dt != F32:
            v = v.bitcast(dt)
        v = v[:, :n // esz]
        if len(shape) == 3:
            v = v.rearrange("p (a b) -> p a b", a=shape[1])
        elif len(shape) == 4:
            v = v.rearrange("p (a b c) -> p a b c", a=shape[1], b=shape[2])
        if shape[0] < 128:
            v = v[:shape[0]]
        return T(v, ablk[off // 512:(off + nb) // 512])

    c32 = tsb([128, 9 * 128 + 16], F32, "c32")
    dma("sp", c32.ap, c32_d, writes=[c32])
    cU, cUb, cNM, cNMb, cST, cSTb, cBO, cONES, cID = [c32.ap[:, k * 128:(k + 1) * 128] for k in range(9)]
    cBM = c32.ap[:, 9 * 128:9 * 128 + 16]
    cbf = tsb([128, 2, 128], BF16, "cbf")
    S.op("dve", lambda e: e.tensor_copy(out=cbf.ap[:, 0, :], in_=cID), reads=[c32], writes=[cbf])
    S.op("dve", lambda e: e.tensor_copy(out=cbf.ap[:, 1, :], in_=cONES), reads=[c32], writes=[cbf])
    identb = cbf.ap[:, 0, :]
    onesb = cbf.ap[:, 1, :]

    banks = [nc.alloc_psum_tensor(f"pb{k}", [128, 512], F32).ap() for k in range(6)]
    PF = [T(banks[k]) for k in range(6)]
    PH = [T(banks[3 + k // 2][:, (k % 2) * 256:(k % 2) * 256 + 256], PF[3 + k // 2].b) for k in range(6)]
    ptr_ap = nc.alloc_psum_tensor("ptr", [128, 8, 128], BF16).ap()
    ptrB = [Buf()]
    PTq = [T(ptr_ap[:, k, :], ptrB) for k in range(8)]
    PTh = [T(ptr_ap[:, 4 * k:4 * k + 4, :], ptrB) for k in range(2)]
    pm_ap = nc.alloc_psum_tensor("pmisc", [128, 512], F32).ap()
    pmB = [Buf()]
    PM = [T(pm_ap[:, k * 128:(k + 1) * 128], pmB) for k in range(4)]
    for t_ in PF:
        t_.b[0].excl = True
    ptrB[0].excl = True
    pmB[0].excl = True
    rPF = Ring(PF)
    rPB = Ring(PF[:3])
    rPH = Ring(PH)
    rPTq = Ring(PTq)
    rPTh = Ring(PTh)
    rPM = Ring(PM)

    NTM = 9
    xbig = sb([128, NTM, D], F32, "x")
    x_t = [T(xbig[:, j, :]) for j in range(NTM)]
    hTbig = sb([128, 8, NTM * 128], BF16, "hT")
    hT_t = [T(hTbig[:, :, j * 128:(j + 1) * 128]) for j in range(NTM)]
    ybig = sb([128, NTM * D], F32, "yacc")
    y_t = [T(ybig[:, j * D:(j + 1) * D]) for j in range(NTM)]
    junk = tsb([128, D], BF16, "junk")
    rSmall = Ring([tsb([128, 2], F32) for _ in range(6)])
    rHb = Ring([tsb([128, D], BF16) for _ in range(2)])
    rHbPool = Ring([tsb([128, D], BF16) for _ in range(2)])
    rTmp = Ring([tsb([128, D], F32) for _ in range(2)])

    def load_gain(src_row):
        g = aalloc([128, D], F32)
        dma("sp", g.ap, src_row.partition_broadcast(128), writes=[g])
        return g

    def rstd_of(src_ap, srcT, rows, n):
        st = rSmall.next()
        sc = float(n) ** -0.5
        S.op("act", lambda e: e.activation(out=junk.ap[:rows, :n], in_=src_ap, func=AF.Square, scale=sc,
                                           accum_out=st.ap[:rows, 0:1]), reads=[srcT], writes=[junk, st])
        S.op("act", lambda e: e.activation(out=st.ap[:rows, 1:2], in_=st.ap[:rows, 0:1], func=AF.Ln,
                                           bias=1e-6, scale=1.0), reads=[st], writes=[st])
        S.op("act", lambda e: e.activation(out=st.ap[:rows, 0:1], in_=st.ap[:rows, 1:2], func=AF.Exp,
                                           scale=-0.5), reads=[st], writes=[st])
        return st

    def pre_norm(j, gain, ring, out32=None):
        st = rstd_of(x_t[j].ap, x_t[j], 128, D)
        hb = ring.next()
        if out32 is not None:
            S.op("dve", lambda e: e.scalar_tensor_tensor(out=out32.ap, in0=x_t[j].ap, scalar=st.ap[:, 0:1],
                                                         in1=gain.ap, op0=ALU.mult, op1=ALU.mult),
                 reads=[x_t[j], st, gain], writes=[out32])
            S.op("act", lambda e: e.activation(out=hb.ap, in_=out32.ap, func=AF.Copy), reads=[out32], writes=[hb])
        else:
            S.op("dve", lambda e: e.scalar_tensor_tensor(out=hb.ap, in0=x_t[j].ap, scalar=st.ap[:, 0:1],
                                                         in1=gain.ap, op0=ALU.mult, op1=ALU.mult),
                 reads=[x_t[j], st, gain], writes=[hb])
        return hb

    def to_hT(j, hb):
        for half in range(2):
            pt = rPTh.next()
            for k in range(4):
                kk = half * 4 + k
                S.op("pe", lambda e, kk=kk, k=k, pt=pt: e.transpose(out=pt.ap[:, k, :], in_=hb.ap[:, kk * 128:(kk + 1) * 128],
                                                                     identity=identb), reads=[hb, cbf], writes=[pt])
            S.op("act", lambda e, pt=pt, half=half: e.activation(out=hT_t[j].ap[:, half * 4:half * 4 + 4, :], in_=pt.ap, func=AF.Copy),
                 reads=[pt], writes=[hT_t[j]])

    def post_norm_add(j, src_ap, srcT, gain):
        st = rstd_of(src_ap, srcT, 128, D)
        tmp = rTmp.next()
        S.op("dve", lambda e: e.scalar_tensor_tensor(out=tmp.ap, in0=src_ap, scalar=st.ap[:, 0:1], in1=gain.ap,
                                                     op0=ALU.mult, op1=ALU.mult), reads=[srcT, st, gain], writes=[tmp])
        S.op("dve", lambda e: e.tensor_tensor(out=x_t[j].ap, in0=x_t[j].ap, in1=tmp.ap, op=ALU.add),
             reads=[x_t[j], tmp], writes=[x_t[j]])

    def ffn(layer, tiles, blocks):
        areset()
        rWg = Ring([aalloc([128, 8, 512], BF16) for _ in range(2)])
        rWu = Ring([aalloc([128, 8, 512], BF16) for _ in range(2)])
        rWo = Ring([aalloc([128, 4, D], BF16) for _ in range(2)])
        rAct = Ring([aalloc([128, 4, 512], BF16) for _ in range(2)])
        rSg = Ring([aalloc([128, 512], F32) for _ in range(2)])
        gpre = load_gain(nfp_d[layer:layer + 1, :])
        gpost = load_gain(nfo_d[layer:layer + 1, :])
        groups = [(f0, min(4, 22 - f0)) for f0 in range(0, 22, 4)]

        def load_group(gi):
            f0, nf = groups[gi]
            wg, wu, wo = rWg.next(), rWu.next(), rWo.next()
            dma("pool", wg.ap[:, :, :nf * 128],
                fwi_d[layer, :, f0 * 128:(f0 + nf) * 128].rearrange("(k p) n -> p k n", p=128), writes=[wg])
            dma("pool", wu.ap[:, :, :nf * 128],
                fwi_d[layer, :, DFF + f0 * 128:DFF + (f0 + nf) * 128].rearrange("(k p) n -> p k n", p=128), writes=[wu])
            dma("pool", wo.ap[:, :nf, :],
                fwo_d[layer, f0 * 128:(f0 + nf) * 128, :].rearrange("(c p) n -> p c n", p=128), writes=[wo])
            return wg, wu, wo

        nxt = load_group(0)
        for j in tiles:
            hb = pre_norm(j, gpre, rHb)
            to_hT(j, hb)
        for gi, (f0, nf) in enumerate(groups):
            wg, wu, wo = nxt
            if gi + 1 < len(groups):
                nxt = load_group(gi + 1)
            for (c0, wd) in blocks:
                tl = list(range(c0 // 128, (c0 + wd) // 128))
                hts = [hT_t[j] for j in tl]
                actT = rAct.next()
                for fc in range(nf):
                    pg = rPF.next()
                    for k in range(8):
                        S.op("pe", lambda e, pg=pg, k=k, fc=fc: e.matmul(pg.ap[:, :wd], lhsT=wg.ap[:, k, fc * 128:(fc + 1) * 128],
                                                                         rhs=hTbig[:, k, c0:c0 + wd], start=(k == 0), stop=(k == 7)),
                             reads=[wg] + hts, writes=[pg])
                    pu = rPF.next()
                    for k in range(8):
                        S.op("pe", lambda e, pu=pu, k=k, fc=fc: e.matmul(pu.ap[:, :wd], lhsT=wu.ap[:, k, fc * 128:(fc + 1) * 128],
                                                                         rhs=hTbig[:, k, c0:c0 + wd], start=(k == 0), stop=(k == 7)),
                             reads=[wu] + hts, writes=[pu])
                    sg = rSg.next()
                    S.op("act", lambda e, pg=pg, sg=sg: e.activation(out=sg.ap[:, :wd], in_=pg.ap[:, :wd], func=AF.Silu),
                         reads=[pg], writes=[sg])
                    S.op("dve", lambda e, pu=pu, sg=sg, fc=fc: e.tensor_tensor(out=actT.ap[:, fc, :wd], in0=pu.ap[:, :wd], in1=sg.ap[:, :wd],
                                                                               op=ALU.mult), reads=[pu, sg], writes=[actT])
                for ti, j in enumerate(tl):
                    for dh in range(2):
                        py = rPF.next()
                        for fc in range(nf):
                            S.op("pe", lambda e, py=py, fc=fc, ti=ti, dh=dh: e.matmul(py.ap, lhsT=actT.ap[:, fc, ti * 128:(ti + 1) * 128],
                                                                                      rhs=wo.ap[:, fc, dh * 512:(dh + 1) * 512],
                                                                                      start=(fc == 0), stop=(fc == nf - 1)),
                                 reads=[actT, wo], writes=[py])
                        ysl = y_t[j].ap[:, dh * 512:(dh + 1) * 512]
                        if gi == 0:
                            S.op("act", lambda e, py=py, ysl=ysl: e.activation(out=ysl, in_=py.ap, func=AF.Copy), reads=[py], writes=[y_t[j]])
                        else:
                            S.op("dve", lambda e, py=py, ysl=ysl: e.tensor_tensor(out=ysl, in0=ysl, in1=py.ap, op=ALU.add),
                                 reads=[py, y_t[j]], writes=[y_t[j]])
        for j in tiles:
            post_norm_add(j, y_t[j].ap, y_t[j], gpost)

    pool_prev = [None]

    def pool_phase(sbi, tiles, kinds):
        areset()
        pmb = aalloc([128, 6, 4, 128], BF16)
        dma("pool", pmb.ap, pm_d.rearrange("p (a g t) -> p a g t", a=6, g=4), writes=[pmb])
        pwb = aalloc([128, 4, 2, 256], BF16)
        pstb = aalloc([128, 2, D], BF16)
        dT = Ring([aalloc([128, 8, 128], BF16) for _ in range(2)])
        gpre = load_gain(nmp_d[0:1, :])
        gpost = load_gain(nmo_d[0:1, :])
        gsc = load_gain(psc_d[0:1, :])
        dma("pool", pwb.ap, pw_d.rearrange("g (c p) e -> p g c e", p=128), writes=[pwb])
        if sbi == 0:
            S.op("dve", lambda e: e.memset(pstb.ap, 0.0), writes=[pstb])
            dma("pool", pstb.ap[:120, :, :], pst_d.rearrange("(a r) d -> r a d", a=2), writes=[pstb])
            if stop_after != "nod2d":
                dma("sp", pso_d[:, 0:7, :], pst_d.rearrange("(s b) d -> s b d", b=15)[:, 8:15, :], writes=[Buf()])
        for j, (kind, pi) in zip(tiles, kinds):
            h32 = y_t[j]
            hb = pre_norm(j, gpre, rHbPool, out32=h32)
            if kind == "s":
                for sq in range(16):
                    dma("sp", pso_d[sq, 7:15, :], h32.ap[sq * 8:(sq + 1) * 8, :], reads=[h32], writes=[Buf()])
            elif pi == 15:
                dma("sp", pp_d, h32.ap[113:128, :], reads=[h32], writes=[Buf()])
            d = dT.next()
            phs = [rPH.next() for _ in range(4)]
            for cc in range(8):
                g = cc // 2
                ph = phs[cc // 2]
                o_ap = ph.ap[:, (cc % 2) * 128:(cc % 2) * 128 + 128]
                lhs_cur = hb.ap[:, cc * 128:(cc + 1) * 128]
                if kind == "s":
                    S.op("pe", lambda e, o_ap=o_ap, lhs_cur=lhs_cur, g=g: e.matmul(o_ap, lhsT=lhs_cur, rhs=pmb.ap[:, 3, g, :], start=True, stop=False),
                         reads=[hb, pmb], writes=[ph])
                    for half in range(2):
                        S.op("pe", lambda e, o_ap=o_ap, cc=cc, g=g, half=half: e.matmul(o_ap, lhsT=pstb.ap[:, half, cc * 128:(cc + 1) * 128],
                                                                                        rhs=pmb.ap[:, 4 + half, g, :], start=False, stop=(half == 1)),
                             reads=[pstb, pmb], writes=[ph])
                elif pi == 0:
                    S.op("pe", lambda e, o_ap=o_ap, lhs_cur=lhs_cur, g=g: e.matmul(o_ap, lhsT=lhs_cur, rhs=pmb.ap[:, 1, g, :], start=True, stop=True),
                         reads=[hb, pmb], writes=[ph])
                else:
                    prev = pool_prev[0]
                    S.op("pe", lambda e, o_ap=o_ap, lhs_cur=lhs_cur, g=g: e.matmul(o_ap, lhsT=lhs_cur, rhs=pmb.ap[:, 0, g, :], start=True, stop=False),
                         reads=[hb, pmb], writes=[ph])
                    S.op("pe", lambda e, o_ap=o_ap, cc=cc, g=g, prev=prev: e.matmul(o_ap, lhsT=prev.ap[:, cc * 128:(cc + 1) * 128], rhs=pmb.ap[:, 2, g, :],
                                                                                    start=False, stop=True), reads=[prev, pmb], writes=[ph])
            for q in range(4):
                S.op("act", lambda e, q=q: e.activation(out=d.ap[:, 2 * q:2 * q + 2, :], in_=phs[q].ap.rearrange("p (a t) -> p a t", a=2), func=AF.Copy),
                     reads=[phs[q]], writes=[d])
            if kind == "p":
                pool_prev[0] = hb
            pys = [rPB.next(), rPB.next()]
            for g in range(4):
                py = pys[g // 2]
                o_ap = py.ap[:, (g % 2) * 256:(g % 2) * 256 + 256]
                for c in range(2):
                    S.op("pe", lambda e, o_ap=o_ap, g=g, c=c: e.matmul(o_ap, lhsT=d.ap[:, 2 * g + c, :], rhs=pwb.ap[:, g, c, :],
                                                                       start=(c == 0), stop=(c == 1)), reads=[d, pwb], writes=[py])
            m32 = rTmp.next()
            for hh in range(2):
                S.op("dve", lambda e, hh=hh: e.tensor_tensor(out=m32.ap[:, hh * 512:(hh + 1) * 512], in0=pys[hh].ap,
                                                             in1=gsc.ap[:, hh * 512:(hh + 1) * 512], op=ALU.mult),
                     reads=[pys[hh], gsc], writes=[m32])
            post_norm_add(j, m32.ap, m32, gpost)


    S32 = [tsb([128, 128], F32, f"S32_{h}") for h in range(16)]
    Sbf = [tsb([128, 128], BF16, f"Sbf_{h}") for h in range(16)]
    ctail = tsb([128, 32, 3], F32, "ctail")

    def gdn_phase(sbi, tiles, kinds, blocks):
        NTl = len(tiles)
        Ncol = NTl * 128
        has_s = kinds[0][0] == "s"
        p0 = 128 if has_s else 0
        Np = Ncol - p0
        areset()
        if dbg2_d is not None and sbi == 0:
            dma("sp", dbg2_d.rearrange("p (a b) -> p a b", a=9), xbig, reads=x_t, writes=[Buf()])
        gpre = load_gain(nmp_d[1:2, :])
        for j in tiles:
            hb = pre_norm(j, gpre, rHb)
            to_hT(j, hb)
        areset()
        gs = aalloc([128, NTl, 6, 16], F32)
        glb = aalloc([128, NTl, 16], F32)
        gls = aalloc([128, 16, 16], F32)
        gblk = aalloc([128, 16, 16], F32)
        wba = aalloc([128, 8, 128], BF16)
        dtb = aalloc([128, 16], F32)
        nea = aalloc([128, 16], F32)
        cw = aalloc([128, 32, 4], F32)
        onb = aalloc([128, 128], F32)
        t16 = Ring([aalloc([128, 16], F32) for _ in range(3)])
        S.op("dve", lambda e: e.memset(wba.ap, 0.0), writes=[wba])
        S.op("dve", lambda e: e.memset(gs.ap, 0.0), writes=[gs])
        dma("pool", wba.ap[:, :, 0:32], gwba_d.rearrange("(k p) n -> p k n", p=128), writes=[wba])
        dma("sp", dtb.ap, gdtb_d.partition_broadcast(128), writes=[dtb])
        dma("sp", nea.ap, galog_d.partition_broadcast(128), writes=[nea])
        dma("sp", cw.ap, gcw_d.rearrange("(c p) t -> p c t", p=128), writes=[cw])
        dma("sp", onb.ap, gon_d.partition_broadcast(128), writes=[onb])
        S.op("act", lambda e: e.activation(out=nea.ap, in_=nea.ap, func=AF.Exp), reads=[nea], writes=[nea])
        S.op("dve", lambda e: e.tensor_scalar(out=nea.ap, in0=nea.ap, scalar1=-1.0, scalar2=None, op0=ALU.mult), reads=[nea], writes=[nea])
        for j, (kind, pi) in zip(tiles, kinds):
            cols = slice(j * 128, (j + 1) * 128)
            pm = rPM.next()
            for k in range(8):
                S.op("pe", lambda e, k=k: e.matmul(pm.ap, lhsT=hTbig[:, k, cols], rhs=wba.ap[:, k, :], start=(k == 0), stop=(k == 7)),
                     reads=[hT_t[j], wba], writes=[pm])
            G = lambda q: gs.ap[:, j, q, :]
            ta, tb = t16.next(), t16.next()
            S.op("act", lambda e: e.activation(out=ta.ap, in_=pm.ap[:, 0:16], func=AF.Exp, scale=-1.0), reads=[pm], writes=[ta])
            S.op("dve", lambda e: e.tensor_tensor(out=tb.ap, in0=pm.ap[:, 16:32], in1=dtb.ap, op=ALU.add), reads=[pm, dtb], writes=[tb])
            S.op("dve", lambda e: e.tensor_scalar(out=ta.ap, in0=ta.ap, scalar1=1.0, scalar2=None, op0=ALU.add), reads=[ta], writes=[ta])
            S.op("dve", lambda e: e.reciprocal(out=G(1), in_=ta.ap), reads=[ta], writes=[gs])
            S.op("dve", lambda e: e.tensor_scalar(out=G(2), in0=G(1), scalar1=-1.0, scalar2=None, op0=ALU.mult), reads=[gs], writes=[gs])
            S.op("act", lambda e: e.activation(out=tb.ap, in_=tb.ap, func=AF.Exp), reads=[tb], writes=[tb])
            S.op("act", lambda e: e.activation(out=tb.ap, in_=tb.ap, func=AF.Ln, bias=1.0, scale=1.0), reads=[tb], writes=[tb])
            S.op("dve", lambda e: e.tensor_tensor(out=G(0), in0=tb.ap, in1=nea.ap, op=ALU.mult), reads=[tb, nea], writes=[gs])
            pm2 = rPM.next()
            pm3 = rPM.next()
            grow = gs.ap[:, j, :, :].rearrange("p a b -> p (a b)")
            S.op("pe", lambda e: e.matmul(pm2.ap[:, 0:96], lhsT=(cUb if kind == "s" else cU), rhs=grow, start=True, stop=True), reads=[c32, gs], writes=[pm2])
            S.op("pe", lambda e: e.matmul(pm3.ap[:, 0:96], lhsT=(cBO if kind == "s" else cONES), rhs=grow, start=True, stop=True), reads=[c32, gs], writes=[pm3])
            S.op("dve", lambda e: e.tensor_copy(out=G(3), in_=pm2.ap[:, 0:16]), reads=[pm2], writes=[gs])
            S.op("act", lambda e: e.activation(out=G(4), in_=pm2.ap[:, 0:16], func=AF.Exp), reads=[pm2], writes=[gs])
            S.op("act", lambda e: e.activation(out=glb.ap[:, j, :], in_=pm3.ap[:, 0:16], func=AF.Exp), reads=[pm3], writes=[glb])
            tc_ = t16.next()
            S.op("dve", lambda e: e.tensor_tensor(out=tc_.ap, in0=pm3.ap[:, 0:16], in1=G(3), op=ALU.subtract), reads=[pm3, gs], writes=[tc_])
            S.op("act", lambda e: e.activation(out=G(5), in_=tc_.ap, func=AF.Exp), reads=[tc_], writes=[gs])
            if kind == "s":
                S.op("dve", lambda e: e.tensor_tensor(out=gblk.ap, in0=G(0).unsqueeze(1).to_broadcast([128, 16, 16]),
                                                      in1=cBM.unsqueeze(2).to_broadcast([128, 16, 16]), op=ALU.mult), reads=[gs, c32], writes=[gblk])
                pb = rPB.next()
                S.op("pe", lambda e: e.matmul(pb.ap[:, 0:256], lhsT=cONES, rhs=gblk.ap.rearrange("p a b -> p (a b)"), start=True, stop=True),
                     reads=[c32, gblk], writes=[pb])
                S.op("act", lambda e: e.activation(out=gls.ap.rearrange("p a b -> p (a b)"), in_=pb.ap[:, 0:256], func=AF.Exp), reads=[pb], writes=[gls])

        onT = T(ybig.bitcast(BF16)[:, :16 * Ncol].rearrange("p (a b) -> p a b", a=16), [b for t in y_t for b in t.b])
        rPBg = Ring([PF[4], PF[5]])
        qTs = [aalloc([128, Ncol], BF16) for _ in range(2)]
        kTs = [aalloc([128, Ncol], BF16) for _ in range(2)]
        ktoks = [aalloc([128, NTl, 128], BF16) for _ in range(2)]
        vtoks = [aalloc([128, NTl, 128], BF16) for _ in range(4)]
        zss = [aalloc([128, NTl, 128], BF16) for _ in range(4)]
        umark = apos[0]
        wring = Ring([aalloc([128, 8, 128], BF16) for _ in range(4)])
        prering = Ring([aalloc([128, 3 + 1024], F32) for _ in range(2)])
        presring = Ring([aalloc([128, 16, 11], F32) for _ in range(2)])
        coring = Ring([aalloc([128, Ncol], F32) for _ in range(2)])
        sqring = Ring([aalloc([128, 512], BF16) for _ in range(1)])
        rnring = Ring([aalloc([128, 512], F32) for _ in range(1)])
        vTring = Ring([aalloc([128, Ncol], BF16) for _ in range(1)])
        uend = apos[0]
        apos[0] = umark
        chains = []
        cstart = []
        for c in range(4):
            cstart.append(apos[0])
            bb_ = aalloc([128, 8, 128], BF16)
            chains.append(dict(b=[T(bb_.ap[:, i_, :], bb_.b[i_ // 2:i_ // 2 + 1]) for i_ in range(8)],
                               f=[aalloc([128, 128], F32) for _ in range(4)],
                               x=[aalloc([128, 2, 128], F32) for _ in range(2)],
                               r=[aalloc([128, 128], F32) for _ in range(2)]))
        uend = max(uend, apos[0])
        if has_s:
            apos[0] = cstart[1]
            padw = aalloc([128, 16 * 136], BF16)
            padq = aalloc([128, 16 * 136], BF16)
            kpad = aalloc([128, 16, 128], BF16)
            s0f = aalloc([128, 8, 128], F32)
            s0b = aalloc([128, 8, 128], BF16)
            snw = aalloc([128, 8, 128], F32)
            uend = max(uend, apos[0])
        apos[0] = uend
        f128 = Ring(chains[0]["f"])
        b128 = Ring(chains[0]["b"])
        xx = Ring(chains[0]["x"])
        rR = Ring(chains[0]["r"])

        def proj_conv(w, cidx, silu=True):
            pre, pres, co = prering.next(), presring.next(), coring.next()
            for (c0, wd) in blocks:
                ps = rPBg.next()
                tl = [hT_t[t] for t in range(c0 // 128, (c0 + wd) // 128)]
                for k in range(8):
                    S.op("pe", lambda e, k=k: e.matmul(ps.ap[:, :wd], lhsT=w.ap[:, k, :], rhs=hTbig[:, k, c0:c0 + wd], start=(k == 0), stop=(k == 7)),
                         reads=[w] + tl, writes=[ps])
                if has_s and c0 == 0:
                    S.op("act", lambda e: e.activation(out=pres.ap[:, :, 3:11], in_=ps.ap[:, 0:128].rearrange("p (a b) -> p a b", a=16), func=AF.Copy),
                         reads=[ps], writes=[pres])
                else:
                    S.op("act", lambda e: e.activation(out=pre.ap[:, 3 + c0 - p0:3 + c0 - p0 + wd], in_=ps.ap[:, :wd], func=AF.Copy), reads=[ps], writes=[pre])
            if sbi == 0:
                S.op("dve", lambda e: e.memset(pre.ap[:, 0:3], 0.0), writes=[pre])
            else:
                S.op("dve", lambda e: e.tensor_copy(out=pre.ap[:, 0:3], in_=ctail.ap[:, cidx, :]), reads=[ctail], writes=[pre])
            if has_s:
                dma("sp", pres.ap[:, :, 0:3], cst_d[cidx * 128:(cidx + 1) * 128, :, :], writes=[pres])
            if sbi == 0:
                S.op("dve", lambda e: e.tensor_copy(out=ctail.ap[:, cidx, :], in_=pre.ap[:, Np:Np + 3]), reads=[pre], writes=[ctail])
            else:
                dma("sp", cp_d[cidx * 128:(cidx + 1) * 128, :], pre.ap[:, Np:Np + 3], reads=[pre], writes=[Buf()])
            if has_s:
                dma("sp", cso_d[cidx * 128:(cidx + 1) * 128, :, :], pres.ap[:, :, 8:11], reads=[pres], writes=[Buf()])
            views = [(co.ap[:, p0:Ncol], lambda tap: pre.ap[:, tap:tap + Np], pre)]
            if has_s:
                views.append((co.ap[:, 0:128].rearrange("p (a b) -> p a b", a=16), lambda tap: pres.ap[:, :, tap:tap + 8], pres))
            for (o_ap, src, srcT) in views:
                S.op("dve", lambda e: e.tensor_scalar(out=o_ap, in0=src(0), scalar1=cw.ap[:, cidx, 0:1], scalar2=None, op0=ALU.mult),
                     reads=[srcT, cw], writes=[co])
                for tap in range(1, 4):
                    S.op("dve", lambda e, tap=tap: e.scalar_tensor_tensor(out=o_ap, in0=src(tap), scalar=cw.ap[:, cidx, tap:tap + 1], in1=o_ap,
                                                                           op0=ALU.mult, op1=ALU.add), reads=[srcT, cw, co], writes=[co])
            S.op("act", lambda e: e.activation(out=co.ap, in_=co.ap, func=AF.Silu), reads=[co], writes=[co])
            return co

        def l2n(co, dst, scale):
            for (c0, wd) in blocks:
                sq, rn = sqring.next(), rnring.next()
                S.op("dve", lambda e: e.tensor_tensor(out=sq.ap[:, :wd], in0=co.ap[:, c0:c0 + wd], in1=co.ap[:, c0:c0 + wd], op=ALU.mult), reads=[co], writes=[sq])
                ps = rPBg.next()
                S.op("pe", lambda e: e.matmul(ps.ap[:, :wd], lhsT=onesb, rhs=sq.ap[:, :wd], start=True, stop=True), reads=[cbf, sq], writes=[ps])
                S.op("act", lambda e: e.activation(out=rn.ap[:, :wd], in_=ps.ap[:, :wd], func=AF.Ln, bias=1e-6, scale=1.0), reads=[ps], writes=[rn])
                S.op("act", lambda e: e.activation(out=rn.ap[:, :wd], in_=rn.ap[:, :wd], func=AF.Exp, scale=-0.5), reads=[rn], writes=[rn])
                S.op("dve", lambda e: e.scalar_tensor_tensor(out=dst.ap[:, c0:c0 + wd], in0=co.ap[:, c0:c0 + wd], scalar=scale, in1=rn.ap[:, :wd],
                                                             op0=ALU.mult, op1=ALU.mult), reads=[co, rn], writes=[dst])

        def to_tok(srcT, dst):
            for j in tiles:
                pt = rPTq.next()
                S.op("pe", lambda e: e.transpose(out=pt.ap, in_=srcT.ap[:, j * 128:(j + 1) * 128], identity=identb), reads=[srcT, cbf], writes=[pt])
                S.op("act", lambda e: e.activation(out=dst.ap[:, j, :], in_=pt.ap, func=AF.Copy), reads=[pt], writes=[dst])

        def wload(col0):
            w = wring.next()
            dma("pool", w.ap, gwi_d[col0 // 128].rearrange("p (k n) -> p k n", k=8), writes=[w])
            return w

        def mm(out_t, out_ap, lhsT, rhs, R, start=True, stop=True):
            S.op("pe", lambda e: e.matmul(out_ap, lhsT=lhsT, rhs=rhs, start=start, stop=stop), reads=R, writes=[out_t])


        def zproj(w, dst):
            for j in tiles:
                pm = rPM.next()
                for k in range(8):
                    mm(pm, pm.ap, hTbig[:, k, j * 128:(j + 1) * 128], w.ap[:, k, :], [hT_t[j], w], start=(k == 0), stop=(k == 7))
                S.op("act", lambda e: e.activation(out=dst.ap[:, j, :], in_=pm.ap, func=AF.Silu), reads=[pm], writes=[dst])

        def finish_o(po_ap, po, on, ob, zs, j, h):
            st = rstd_of(po_ap, po, 128, 128)
            S.op("dve", lambda e: e.scalar_tensor_tensor(out=on.ap, in0=po_ap, scalar=st.ap[:, 0:1], in1=onb.ap, op0=ALU.mult, op1=ALU.mult),
                 reads=[po, st, onb], writes=[on])
            S.op("dve", lambda e: e.tensor_tensor(out=ob.ap, in0=on.ap, in1=zs.ap[:, j, :], op=ALU.mult), reads=[on, zs], writes=[ob])

        def sample_state(h, qT, kT, ktok, vtok, zs):
            j = 0
            smp = True
            cols = slice(0, 128)
            G = lambda q: gs.ap[:, j, q, h:h + 1]
            pkq = rPH.next()
            mm(pkq, pkq.ap[:, 0:128], kT.ap[:, cols], qT.ap[:, cols], [kT, qT])
            mm(pkq, pkq.ap[:, 128:256], kT.ap[:, cols], kT.ap[:, cols], [kT])
            prow = rPM.next()
            mm(prow, prow.ap, gs.ap[:, j, 0, h:h + 1].to_broadcast([128, 128]), cUb, [gs, c32])
            E = f128.next()
            S.op("dve", lambda e: e.scalar_tensor_tensor(out=E.ap, in0=prow.ap, scalar=G(3), in1=cNMb, op0=ALU.subtract, op1=ALU.add),
                 reads=[prow, gs, c32], writes=[E])
            S.op("act", lambda e: e.activation(out=E.ap, in_=E.ap, func=AF.Exp), reads=[E], writes=[E])
            egb = f128.next()
            S.op("act", lambda e: e.activation(out=egb.ap, in_=prow.ap, func=AF.Exp), reads=[prow], writes=[egb])
            qkT = b128.next()
            S.op("dve", lambda e: e.tensor_tensor(out=qkT.ap, in0=pkq.ap[:, 0:128], in1=E.ap, op=ALU.mult), reads=[pkq, E], writes=[qkT])
            dS = f128.next()
            S.op("dve", lambda e: e.tensor_tensor(out=dS.ap, in0=E.ap, in1=cSTb, op=ALU.mult), reads=[E, c32], writes=[dS])
            X = xx.next()
            S.op("dve", lambda e: e.scalar_tensor_tensor(out=X.ap[:, 0, :], in0=pkq.ap[:, 128:256], scalar=G(2), in1=dS.ap, op0=ALU.mult, op1=ALU.mult),
                 reads=[pkq, gs, dS], writes=[X])
            pt = rPM.next()
            mm(pt, pt.ap, X.ap[:, 0, :], cID, [X, c32])
            S.op("act", lambda e: e.activation(out=X.ap[:, 1, :], in_=pt.ap, func=AF.Copy), reads=[pt], writes=[X])
            Rm = rR.next()
            S.op("dve", lambda e: e.tensor_tensor(out=Rm.ap, in0=X.ap[:, 0, :], in1=cID, op=ALU.add), reads=[X, c32], writes=[Rm])
            nlev = 2
            for lev in range(nlev):
                px = rPH.next()
                last = lev == nlev - 1
                mm(px, px.ap[:, 128:256], X.ap[:, 0, :], X.ap[:, 1, :], [X])
                if not last:
                    mm(px, px.ap[:, 0:128], X.ap[:, 1, :], X.ap[:, 0, :], [X])
                X2 = xx.next()
                if last:
                    S.op("act", lambda e: e.activation(out=X2.ap[:, 1, :], in_=px.ap[:, 128:256], func=AF.Copy), reads=[px], writes=[X2])
                else:
                    S.op("act", lambda e: e.activation(out=X2.ap, in_=px.ap.rearrange("p (a b) -> p a b", a=2), func=AF.Copy), reads=[px], writes=[X2])
                pr = rPM.next()
                mm(pr, pr.ap, X2.ap[:, 1, :], Rm.ap, [X2, Rm])
                R2 = rR.next()
                S.op("dve", lambda e: e.tensor_tensor(out=R2.ap, in0=pr.ap, in1=Rm.ap, op=ALU.add), reads=[pr, Rm], writes=[R2])
                Rm, X = R2, X2
            Rb = b128.next()
            S.op("act", lambda e: e.activation(out=Rb.ap, in_=Rm.ap, func=AF.Copy), reads=[Rm], writes=[Rb])
            Rm = Rb
            kg = b128.next()
            S.op("act", lambda e: e.mul(out=kg.ap, in_=ktok.ap[:, j, :], mul=G(4)), reads=[ktok, gs], writes=[kg])
            kd = b128.next()
            S.op("act", lambda e: e.mul(out=kd.ap, in_=ktok.ap[:, j, :], mul=G(5)), reads=[ktok, gs], writes=[kd])
            pw_ = rPM.next()
            mm(pw_, pw_.ap, kg.ap, Rm.ap, [kg, Rm])
            qd = b128.next()
            S.op("dve", lambda e: e.tensor_tensor(out=qd.ap, in0=qT.ap[:, cols], in1=egb.ap, op=ALU.mult), reads=[qT, egb], writes=[qd])
            vn = b128.next()
            if True:
                        pw3 = padw.ap[:, :].rearrange("p (a b) -> p a b", b=136)[:, :, 0:8]
                        pq3 = padq.ap[:, :].rearrange("p (a b) -> p a b", b=136)[:, :, 0:8]
                        S.op("dve", lambda e: e.tensor_scalar(out=pw3, in0=pw_.ap.rearrange("p (a b) -> p a b", a=16), scalar1=-1.0, scalar2=None, op0=ALU.mult),
                             reads=[pw_], writes=[padw])
                        S.op("dve", lambda e: e.tensor_copy(out=pq3, in_=qd.ap.rearrange("p (a b) -> p a b", a=16)), reads=[qd], writes=[padq])
                        S.op("dve", lambda e: e.tensor_tensor(out=kpad.ap, in0=kd.ap.unsqueeze(1).to_broadcast([128, 16, 128]),
                                                              in1=cBM.unsqueeze(2).to_broadcast([128, 16, 128]), op=ALU.mult), reads=[kd, c32], writes=[kpad])
                        pv = rPM.next()
                        po = rPH.next()
                        po_ap = po.ap[:, 0:128]
                        for half in range(2):
                            dma("sp", s0f.ap, rst_d[half * 8:(half + 1) * 8, h].rearrange("s k v -> k s v"), writes=[s0f])
                            dma("pool", s0b.ap, rst_d[half * 8:(half + 1) * 8, h].rearrange("s k v -> k s v"), writes=[s0b])
                            if half == 0:
                                mm(pv, pv.ap, Rm.ap, vtok.ap[:, j, :], [Rm, vtok], start=True, stop=False)
                            for sl in range(8):
                                sq_ = half * 8 + sl
                                mm(pv, pv.ap, padw.ap[:, sq_ * 128:(sq_ + 1) * 128], s0b.ap[:, sl, :], [padw, s0b], start=False, stop=(sq_ == 15))
                            for sl in range(8):
                                sq_ = half * 8 + sl
                                mm(po, po_ap, padq.ap[:, sq_ * 128:(sq_ + 1) * 128], s0b.ap[:, sl, :], [padq, s0b], start=(sq_ == 0), stop=False)
                            if half == 1:
                                S.op("act", lambda e: e.mul(out=vn.ap, in_=pv.ap, mul=G(1)), reads=[pv, gs], writes=[vn])
                                mm(po, po_ap, qkT.ap, vn.ap, [qkT, vn], start=False, stop=True)
                        for half in range(2):
                            dma("sp", s0f.ap, rst_d[half * 8:(half + 1) * 8, h].rearrange("s k v -> k s v"), writes=[s0f])
                            for qd4 in range(2):
                                pb = rPB.next()
                                for s4 in range(4):
                                    sq_ = half * 8 + qd4 * 4 + s4
                                    mm(pb, pb.ap[:, s4 * 128:(s4 + 1) * 128], kpad.ap[:, sq_, :], vn.ap, [kpad, vn])
                                sl0 = qd4 * 4
                                S.op("dve", lambda e: e.tensor_tensor(out=snw.ap[:, sl0:sl0 + 4, :], in0=s0f.ap[:, sl0:sl0 + 4, :],
                                                                      in1=gls.ap[:, half * 8 + sl0:half * 8 + sl0 + 4, h:h + 1].to_broadcast([128, 4, 128]), op=ALU.mult),
                                     reads=[s0f, gls], writes=[snw])
                                S.op("dve", lambda e: e.tensor_tensor(out=snw.ap[:, sl0:sl0 + 4, :], in0=snw.ap[:, sl0:sl0 + 4, :],
                                                                      in1=pb.ap.rearrange("p (a b) -> p a b", a=4), op=ALU.add), reads=[snw, pb], writes=[snw])
                            dma("sp", rso_d[half * 8:(half + 1) * 8, h].rearrange("s k v -> k s v"), snw.ap, reads=[snw], writes=[Buf()])

            on = f128.next()
            ob = b128.next()
            finish_o(po_ap, po, on, ob, zs, j, h)
            pt2 = rPTq.next()
            S.op("pe", lambda e: e.transpose(out=pt2.ap, in_=ob.ap, identity=identb), reads=[ob, cbf], writes=[pt2])
            S.op("act", lambda e: e.activation(out=onT.ap[:, h, cols], in_=pt2.ap, func=AF.Copy), reads=[pt2], writes=[onT])

        def chain(c, h, qT, kT, ktok, vtok, zs):
            cb = chains[c]
            QB = [T(banks[c][:, q * 128:(q + 1) * 128], PF[c].b) for q in range(4)]
            F0 = T(banks[c][:, 256:384], PF[c].b)
            F1 = T(banks[c][:, 384:512], PF[c].b)
            F01 = T(banks[c][:, 256:512].rearrange("p (a b) -> p a b", a=2), PF[c].b)
            qkT, Rb, kg, kd, qd, vn, nw, ob = cb["b"]
            E, egb, dS, on = cb["f"]
            for j, (kind, pi) in zip(tiles, kinds):
                if kind == "s":
                    continue
                cols = slice(j * 128, (j + 1) * 128)
                G = lambda q: gs.ap[:, j, q, h:h + 1]
                mm(QB[0], QB[0].ap, kT.ap[:, cols], qT.ap[:, cols], [kT, qT])
                mm(QB[1], QB[1].ap, kT.ap[:, cols], kT.ap[:, cols], [kT])
                mm(F0, F0.ap, gs.ap[:, j, 0, h:h + 1].to_broadcast([128, 128]), cU, [gs, c32])
                yield
                S.op("dve", lambda e: e.scalar_tensor_tensor(out=E.ap, in0=F0.ap, scalar=G(3), in1=cNM, op0=ALU.subtract, op1=ALU.add),
                     reads=[F0, gs, c32], writes=[E])
                S.op("act", lambda e: e.activation(out=egb.ap, in_=F0.ap, func=AF.Exp), reads=[F0], writes=[egb])
                yield
                S.op("act", lambda e: e.activation(out=E.ap, in_=E.ap, func=AF.Exp), reads=[E], writes=[E])
                yield
                S.op("dve", lambda e: e.tensor_tensor(out=qkT.ap, in0=QB[0].ap, in1=E.ap, op=ALU.mult), reads=[QB[0], E], writes=[qkT])
                S.op("dve", lambda e: e.tensor_tensor(out=dS.ap, in0=E.ap, in1=cST, op=ALU.mult), reads=[E, c32], writes=[dS])
                X = cb["x"][0]
                S.op("dve", lambda e: e.scalar_tensor_tensor(out=X.ap[:, 0, :], in0=QB[1].ap, scalar=G(2), in1=dS.ap, op0=ALU.mult, op1=ALU.mult),
                     reads=[QB[1], gs, dS], writes=[X])
                S.op("dve", lambda e: e.tensor_tensor(out=qd.ap, in0=qT.ap[:, cols], in1=egb.ap, op=ALU.mult), reads=[qT, egb], writes=[qd])
                yield
                mm(F1, F1.ap, X.ap[:, 0, :], cID, [X, c32])
                Rm = cb["r"][0]
                S.op("dve", lambda e: e.tensor_tensor(out=Rm.ap, in0=X.ap[:, 0, :], in1=cID, op=ALU.add), reads=[X, c32], writes=[Rm])
                S.op("act", lambda e: e.mul(out=kg.ap, in_=ktok.ap[:, j, :], mul=G(4)), reads=[ktok, gs], writes=[kg])
                S.op("act", lambda e: e.mul(out=kd.ap, in_=ktok.ap[:, j, :], mul=G(5)), reads=[ktok, gs], writes=[kd])
                yield
                S.op("act", lambda e: e.activation(out=X.ap[:, 1, :], in_=F1.ap, func=AF.Copy), reads=[F1], writes=[X])
                yield
                for lev in range(6):
                    last = lev == 5
                    Xn = cb["x"][(lev + 1) % 2]
                    Rn = cb["r"][(lev + 1) % 2]
                    mm(F1, F1.ap, X.ap[:, 0, :], X.ap[:, 1, :], [X])
                    if not last:
                        mm(F0, F0.ap, X.ap[:, 1, :], X.ap[:, 0, :], [X])
                    yield
                    if last:
                        S.op("act", lambda e: e.activation(out=Xn.ap[:, 1, :], in_=F1.ap, func=AF.Copy), reads=[F1], writes=[Xn])
                    else:
                        S.op("act", lambda e: e.activation(out=Xn.ap, in_=F01.ap, func=AF.Copy), reads=[F01], writes=[Xn])
                    yield
                    pr = F0
                    mm(pr, pr.ap, Xn.ap[:, 1, :], Rm.ap, [Xn, Rm])
                    yield
                    S.op("dve", lambda e: e.tensor_tensor(out=Rn.ap, in0=pr.ap, in1=Rm.ap, op=ALU.add), reads=[pr, Rm], writes=[Rn])
                    yield
                    X, Rm = Xn, Rn
                S.op("act", lambda e: e.activation(out=Rb.ap, in_=Rm.ap, func=AF.Copy), reads=[Rm], writes=[Rb])
                yield
                mm(QB[2], QB[2].ap, kg.ap, Rb.ap, [kg, Rb])
                yield
                S.op("dve", lambda e: e.tensor_scalar(out=nw.ap, in0=QB[2].ap, scalar1=-1.0, scalar2=None, op0=ALU.mult), reads=[QB[2]], writes=[nw])
                yield
                mm(QB[3], QB[3].ap, Rb.ap, vtok.ap[:, j, :], [Rb, vtok], start=True, stop=False)
                mm(QB[3], QB[3].ap, nw.ap, Sbf[h].ap, [nw, Sbf[h]], start=False, stop=True)
                yield
                S.op("act", lambda e: e.mul(out=vn.ap, in_=QB[3].ap, mul=G(1)), reads=[QB[3], gs], writes=[vn])
                yield
                mm(QB[0], QB[0].ap, qd.ap, Sbf[h].ap, [qd, Sbf[h]], start=True, stop=False)
                mm(QB[0], QB[0].ap, qkT.ap, vn.ap, [qkT, vn], start=False, stop=True)
                mm(QB[1], QB[1].ap, kd.ap, vn.ap, [kd, vn])
                yield
                S.op("dve", lambda e: e.scalar_tensor_tensor(out=S32[h].ap, in0=S32[h].ap, scalar=glb.ap[:, j, h:h + 1], in1=QB[1].ap,
                                                             op0=ALU.mult, op1=ALU.add), reads=[S32[h], glb, QB[1]], writes=[S32[h]])
                st = rstd_of(QB[0].ap, QB[0], 128, 128)
                yield
                S.op("act", lambda e: e.activation(out=Sbf[h].ap, in_=S32[h].ap, func=AF.Copy), reads=[S32[h]], writes=[Sbf[h]])
                S.op("dve", lambda e: e.scalar_tensor_tensor(out=on.ap, in0=QB[0].ap, scalar=st.ap[:, 0:1], in1=onb.ap, op0=ALU.mult, op1=ALU.mult),
                     reads=[QB[0], st, onb], writes=[on])
                S.op("dve", lambda e: e.tensor_tensor(out=ob.ap, in0=on.ap, in1=zs.ap[:, j, :], op=ALU.mult), reads=[on, zs], writes=[ob])
                if pi == 15:
                    dma("sp", rp_d[h], S32[h].ap, reads=[S32[h]], writes=[Buf()])
                yield
                pt2 = rPTq.next()
                S.op("pe", lambda e: e.transpose(out=pt2.ap, in_=ob.ap, identity=identb), reads=[ob, cbf], writes=[pt2])
                yield
                S.op("act", lambda e: e.activation(out=onT.ap[:, h, cols], in_=pt2.ap, func=AF.Copy), reads=[pt2], writes=[onT])
                yield

        for grp in range(4):
            heads = [4 * grp + i for i in range(4)]
            for kk in range(2):
                kh = 2 * grp + kk
                wq = wload(kh * 128)
                wk = wload(1024 + kh * 128)
                co = proj_conv(wq, kh)
                l2n(co, qTs[kk], 128.0 ** -0.5)
                co = proj_conv(wk, 8 + kh)
                l2n(co, kTs[kk], 1.0)
                to_tok(kTs[kk], ktoks[kk])
            for c, h in enumerate(heads):
                wv = wload(2048 + h * 128)
                wz = wload(4096 + h * 128)
                co = proj_conv(wv, 16 + h)
                vT = vTring.next()
                S.op("act", lambda e: e.activation(out=vT.ap, in_=co.ap, func=AF.Copy), reads=[co], writes=[vT])
                to_tok(vT, vtoks[c])
                zproj(wz, zss[c])
                if sbi == 0:
                    S.op("dve", lambda e: e.memset(S32[h].ap, 0.0), writes=[S32[h]])
                    S.op("dve", lambda e: e.memset(Sbf[h].ap, 0.0), writes=[Sbf[h]])
            if has_s:
                S.op("dve", lambda e: e.memset(padw.ap, 0.0), writes=[padw])
                S.op("dve", lambda e: e.memset(padq.ap, 0.0), writes=[padq])
                for c, h in enumerate(heads):
                    sample_state(h, qTs[c // 2], kTs[c // 2], ktoks[c // 2], vtoks[c], zss[c])
            allg = [chain(c, h, qTs[c // 2], kTs[c // 2], ktoks[c // 2], vtoks[c], zss[c]) for c, h in enumerate(heads)]
            gw = 4
            for g0 in range(0, 4, gw):
              gens = allg[g0:g0 + gw]
              while gens:
                alive = []
                for g_ in gens:
                    try:
                        next(g_)
                        alive.append(g_)
                    except StopIteration:
                        pass
                gens = alive

        if dbg_d is not None and sbi == 0:
            dma("sp", dbg_d, ybig, reads=[onT], writes=[Buf()])
        areset()
        gpost = aalloc([128, D], F32)
        dma("sp", gpost.ap, nmo_d[1:2, :].partition_broadcast(128), writes=[gpost])
        wo_all = [aalloc([128, 4, D], BF16) for _ in range(4)]
        for hg in range(4):
            dma("pool", wo_all[hg].ap, gwo_d[hg * 512:(hg + 1) * 512, :].rearrange("(c p) n -> p c n", p=128), writes=[wo_all[hg]])
        for j in tiles:
            pys = [rPF.next(), rPF.next()]
            for hg in range(4):
                wo = wo_all[hg]
                for hh in range(4):
                    h = hg * 4 + hh
                    for dh in range(2):
                        mm(pys[dh], pys[dh].ap, onT.ap[:, h, j * 128:(j + 1) * 128], wo.ap[:, hh, dh * 512:(dh + 1) * 512], [onT, wo],
                           start=(h == 0), stop=(h == 15))
            m32 = rTmp.next()
            for dh in range(2):
                S.op("act", lambda e, dh=dh: e.activation(out=m32.ap[:, dh * 512:(dh + 1) * 512], in_=pys[dh].ap, func=AF.Copy), reads=[pys[dh]], writes=[m32])
            post_norm_add(j, m32.ap, m32, gpost)

    for sbi in range(2):
        if sbi == 0:
            kinds = [("s", None)] + [("p", i) for i in range(8)]
            blocks = [(0, 128), (128, 512), (640, 512)]
        else:
            kinds = [("p", 8 + i) for i in range(8)]
            blocks = [(0, 512), (512, 512)]
        tiles = list(range(len(kinds)))
        for j, (kind, pi) in zip(tiles, kinds):
            src = xs_d if kind == "s" else xp_d[pi * 128:(pi + 1) * 128, :]
            dma("sp", x_t[j].ap, src, writes=[x_t[j]])
        pool_phase(sbi, tiles, kinds)
        if stop_after not in ("pool", "nod2d"):
            ffn(0, tiles, blocks)
        if stop_after not in ("pool", "nod2d", "l0"):
            gdn_phase(sbi, tiles, kinds, blocks)
            if stop_after != "gdn":
                ffn(1, tiles, blocks)
        for j, (kind, pi) in zip(tiles, kinds):
            dst = ys_d if kind == "s" else yp_d[pi * 128:(pi + 1) * 128, :]
            dma("sp", dst, x_t[j].ap, reads=[x_t[j]], writes=[Buf()])

    with nc.allow_low_precision("bf16 matmul operands, fp32 accumulation"):
        S.emit()
    return nc


_NC_CACHE = {}


def make_in_maps(inp):
    c32, pm = _consts()
    f = lambda a: np.ascontiguousarray(np.asarray(a, dtype=np.float32))
    shared = {
        "nmp": f(inp["norm_mix_pre"]), "nmo": f(inp["norm_mix_post"]),
        "nfp": f(inp["norm_ffn_pre"]), "nfo": f(inp["norm_ffn_post"]),
        "pw": f(inp["pool_w"][0]), "psc": f(inp["pool_scale"]),
        "gwi": f(np.asarray(inp["gdn_w_in"][0])[:, :6144].reshape(8, 128, 48, 128).transpose(2, 1, 0, 3).reshape(48, 128, 1024)),
        "gwba": f(np.asarray(inp["gdn_w_in"][0])[:, 6144:6176]), "gcw": f(np.asarray(inp["gdn_conv_w"][0]).T),
        "galog": f(inp["gdn_a_log"]), "gdtb": f(inp["gdn_dt_bias"]), "gon": f(inp["gdn_o_norm"]),
        "gwo": f(inp["gdn_w_out"][0]), "fwi": f(inp["ffn_w_in"]), "fwo": f(inp["ffn_w_out"]),
        "c32": c32, "pm": pm,
    }
    maps = []
    for c in range(NCORE):
        sl = slice(16 * c, 16 * (c + 1))
        m = dict(shared)
        m["xp"] = f(inp["x_prompt"][c])
        m["xs"] = f(np.asarray(inp["x_sample"][sl]).reshape(128, D))
        m["pst"] = f(np.asarray(inp["state_pool"][0, sl]).reshape(240, D))
        m["cst"] = f(np.asarray(inp["state_gdn_conv"][0, sl]).transpose(2, 0, 1))
        m["rst"] = f(inp["state_gdn_rec"][0, sl])
        maps.append(m)
    return maps


def kernel(**inp):
    if "nc" not in _NC_CACHE:
        _NC_CACHE["nc"] = build()
    nc = _NC_CACHE["nc"]
    maps = make_in_maps(inp)
    res = run_bass_kernel_spmd(nc, maps, core_ids=list(range(NCORE)))
    R = res.results
    y_p = np.stack([R[c]["yp"] for c in range(NCORE)]).reshape(8, 2048, D)
    y_s = np.concatenate([R[c]["ys"].reshape(16, 8, D) for c in range(NCORE)])
    pool_p = np.stack([R[c]["pp"] for c in range(NCORE)])[None]
    pool_s = np.concatenate([R[c]["pso"] for c in range(NCORE)])[None]
    conv_p = np.stack([R[c]["cp"].T for c in range(NCORE)])[None]
    conv_s = np.concatenate([R[c]["cso"].transpose(1, 2, 0) for c in range(NCORE)])[None]
    rec_p = np.stack([R[c]["rp"] for c in range(NCORE)])[None]
    rec_s = np.concatenate([R[c]["rso"] for c in range(NCORE)])[None]
    return (y_p, y_s, pool_p, pool_s, conv_p, conv_s, rec_p, rec_s)
```

```python
import numpy as np
import concourse.bass as bass
import concourse.mybir as mybir
from concourse.bass_utils import run_bass_kernel_spmd

F32 = mybir.dt.float32
BF16 = mybir.dt.bfloat16
AF = mybir.ActivationFunctionType
ALU = mybir.AluOpType

D = 1024
DFF = 2816
NCORE = 8
NEG = -30000.0


class Buf:
    __slots__ = ("w", "r", "excl")

    def __init__(self):
        self.w = None
        self.r = []
        self.excl = False


class T:
    def __init__(self, ap, bufs=None):
        self.ap = ap
        self.b = bufs if bufs is not None else [Buf()]


class Ring:
    def __init__(self, items):
        self.items = items
        self.i = 0

    def next(self):
        t = self.items[self.i % len(self.items)]
        self.i += 1
        return t


def _bufs(lst):
    out = []
    for t in lst:
        if isinstance(t, Buf):
            out.append(t)
        else:
            out.extend(t.b)
    return out


class _Rec:
    def __getattr__(self, name):
        def f(*a, **k):
            self.__dict__["call"] = (name, a, k)
            return self
        return f


class Sched:
    ENG = ["pe", "act", "dve", "pool", "sp"]
    NDSEM = 12

    def __init__(self, nc):
        self.nc = nc
        self.ops = []
        self.by_eng = {e: [] for e in self.ENG}
        self.ndma = {e: 0 for e in self.ENG}

    def op(self, eng, fn, reads=(), writes=(), dma=False):
        import os
        if len(self.ops) >= int(os.environ.get("MAXOPS", "100000000")):
            return None
        rec_ = _Rec()
        fn(rec_)
        call = rec_.call
        fn = lambda e, call=call: getattr(e, call[0])(*call[1], **call[2])
        reads = _bufs(reads)
        writes = _bufs(writes)
        writes = writes + [b for b in reads if b.excl and b not in writes]
        reads = [b for b in reads if not b.excl]
        oid = len(self.ops)
        deps = {}
        for b in reads:
            if b.w is not None:
                deps[(b.w, "raw")] = 1
        for b in writes:
            if b.w is not None:
                deps[(b.w, "waw")] = 1
            for r in b.r:
                deps[(r, "war")] = 1
        for b in reads:
            b.r.append(oid)
        for b in writes:
            b.w = oid
            b.r = []
        real = {}
        for (p, kind) in deps:
            if p == oid:
                continue
            po = self.ops[p]
            if not po[3] and po[0] == eng and eng == "pe" and not dma:
                continue
            real[p] = 1
        rec = [eng, fn, list(real.keys()), dma, False, None, None]
        if dma:
            rec[5] = self.ndma[eng]
            self.ndma[eng] += 1
        self.ops.append(rec)
        self.by_eng[eng].append(oid)
        for p in real:
            self.ops[p][4] = True
        return oid

    def emit(self):
        nc = self.nc
        engsem = {e: nc.alloc_semaphore(name=f"es_{e}") for e in self.ENG}
        dsem = {e: [nc.alloc_semaphore(name=f"ds_{e}{i}") for i in range(self.NDSEM)]
                for e in self.ENG if self.ndma[e] > 0}
        for e in self.ENG:
            c = 0
            for oid in self.by_eng[e]:
                o = self.ops[oid]
                if not o[3] and o[4]:
                    c += 1
                    o[6] = c
        K = self.NDSEM
        ops = self.ops
        allsems = list(engsem.values()) + [x for v in dsem.values() for x in v]
        for sm in allsems:
            nc.gpsimd.sem_clear(sm)
        nc.all_engine_barrier()

        def run(e, eng):
            waited = {}

            def wait(sem, val):
                key = id(sem)
                if waited.get(key, 0) >= val:
                    return
                waited[key] = val
                eng.wait_ge(sem, val)

            for oid in self.by_eng[e]:
                o = ops[oid]
                for p in o[2]:
                    po = ops[p]
                    if po[3]:
                        wait(dsem[po[0]][po[5] % K], 16 * (po[5] // K + 1))
                    else:
                        wait(engsem[po[0]], po[6])
                if o[3]:
                    j = o[5]
                    if j >= K:
                        wait(dsem[e][j % K], 16 * (j // K))
                inst = o[1](eng)
                if o[3]:
                    inst.then_inc(dsem[e][o[5] % K], 16)
                elif o[4]:
                    inst.then_inc(engsem[e], 1)
            if e == "sp":
                for q in dsem:
                    n = self.ndma[q]
                    for i in range(min(K, n)):
                        wait(dsem[q][i], 16 * ((n - 1 - i) // K + 1))

        with nc.Block() as block:
            @block.tensor
            def _(eng):
                run("pe", eng)

            @block.scalar
            def _(eng):
                run("act", eng)

            @block.vector
            def _(eng):
                run("dve", eng)

            @block.gpsimd
            def _(eng):
                run("pool", eng)

            @block.sync
            def _(eng):
                run("sp", eng)

        nc.all_engine_barrier()
        nc.clear_and_free_semaphores(allsems)
        nc.all_engine_barrier()


def _consts():
    idx = np.arange(128)
    j = idx[:, None]
    i = idx[None, :]
    same = (j // 8) == (i // 8)
    U = (j <= i).astype(np.float32)
    Ub = ((j <= i) & same).astype(np.float32)
    NM = np.where(i >= j, 0.0, NEG).astype(np.float32)
    NMb = np.where((i >= j) & same, 0.0, NEG).astype(np.float32)
    ST = (i > j).astype(np.float32)
    STb = ((i > j) & same).astype(np.float32)
    BO = same.astype(np.float32)
    ONES = np.ones((128, 128), np.float32)
    ID = np.eye(128, dtype=np.float32)
    BM = ((idx[:, None] // 8) == np.arange(16)[None, :]).astype(np.float32)
    c32 = np.concatenate([U, Ub, NM, NMb, ST, STb, BO, ONES, ID, BM], axis=1)
    wins = (2, 4, 8, 16)
    pm = np.zeros((128, 6, 4, 128), np.float32)
    s = idx[:, None]
    t = idx[None, :]
    for g, w in enumerate(wins):
        pm[:, 0, g] = ((s > t - w) & (s <= t)) / w - (s == t)
        cnt = np.minimum(w, t + 1)
        pm[:, 1, g] = ((s > t - w) & (s <= t)) / cnt - (s == t)
        pm[:, 2, g] = ((s - 128) > (t - w)) / w
        sq, sp_ = s // 8, s % 8
        tq, tp = t // 8, t % 8
        pm[:, 3, g] = ((sq == tq) & (sp_ > tp - w) & (sp_ <= tp)) / w - (s == t)
        r = np.arange(120)[:, None]
        rq, rb = r // 15, r % 15
        for half in range(2):
            pm[:120, 4 + half, g] = ((rq + 8 * half == tq) & ((rb - 15) > (tp - w))) / w
    return c32, pm.reshape(128, 6 * 512)


def build(stop_after=None):
    nc = bass.Bass("TRN2", target_bir_lowering=False)

    def din(name, shape):
        return nc.dram_tensor(name, list(shape), F32, kind="ExternalInput").ap()

    def dout(name, shape):
        return nc.dram_tensor(name, list(shape), F32, kind="ExternalOutput").ap()

    xp_d = din("xp", [2048, D])
    xs_d = din("xs", [128, D])
    pst_d = din("pst", [240, D])
    cst_d = din("cst", [4096, 16, 3])
    rst_d = din("rst", [16, 16, 128, 128])
    nmp_d = din("nmp", [2, D])
    nmo_d = din("nmo", [2, D])
    nfp_d = din("nfp", [2, D])
    nfo_d = din("nfo", [2, D])
    pw_d = din("pw", [4, 256, 256])
    psc_d = din("psc", [1, D])
    gwi_d = din("gwi", [48, 128, 8 * 128])
    gwba_d = din("gwba", [D, 32])
    gcw_d = din("gcw", [4096, 4])
    galog_d = din("galog", [1, 16])
    gdtb_d = din("gdtb", [1, 16])
    gon_d = din("gon", [1, 128])
    gwo_d = din("gwo", [2048, D])
    fwi_d = din("fwi", [2, D, 2 * DFF])
    fwo_d = din("fwo", [2, DFF, D])
    c32_d = din("c32", [128, 9 * 128 + 16])
    pm_d = din("pm", [128, 6 * 512])

    yp_d = dout("yp", [2048, D])
    ys_d = dout("ys", [128, D])
    pp_d = dout("pp", [15, D])
    pso_d = dout("pso", [16, 15, D])
    cp_d = dout("cp", [4096, 3])
    cso_d = dout("cso", [4096, 16, 3])
    rp_d = dout("rp", [16, 128, 128])
    rso_d = dout("rso", [16, 16, 128, 128])
    dbg_d = dout("dbg", [128, 9 * D]) if stop_after == "gdn" else None
    dbg2_d = dout("dbg2", [128, 9 * D]) if stop_after == "gdn" else None
    dbg3_d = dout("dbg3", [16, 128, 128]) if stop_after == "gdn" else None
    dbgn = [0]

    def dump(t, ap):
        if dbg3_d is None or dbgn[0] >= 16:
            return
        dma("pool", dbg3_d[dbgn[0]][:, 0:ap.shape[-1]] if len(ap.shape) == 2 else dbg3_d[dbgn[0]], ap, reads=[t], writes=[Buf()])
        dbgn[0] += 1

    S = Sched(nc)
    cnt = [0]

    def sb(shape, dt, name=None):
        cnt[0] += 1
        return nc.alloc_sbuf_tensor(f"s_{name}" if name else f"sb{cnt[0]}", list(shape), dt).ap()

    def tsb(shape, dt, name=None):
        return T(sb(shape, dt, name))

    dram_out = Buf()

    def dma(q, out, in_, reads=(), writes=()):
        S.op(q, lambda e: e.dma_start(out=out, in_=in_), reads=reads, writes=writes, dma=True)

    ARENA = 82 * 1024
    arena = sb([128, ARENA // 4], F32, "arena")
    ablk = [Buf() for _ in range(ARENA // 512)]
    apos = [0]

    def areset():
        apos[0] = 0

    def aalloc(shape, dt):
        esz = 4 if dt == F32 else 2
        n = esz
        for d_ in shape[1:]:
            n *= d_
        off = apos[0]
        nb = (n + 511) // 512 * 512
        assert off + nb <= ARENA, ("arena overflow", off, nb)
        apos[0] = off + nb
        v = arena[:, off // 4:(off + nb) // 4]
        if dt != F32:
            v = v.bitcast(dt)
        v = v[:, :n // esz]
        if len(shape) == 3:
            v = v.rearrange("p (a b) -> p a b", a=shape[1])
        elif len(shape) == 4:
            v = v.rearrange("p (a b c) -> p a b c", a=shape[1], b=shape[2])
        if shape[0] < 128:
            v = v[:shape[0]]
        return T(v, ablk[off // 512:(off + nb) // 512])

    c32 = tsb([128, 9 * 128 + 16], F32, "c32")
    dma("sp", c32.ap, c32_d, writes=[c32])
    cU, cUb, cNM, cNMb, cST, cSTb, cBO, cONES, cID = [c32.ap[:, k * 128:(k + 1) * 128] for k in range(9)]
    cBM = c32.ap[:, 9 * 128:9 * 128 + 16]
    cbf = tsb([128, 2, 128], BF16, "cbf")
    S.op("dve", lambda e: e.tensor_copy(out=cbf.ap[:, 0, :], in_=cID), reads=[c32], writes=[cbf])
    S.op("dve", lambda e: e.tensor_copy(out=cbf.ap[:, 1, :], in_=cONES), reads=[c32], writes=[cbf])
    identb = cbf.ap[:, 0, :]
    onesb = cbf.ap[:, 1, :]

    banks = [nc.alloc_psum_tensor(f"pb{k}", [128, 512], F32).ap() for k in range(6)]
    PF = [T(banks[k]) for k in range(6)]
    PH = [T(banks[3 + k // 2][:, (k % 2) * 256:(k % 2) * 256 + 256], PF[3 + k // 2].b) for k in range(6)]
    ptr_ap = nc.alloc_psum_tensor("ptr", [128, 8, 128], BF16).ap()
    ptrB = [Buf()]
    PTq = [T(ptr_ap[:, k, :], ptrB) for k in range(8)]
    PTh = [T(ptr_ap[:, 4 * k:4 * k + 4, :], ptrB) for k in range(2)]
    pm_ap = nc.alloc_psum_tensor("pmisc", [128, 512], F32).ap()
    pmB = [Buf()]
    PM = [T(pm_ap[:, k * 128:(k + 1) * 128], pmB) for k in range(4)]
    for t_ in PF:
        t_.b[0].excl = True
    ptrB[0].excl = True
    pmB[0].excl = True
    rPF = Ring(PF)
    rPB = Ring(PF[:3])
    rPH = Ring(PH)
    rPTq = Ring(PTq)
    rPTh = Ring(PTh)
    rPM = Ring(PM)

    NTM = 9
    xbig = sb([128, NTM, D], F32, "x")
    x_t = [T(xbig[:, j, :]) for j in range(NTM)]
    hTbig = sb([128, 8, NTM * 128], BF16, "hT")
    hT_t = [T(hTbig[:, :, j * 128:(j + 1) * 128]) for j in range(NTM)]
    ybig = sb([128, NTM * D], F32, "yacc")
    y_t = [T(ybig[:, j * D:(j + 1) * D]) for j in range(NTM)]
    junk = tsb([128, D], BF16, "junk")
    rSmall = Ring([tsb([128, 2], F32) for _ in range(6)])
    rHb = Ring([tsb([128, D], BF16) for _ in range(2)])
    rHbPool = Ring([tsb([128, D], BF16) for _ in range(2)])
    rTmp = Ring([tsb([128, D], F32) for _ in range(2)])

    def load_gain(src_row):
        g = aalloc([128, D], F32)
        dma("sp", g.ap, src_row.partition_broadcast(128), writes=[g])
        return g

    def rstd_of(src_ap, srcT, rows, n):
        st = rSmall.next()
        sc = float(n) ** -0.5
        S.op("act", lambda e: e.activation(out=junk.ap[:rows, :n], in_=src_ap, func=AF.Square, scale=sc,
                                           accum_out=st.ap[:rows, 0:1]), reads=[srcT], writes=[junk, st])
        S.op("act", lambda e: e.activation(out=st.ap[:rows, 1:2], in_=st.ap[:rows, 0:1], func=AF.Ln,
                                           bias=1e-6, scale=1.0), reads=[st], writes=[st])
        S.op("act", lambda e: e.activation(out=st.ap[:rows, 0:1], in_=st.ap[:rows, 1:2], func=AF.Exp,
                                           scale=-0.5), reads=[st], writes=[st])
        return st

    def pre_norm(j, gain, ring, out32=None):
        st = rstd_of(x_t[j].ap, x_t[j], 128, D)
        hb = ring.next()
        if out32 is not None:
            S.op("dve", lambda e: e.scalar_tensor_tensor(out=out32.ap, in0=x_t[j].ap, scalar=st.ap[:, 0:1],
                                                         in1=gain.ap, op0=ALU.mult, op1=ALU.mult),
                 reads=[x_t[j], st, gain], writes=[out32])
            S.op("act", lambda e: e.activation(out=hb.ap, in_=out32.ap, func=AF.Copy), reads=[out32], writes=[hb])
        else:
            S.op("dve", lambda e: e.scalar_tensor_tensor(out=hb.ap, in0=x_t[j].ap, scalar=st.ap[:, 0:1],
                                                         in1=gain.ap, op0=ALU.mult, op1=ALU.mult),
                 reads=[x_t[j], st, gain], writes=[hb])
        return hb

    def to_hT(j, hb):
        for half in range(2):
            pt = rPTh.next()
            for k in range(4):
                kk = half * 4 + k
                S.op("pe", lambda e, kk=kk, k=k, pt=pt: e.transpose(out=pt.ap[:, k, :], in_=hb.ap[:, kk * 128:(kk + 1) * 128],
                                                                     identity=identb), reads=[hb, cbf], writes=[pt])
            S.op("act", lambda e, pt=pt, half=half: e.activation(out=hT_t[j].ap[:, half * 4:half * 4 + 4, :], in_=pt.ap, func=AF.Copy),
                 reads=[pt], writes=[hT_t[j]])

    def post_norm_add(j, src_ap, srcT, gain):
        st = rstd_of(src_ap, srcT, 128, D)
        tmp = rTmp.next()
        S.op("dve", lambda e: e.scalar_tensor_tensor(out=tmp.ap, in0=src_ap, scalar=st.ap[:, 0:1], in1=gain.ap,
                                                     op0=ALU.mult, op1=ALU.mult), reads=[srcT, st, gain], writes=[tmp])
        S.op("dve", lambda e: e.tensor_tensor(out=x_t[j].ap, in0=x_t[j].ap, in1=tmp.ap, op=ALU.add),
             reads=[x_t[j], tmp], writes=[x_t[j]])

    def ffn(layer, tiles, blocks):
        areset()
        rWg = Ring([aalloc([128, 8, 512], BF16) for _ in range(2)])
        rWu = Ring([aalloc([128, 8, 512], BF16) for _ in range(2)])
        rWo = Ring([aalloc([128, 4, D], BF16) for _ in range(2)])
        rAct = Ring([aalloc([128, 4, 512], BF16) for _ in range(2)])
        rSg = Ring([aalloc([128, 512], F32) for _ in range(2)])
        gpre = load_gain(nfp_d[layer:layer + 1, :])
        gpost = load_gain(nfo_d[layer:layer + 1, :])
        groups = [(f0, min(4, 22 - f0)) for f0 in range(0, 22, 4)]

        def load_group(gi):
            f0, nf = groups[gi]
            wg, wu, wo = rWg.next(), rWu.next(), rWo.next()
            dma("pool", wg.ap[:, :, :nf * 128],
                fwi_d[layer, :, f0 * 128:(f0 + nf) * 128].rearrange("(k p) n -> p k n", p=128), writes=[wg])
            dma("pool", wu.ap[:, :, :nf * 128],
                fwi_d[layer, :, DFF + f0 * 128:DFF + (f0 + nf) * 128].rearrange("(k p) n -> p k n", p=128), writes=[wu])
            dma("pool", wo.ap[:, :nf, :],
                fwo_d[layer, f0 * 128:(f0 + nf) * 128, :].rearrange("(c p) n -> p c n", p=128), writes=[wo])
            return wg, wu, wo

        nxt = load_group(0)
        for j in tiles:
            hb = pre_norm(j, gpre, rHb)
            to_hT(j, hb)
        for gi, (f0, nf) in enumerate(groups):
            wg, wu, wo = nxt
            if gi + 1 < len(groups):
                nxt = load_group(gi + 1)
            for (c0, wd) in blocks:
                tl = list(range(c0 // 128, (c0 + wd) // 128))
                hts = [hT_t[j] for j in tl]
                actT = rAct.next()
                for fc in range(nf):
                    pg = rPF.next()
                    for k in range(8):
                        S.op("pe", lambda e, pg=pg, k=k, fc=fc: e.matmul(pg.ap[:, :wd], lhsT=wg.ap[:, k, fc * 128:(fc + 1) * 128],
                                                                         rhs=hTbig[:, k, c0:c0 + wd], start=(k == 0), stop=(k == 7)),
                             reads=[wg] + hts, writes=[pg])
                    pu = rPF.next()
                    for k in range(8):
                        S.op("pe", lambda e, pu=pu, k=k, fc=fc: e.matmul(pu.ap[:, :wd], lhsT=wu.ap[:, k, fc * 128:(fc + 1) * 128],
                                                                         rhs=hTbig[:, k, c0:c0 + wd], start=(k == 0), stop=(k == 7)),
                             reads=[wu] + hts, writes=[pu])
                    sg = rSg.next()
                    S.op("act", lambda e, pg=pg, sg=sg: e.activation(out=sg.ap[:, :wd], in_=pg.ap[:, :wd], func=AF.Silu),
                         reads=[pg], writes=[sg])
                    S.op("dve", lambda e, pu=pu, sg=sg, fc=fc: e.tensor_tensor(out=actT.ap[:, fc, :wd], in0=pu.ap[:, :wd], in1=sg.ap[:, :wd],
                                                                               op=ALU.mult), reads=[pu, sg], writes=[actT])
                for ti, j in enumerate(tl):
                    for dh in range(2):
                        py = rPF.next()
                        for fc in range(nf):
                            S.op("pe", lambda e, py=py, fc=fc, ti=ti, dh=dh: e.matmul(py.ap, lhsT=actT.ap[:, fc, ti * 128:(ti + 1) * 128],
                                                                                      rhs=wo.ap[:, fc, dh * 512:(dh + 1) * 512],
                                                                                      start=(fc == 0), stop=(fc == nf - 1)),
                                 reads=[actT, wo], writes=[py])
                        ysl = y_t[j].ap[:, dh * 512:(dh + 1) * 512]
                        if gi == 0:
                            S.op("act", lambda e, py=py, ysl=ysl: e.activation(out=ysl, in_=py.ap, func=AF.Copy), reads=[py], writes=[y_t[j]])
                        else:
                            S.op("dve", lambda e, py=py, ysl=ysl: e.tensor_tensor(out=ysl, in0=ysl, in1=py.ap, op=ALU.add),
                                 reads=[py, y_t[j]], writes=[y_t[j]])
        for j in tiles:
            post_norm_add(j, y_t[j].ap, y_t[j], gpost)

    pool_prev = [None]

    def pool_phase(sbi, tiles, kinds):
        areset()
        pmb = aalloc([128, 6, 4, 128], BF16)
        dma("pool", pmb.ap, pm_d.rearrange("p (a g t) -> p a g t", a=6, g=4), writes=[pmb])
        pwb = aalloc([128, 4, 2, 256], BF16)
        pstb = aalloc([128, 2, D], BF16)
        dT = Ring([aalloc([128, 8, 128], BF16) for _ in range(2)])
        gpre = load_gain(nmp_d[0:1, :])
        gpost = load_gain(nmo_d[0:1, :])
        gsc = load_gain(psc_d[0:1, :])
        dma("pool", pwb.ap, pw_d.rearrange("g (c p) e -> p g c e", p=128), writes=[pwb])
        if sbi == 0:
            S.op("dve", lambda e: e.memset(pstb.ap, 0.0), writes=[pstb])
            dma("pool", pstb.ap[:120, :, :], pst_d.rearrange("(a r) d -> r a d", a=2), writes=[pstb])
            if stop_after != "nod2d":
                dma("sp", pso_d[:, 0:7, :], pst_d.rearrange("(s b) d -> s b d", b=15)[:, 8:15, :], writes=[Buf()])
        for j, (kind, pi) in zip(tiles, kinds):
            h32 = y_t[j]
            hb = pre_norm(j, gpre, rHbPool, out32=h32)
            if kind == "s":
                for sq in range(16):
                    dma("sp", pso_d[sq, 7:15, :], h32.ap[sq * 8:(sq + 1) * 8, :], reads=[h32], writes=[Buf()])
            elif pi == 15:
                dma("sp", pp_d, h32.ap[113:128, :], reads=[h32], writes=[Buf()])
            d = dT.next()
            phs = [rPH.next() for _ in range(4)]
            for cc in range(8):
                g = cc // 2
                ph = phs[cc // 2]
                o_ap = ph.ap[:, (cc % 2) * 128:(cc % 2) * 128 + 128]
                lhs_cur = hb.ap[:, cc * 128:(cc + 1) * 128]
                if kind == "s":
                    S.op("pe", lambda e, o_ap=o_ap, lhs_cur=lhs_cur, g=g: e.matmul(o_ap, lhsT=lhs_cur, rhs=pmb.ap[:, 3, g, :], start=True, stop=False),
                         reads=[hb, pmb], writes=[ph])
                    for half in range(2):
                        S.op("pe", lambda e, o_ap=o_ap, cc=cc, g=g, half=half: e.matmul(o_ap, lhsT=pstb.ap[:, half, cc * 128:(cc + 1) * 128],
                                                                                        rhs=pmb.ap[:, 4 + half, g, :], start=False, stop=(half == 1)),
                             reads=[pstb, pmb], writes=[ph])
                elif pi == 0:
                    S.op("pe", lambda e, o_ap=o_ap, lhs_cur=lhs_cur, g=g: e.matmul(o_ap, lhsT=lhs_cur, rhs=pmb.ap[:, 1, g, :], start=True, stop=True),
                         reads=[hb, pmb], writes=[ph])
                else:
                    prev = pool_prev[0]
                    S.op("pe", lambda e, o_ap=o_ap, lhs_cur=lhs_cur, g=g: e.matmul(o_ap, lhsT=lhs_cur, rhs=pmb.ap[:, 0, g, :], start=True, stop=False),
                         reads=[hb, pmb], writes=[ph])
                    S.op("pe", lambda e, o_ap=o_ap, cc=cc, g=g, prev=prev: e.matmul(o_ap, lhsT=prev.ap[:, cc * 128:(cc + 1) * 128], rhs=pmb.ap[:, 2, g, :],
                                                                                    start=False, stop=True), reads=[prev, pmb], writes=[ph])
            for q in range(4):
                S.op("act", lambda e, q=q: e.activation(out=d.ap[:, 2 * q:2 * q + 2, :], in_=phs[q].ap.rearrange("p (a t) -> p a t", a=2), func=AF.Copy),
                     reads=[phs[q]], writes=[d])
            if kind == "p":
                pool_prev[0] = hb
            pys = [rPB.next(), rPB.next()]
            for g in range(4):
                py = pys[g // 2]
                o_ap = py.ap[:, (g % 2) * 256:(g % 2) * 256 + 256]
                for c in range(2):
                    S.op("pe", lambda e, o_ap=o_ap, g=g, c=c: e.matmul(o_ap, lhsT=d.ap[:, 2 * g + c, :], rhs=pwb.ap[:, g, c, :],
                                                                       start=(c == 0), stop=(c == 1)), reads=[d, pwb], writes=[py])
            m32 = rTmp.next()
            for hh in range(2):
                S.op("dve", lambda e, hh=hh: e.tensor_tensor(out=m32.ap[:, hh * 512:(hh + 1) * 512], in0=pys[hh].ap,
                                                             in1=gsc.ap[:, hh * 512:(hh + 1) * 512], op=ALU.mult),
                     reads=[pys[hh], gsc], writes=[m32])
            post_norm_add(j, m32.ap, m32, gpost)


    S32 = [tsb([128, 128], F32, f"S32_{h}") for h in range(16)]
    Sbf = [tsb([128, 128], BF16, f"Sbf_{h}") for h in range(16)]
    ctail = tsb([128, 32, 3], F32, "ctail")

    def gdn_phase(sbi, tiles, kinds, blocks):
        NTl = len(tiles)
        Ncol = NTl * 128
        has_s = kinds[0][0] == "s"
        p0 = 128 if has_s else 0
        Np = Ncol - p0
        areset()
        if dbg2_d is not None and sbi == 0:
            dma("sp", dbg2_d.rearrange("p (a b) -> p a b", a=9), xbig, reads=x_t, writes=[Buf()])
        gpre = load_gain(nmp_d[1:2, :])
        for j in tiles:
            hb = pre_norm(j, gpre, rHb)
            to_hT(j, hb)
        areset()
        gs = aalloc([128, NTl, 6, 16], F32)
        glb = aalloc([128, NTl, 16], F32)
        gls = aalloc([128, 16, 16], F32)
        gblk = aalloc([128, 16, 16], F32)
        wba = aalloc([128, 8, 128], BF16)
        dtb = aalloc([128, 16], F32)
        nea = aalloc([128, 16], F32)
        cw = aalloc([128, 32, 4], F32)
        onb = aalloc([128, 128], F32)
        t16 = Ring([aalloc([128, 16], F32) for _ in range(3)])
        S.op("dve", lambda e: e.memset(wba.ap, 0.0), writes=[wba])
        S.op("dve", lambda e: e.memset(gs.ap, 0.0), writes=[gs])
        dma("pool", wba.ap[:, :, 0:32], gwba_d.rearrange("(k p) n -> p k n", p=128), writes=[wba])
        dma("sp", dtb.ap, gdtb_d.partition_broadcast(128), writes=[dtb])
        dma("sp", nea.ap, galog_d.partition_broadcast(128), writes=[nea])
        dma("sp", cw.ap, gcw_d.rearrange("(c p) t -> p c t", p=128), writes=[cw])
        dma("sp", onb.ap, gon_d.partition_broadcast(128), writes=[onb])
        S.op("act", lambda e: e.activation(out=nea.ap, in_=nea.ap, func=AF.Exp), reads=[nea], writes=[nea])
        S.op("dve", lambda e: e.tensor_scalar(out=nea.ap, in0=nea.ap, scalar1=-1.0, scalar2=None, op0=ALU.mult), reads=[nea], writes=[nea])
        for j, (kind, pi) in zip(tiles, kinds):
            cols = slice(j * 128, (j + 1) * 128)
            pm = rPM.next()
            for k in range(8):
                S.op("pe", lambda e, k=k: e.matmul(pm.ap, lhsT=hTbig[:, k, cols], rhs=wba.ap[:, k, :], start=(k == 0), stop=(k == 7)),
                     reads=[hT_t[j], wba], writes=[pm])
            G = lambda q: gs.ap[:, j, q, :]
            ta, tb = t16.next(), t16.next()
            S.op("act", lambda e: e.activation(out=ta.ap, in_=pm.ap[:, 0:16], func=AF.Exp, scale=-1.0), reads=[pm], writes=[ta])
            S.op("dve", lambda e: e.tensor_tensor(out=tb.ap, in0=pm.ap[:, 16:32], in1=dtb.ap, op=ALU.add), reads=[pm, dtb], writes=[tb])
            S.op("dve", lambda e: e.tensor_scalar(out=ta.ap, in0=ta.ap, scalar1=1.0, scalar2=None, op0=ALU.add), reads=[ta], writes=[ta])
            S.op("dve", lambda e: e.reciprocal(out=G(1), in_=ta.ap), reads=[ta], writes=[gs])
            S.op("dve", lambda e: e.tensor_scalar(out=G(2), in0=G(1), scalar1=-1.0, scalar2=None, op0=ALU.mult), reads=[gs], writes=[gs])
            S.op("act", lambda e: e.activation(out=tb.ap, in_=tb.ap, func=AF.Exp), reads=[tb], writes=[tb])
            S.op("act", lambda e: e.activation(out=tb.ap, in_=tb.ap, func=AF.Ln, bias=1.0, scale=1.0), reads=[tb], writes=[tb])
            S.op("dve", lambda e: e.tensor_tensor(out=G(0), in0=tb.ap, in1=nea.ap, op=ALU.mult), reads=[tb, nea], writes=[gs])
            pm2 = rPM.next()
            pm3 = rPM.next()
            grow = gs.ap[:, j, :, :].rearrange("p a b -> p (a b)")
            S.op("pe", lambda e: e.matmul(pm2.ap[:, 0:96], lhsT=(cUb if kind == "s" else cU), rhs=grow, start=True, stop=True), reads=[c32, gs], writes=[pm2])
            S.op("pe", lambda e: e.matmul(pm3.ap[:, 0:96], lhsT=(cBO if kind == "s" else cONES), rhs=grow, start=True, stop=True), reads=[c32, gs], writes=[pm3])
            S.op("dve", lambda e: e.tensor_copy(out=G(3), in_=pm2.ap[:, 0:16]), reads=[pm2], writes=[gs])
            S.op("act", lambda e: e.activation(out=G(4), in_=pm2.ap[:, 0:16], func=AF.Exp), reads=[pm2], writes=[gs])
            S.op("act", lambda e: e.activation(out=glb.ap[:, j, :], in_=pm3.ap[:, 0:16], func=AF.Exp), reads=[pm3], writes=[glb])
            tc_ = t16.next()
            S.op("dve", lambda e: e.tensor_tensor(out=tc_.ap, in0=pm3.ap[:, 0:16], in1=G(3), op=ALU.subtract), reads=[pm3, gs], writes=[tc_])
            S.op("act", lambda e: e.activation(out=G(5), in_=tc_.ap, func=AF.Exp), reads=[tc_], writes=[gs])
            if kind == "s":
                S.op("dve", lambda e: e.tensor_tensor(out=gblk.ap, in0=G(0).unsqueeze(1).to_broadcast([128, 16, 16]),
                                                      in1=cBM.unsqueeze(2).to_broadcast([128, 16, 16]), op=ALU.mult), reads=[gs, c32], writes=[gblk])
                pb = rPB.next()
                S.op("pe", lambda e: e.matmul(pb.ap[:, 0:256], lhsT=cONES, rhs=gblk.ap.rearrange("p a b -> p (a b)"), start=True, stop=True),
                     reads=[c32, gblk], writes=[pb])
                S.op("act", lambda e: e.activation(out=gls.ap.rearrange("p a b -> p (a b)"), in_=pb.ap[:, 0:256], func=AF.Exp), reads=[pb], writes=[gls])

        onT = T(ybig.bitcast(BF16)[:, :16 * Ncol].rearrange("p (a b) -> p a b", a=16), [b for t in y_t for b in t.b])
        rPBg = Ring([PF[4], PF[5]])
        qTs = [aalloc([128, Ncol], BF16) for _ in range(2)]
        kTs = [aalloc([128, Ncol], BF16) for _ in range(2)]
        ktoks = [aalloc([128, NTl, 128], BF16) for _ in range(2)]
        vtoks = [aalloc([128, NTl, 128], BF16) for _ in range(4)]
        zss = [aalloc([128, NTl, 128], BF16) for _ in range(4)]
        umark = apos[0]
        wring = Ring([aalloc([128, 8, 128], BF16) for _ in range(4)])
        prering = Ring([aalloc([128, 3 + 1024], F32) for _ in range(2)])
        presring = Ring([aalloc([128, 16, 11], F32) for _ in range(2)])
        coring = Ring([aalloc([128, Ncol], F32) for _ in range(2)])
        sqring = Ring([aalloc([128, 512], BF16) for _ in range(1)])
        rnring = Ring([aalloc([128, 512], F32) for _ in range(1)])
        vTring = Ring([aalloc([128, Ncol], BF16) for _ in range(1)])
        uend = apos[0]
        apos[0] = umark
        chains = []
        cstart = []
        for c in range(4):
            cstart.append(apos[0])
            bb_ = aalloc([128, 8, 128], BF16)
            chains.append(dict(b=[T(bb_.ap[:, i_, :], bb_.b[i_ // 2:i_ // 2 + 1]) for i_ in range(8)],
                               f=[aalloc([128, 128], F32) for _ in range(4)],
                               x=[aalloc([128, 2, 128], F32) for _ in range(2)],
                               r=[aalloc([128, 128], F32) for _ in range(2)]))
        uend = max(uend, apos[0])
        if has_s:
            apos[0] = cstart[1]
            padw = aalloc([128, 16 * 136], BF16)
            padq = aalloc([128, 16 * 136], BF16)
            kpad = aalloc([128, 16, 128], BF16)
            s0f = aalloc([128, 8, 128], F32)
            s0b = aalloc([128, 8, 128], BF16)
            snw = aalloc([128, 8, 128], F32)
            uend = max(uend, apos[0])
        apos[0] = uend
        f128 = Ring(chains[0]["f"])
        b128 = Ring(chains[0]["b"])
        xx = Ring(chains[0]["x"])
        rR = Ring(chains[0]["r"])

        def proj_conv(w, cidx, silu=True):
            pre, pres, co = prering.next(), presring.next(), coring.next()
            for (c0, wd) in blocks:
                ps = rPBg.next()
                tl = [hT_t[t] for t in range(c0 // 128, (c0 + wd) // 128)]
                for k in range(8):
                    S.op("pe", lambda e, k=k: e.matmul(ps.ap[:, :wd], lhsT=w.ap[:, k, :], rhs=hTbig[:, k, c0:c0 + wd], start=(k == 0), stop=(k == 7)),
                         reads=[w] + tl, writes=[ps])
                if has_s and c0 == 0:
                    S.op("act", lambda e: e.activation(out=pres.ap[:, :, 3:11], in_=ps.ap[:, 0:128].rearrange("p (a b) -> p a b", a=16), func=AF.Copy),
                         reads=[ps], writes=[pres])
                else:
                    S.op("act", lambda e: e.activation(out=pre.ap[:, 3 + c0 - p0:3 + c0 - p0 + wd], in_=ps.ap[:, :wd], func=AF.Copy), reads=[ps], writes=[pre])
            if sbi == 0:
                S.op("dve", lambda e: e.memset(pre.ap[:, 0:3], 0.0), writes=[pre])
            else:
                S.op("dve", lambda e: e.tensor_copy(out=pre.ap[:, 0:3], in_=ctail.ap[:, cidx, :]), reads=[ctail], writes=[pre])
            if has_s:
                dma("sp", pres.ap[:, :, 0:3], cst_d[cidx * 128:(cidx + 1) * 128, :, :], writes=[pres])
            if sbi == 0:
                S.op("dve", lambda e: e.tensor_copy(out=ctail.ap[:, cidx, :], in_=pre.ap[:, Np:Np + 3]), reads=[pre], writes=[ctail])
            else:
                dma("sp", cp_d[cidx * 128:(cidx + 1) * 128, :], pre.ap[:, Np:Np + 3], reads=[pre], writes=[Buf()])
            if has_s:
                dma("sp", cso_d[cidx * 128:(cidx + 1) * 128, :, :], pres.ap[:, :, 8:11], reads=[pres], writes=[Buf()])
            views = [(co.ap[:, p0:Ncol], lambda tap: pre.ap[:, tap:tap + Np], pre)]
            if has_s:
                views.append((co.ap[:, 0:128].rearrange("p (a b) -> p a b", a=16), lambda tap: pres.ap[:, :, tap:tap + 8], pres))
            for (o_ap, src, srcT) in views:
                S.op("dve", lambda e: e.tensor_scalar(out=o_ap, in0=src(0), scalar1=cw.ap[:, cidx, 0:1], scalar2=None, op0=ALU.mult),
                     reads=[srcT, cw], writes=[co])
                for tap in range(1, 4):
                    S.op("dve", lambda e, tap=tap: e.scalar_tensor_tensor(out=o_ap, in0=src(tap), scalar=cw.ap[:, cidx, tap:tap + 1], in1=o_ap,
                                                                           op0=ALU.mult, op1=ALU.add), reads=[srcT, cw, co], writes=[co])
            S.op("act", lambda e: e.activation(out=co.ap, in_=co.ap, func=AF.Silu), reads=[co], writes=[co])
            return co

        def l2n(co, dst, scale):
            for (c0, wd) in blocks:
                sq, rn = sqring.next(), rnring.next()
                S.op("dve", lambda e: e.tensor_tensor(out=sq.ap[:, :wd], in0=co.ap[:, c0:c0 + wd], in1=co.ap[:, c0:c0 + wd], op=ALU.mult), reads=[co], writes=[sq])
                ps = rPBg.next()
                S.op("pe", lambda e: e.matmul(ps.ap[:, :wd], lhsT=onesb, rhs=sq.ap[:, :wd], start=True, stop=True), reads=[cbf, sq], writes=[ps])
                S.op("act", lambda e: e.activation(out=rn.ap[:, :wd], in_=ps.ap[:, :wd], func=AF.Ln, bias=1e-6, scale=1.0), reads=[ps], writes=[rn])
                S.op("act", lambda e: e.activation(out=rn.ap[:, :wd], in_=rn.ap[:, :wd], func=AF.Exp, scale=-0.5), reads=[rn], writes=[rn])
                S.op("dve", lambda e: e.scalar_tensor_tensor(out=dst.ap[:, c0:c0 + wd], in0=co.ap[:, c0:c0 + wd], scalar=scale, in1=rn.ap[:, :wd],
                                                             op0=ALU.mult, op1=ALU.mult), reads=[co, rn], writes=[dst])

        PTall = T(ptr_ap, ptrB)
        PMall = T(pm_ap, pmB)

        def to_tok(srcT, dst):
            j0 = 0
            while j0 < NTl:
                n = min(8, NTl - j0)
                for i in range(n):
                    S.op("pe", lambda e, i=i: e.transpose(out=ptr_ap[:, i, :], in_=srcT.ap[:, (j0 + i) * 128:(j0 + i + 1) * 128], identity=identb),
                         reads=[srcT, cbf], writes=[PTall])
                S.op("act", lambda e: e.activation(out=dst.ap[:, j0:j0 + n, :], in_=ptr_ap[:, 0:n, :], func=AF.Copy), reads=[PTall], writes=[dst])
                j0 += n

        def wload(col0):
            w = wring.next()
            dma("pool", w.ap, gwi_d[col0 // 128].rearrange("p (k n) -> p k n", k=8), writes=[w])
            return w

        def mm(out_t, out_ap, lhsT, rhs, R, start=True, stop=True):
            S.op("pe", lambda e: e.matmul(out_ap, lhsT=lhsT, rhs=rhs, start=start, stop=stop), reads=R, writes=[out_t])


        def zproj(w, dst):
            j0 = 0
            while j0 < NTl:
                n = min(4, NTl - j0)
                for i in range(n):
                    jj = j0 + i
                    for k in range(8):
                        mm(PMall, pm_ap[:, i * 128:(i + 1) * 128], hTbig[:, k, jj * 128:(jj + 1) * 128], w.ap[:, k, :], [hT_t[jj], w],
                           start=(k == 0), stop=(k == 7))
                S.op("act", lambda e: e.activation(out=dst.ap[:, j0:j0 + n, :], in_=pm_ap[:, 0:n * 128].rearrange("p (a b) -> p a b", a=n), func=AF.Silu),
                     reads=[PMall], writes=[dst])
                j0 += n

        def finish_o(po_ap, po, on, ob, zs, j, h):
            st = rstd_of(po_ap, po, 128, 128)
            S.op("dve", lambda e: e.scalar_tensor_tensor(out=on.ap, in0=po_ap, scalar=st.ap[:, 0:1], in1=onb.ap, op0=ALU.mult, op1=ALU.mult),
                 reads=[po, st, onb], writes=[on])
            S.op("dve", lambda e: e.tensor_tensor(out=ob.ap, in0=on.ap, in1=zs.ap[:, j, :], op=ALU.mult), reads=[on, zs], writes=[ob])

        def sample_state(h, qT, kT, ktok, vtok, zs):
            j = 0
            smp = True
            cols = slice(0, 128)
            G = lambda q: gs.ap[:, j, q, h:h + 1]
            pkq = rPH.next()
            mm(pkq, pkq.ap[:, 0:128], kT.ap[:, cols], qT.ap[:, cols], [kT, qT])
            mm(pkq, pkq.ap[:, 128:256], kT.ap[:, cols], kT.ap[:, cols], [kT])
            prow = rPM.next()
            mm(prow, prow.ap, gs.ap[:, j, 0, h:h + 1].to_broadcast([128, 128]), cUb, [gs, c32])
            E = f128.next()
            S.op("dve", lambda e: e.scalar_tensor_tensor(out=E.ap, in0=prow.ap, scalar=G(3), in1=cNMb, op0=ALU.subtract, op1=ALU.add),
                 reads=[prow, gs, c32], writes=[E])
            S.op("act", lambda e: e.activation(out=E.ap, in_=E.ap, func=AF.Exp), reads=[E], writes=[E])
            egb = f128.next()
            S.op("act", lambda e: e.activation(out=egb.ap, in_=prow.ap, func=AF.Exp), reads=[prow], writes=[egb])
            qkT = b128.next()
            S.op("dve", lambda e: e.tensor_tensor(out=qkT.ap, in0=pkq.ap[:, 0:128], in1=E.ap, op=ALU.mult), reads=[pkq, E], writes=[qkT])
            dS = f128.next()
            S.op("dve", lambda e: e.tensor_tensor(out=dS.ap, in0=E.ap, in1=cSTb, op=ALU.mult), reads=[E, c32], writes=[dS])
            X = xx.next()
            S.op("dve", lambda e: e.scalar_tensor_tensor(out=X.ap[:, 0, :], in0=pkq.ap[:, 128:256], scalar=G(2), in1=dS.ap, op0=ALU.mult, op1=ALU.mult),
                 reads=[pkq, gs, dS], writes=[X])
            pt = rPM.next()
            mm(pt, pt.ap, X.ap[:, 0, :], cID, [X, c32])
            S.op("act", lambda e: e.activation(out=X.ap[:, 1, :], in_=pt.ap, func=AF.Copy), reads=[pt], writes=[X])
            Rm = rR.next()
            S.op("dve", lambda e: e.tensor_tensor(out=Rm.ap, in0=X.ap[:, 0, :], in1=cID, op=ALU.add), reads=[X, c32], writes=[Rm])
            nlev = 2
            for lev in range(nlev):
                px = rPH.next()
                last = lev == nlev - 1
                mm(px, px.ap[:, 128:256], X.ap[:, 0, :], X.ap[:, 1, :], [X])
                if not last:
                    mm(px, px.ap[:, 0:128], X.ap[:, 1, :], X.ap[:, 0, :], [X])
                X2 = xx.next()
                if last:
                    S.op("act", lambda e: e.activation(out=X2.ap[:, 1, :], in_=px.ap[:, 128:256], func=AF.Copy), reads=[px], writes=[X2])
                else:
                    S.op("act", lambda e: e.activation(out=X2.ap, in_=px.ap.rearrange("p (a b) -> p a b", a=2), func=AF.Copy), reads=[px], writes=[X2])
                pr = rPM.next()
                mm(pr, pr.ap, X2.ap[:, 1, :], Rm.ap, [X2, Rm])
                R2 = rR.next()
                S.op("dve", lambda e: e.tensor_tensor(out=R2.ap, in0=pr.ap, in1=Rm.ap, op=ALU.add), reads=[pr, Rm], writes=[R2])
                Rm, X = R2, X2
            Rb = b128.next()
            S.op("act", lambda e: e.activation(out=Rb.ap, in_=Rm.ap, func=AF.Copy), reads=[Rm], writes=[Rb])
            Rm = Rb
            kg = b128.next()
            S.op("act", lambda e: e.mul(out=kg.ap, in_=ktok.ap[:, j, :], mul=G(4)), reads=[ktok, gs], writes=[kg])
            kd = b128.next()
            S.op("act", lambda e: e.mul(out=kd.ap, in_=ktok.ap[:, j, :], mul=G(5)), reads=[ktok, gs], writes=[kd])
            pw_ = rPM.next()
            mm(pw_, pw_.ap, kg.ap, Rm.ap, [kg, Rm])
            qd = b128.next()
            S.op("dve", lambda e: e.tensor_tensor(out=qd.ap, in0=qT.ap[:, cols], in1=egb.ap, op=ALU.mult), reads=[qT, egb], writes=[qd])
            vn = b128.next()
            if True:
                        pw3 = padw.ap[:, :].rearrange("p (a b) -> p a b", b=136)[:, :, 0:8]
                        pq3 = padq.ap[:, :].rearrange("p (a b) -> p a b", b=136)[:, :, 0:8]
                        S.op("dve", lambda e: e.tensor_scalar(out=pw3, in0=pw_.ap.rearrange("p (a b) -> p a b", a=16), scalar1=-1.0, scalar2=None, op0=ALU.mult),
                             reads=[pw_], writes=[padw])
                        S.op("dve", lambda e: e.tensor_copy(out=pq3, in_=qd.ap.rearrange("p (a b) -> p a b", a=16)), reads=[qd], writes=[padq])
                        S.op("dve", lambda e: e.tensor_tensor(out=kpad.ap, in0=kd.ap.unsqueeze(1).to_broadcast([128, 16, 128]),
                                                              in1=cBM.unsqueeze(2).to_broadcast([128, 16, 128]), op=ALU.mult), reads=[kd, c32], writes=[kpad])
                        pv = rPM.next()
                        po = rPH.next()
                        po_ap = po.ap[:, 0:128]
                        for half in range(2):
                            dma("sp", s0f.ap, rst_d[half * 8:(half + 1) * 8, h].rearrange("s k v -> k s v"), writes=[s0f])
                            dma("pool", s0b.ap, rst_d[half * 8:(half + 1) * 8, h].rearrange("s k v -> k s v"), writes=[s0b])
                            if half == 0:
                                mm(pv, pv.ap, Rm.ap, vtok.ap[:, j, :], [Rm, vtok], start=True, stop=False)
                            for sl in range(8):
                                sq_ = half * 8 + sl
                                mm(pv, pv.ap, padw.ap[:, sq_ * 128:(sq_ + 1) * 128], s0b.ap[:, sl, :], [padw, s0b], start=False, stop=(sq_ == 15))
                            for sl in range(8):
                                sq_ = half * 8 + sl
                                mm(po, po_ap, padq.ap[:, sq_ * 128:(sq_ + 1) * 128], s0b.ap[:, sl, :], [padq, s0b], start=(sq_ == 0), stop=False)
                            if half == 1:
                                S.op("act", lambda e: e.mul(out=vn.ap, in_=pv.ap, mul=G(1)), reads=[pv, gs], writes=[vn])
                                mm(po, po_ap, qkT.ap, vn.ap, [qkT, vn], start=False, stop=True)
                        for half in range(2):
                            dma("sp", s0f.ap, rst_d[half * 8:(half + 1) * 8, h].rearrange("s k v -> k s v"), writes=[s0f])
                            for qd4 in range(2):
                                pb = rPB.next()
                                for s4 in range(4):
                                    sq_ = half * 8 + qd4 * 4 + s4
                                    mm(pb, pb.ap[:, s4 * 128:(s4 + 1) * 128], kpad.ap[:, sq_, :], vn.ap, [kpad, vn])
                                sl0 = qd4 * 4
                                S.op("dve", lambda e: e.tensor_tensor(out=snw.ap[:, sl0:sl0 + 4, :], in0=s0f.ap[:, sl0:sl0 + 4, :],
                                                                      in1=gls.ap[:, half * 8 + sl0:half * 8 + sl0 + 4, h:h + 1].to_broadcast([128, 4, 128]), op=ALU.mult),
                                     reads=[s0f, gls], writes=[snw])
                                S.op("dve", lambda e: e.tensor_tensor(out=snw.ap[:, sl0:sl0 + 4, :], in0=snw.ap[:, sl0:sl0 + 4, :],
                                                                      in1=pb.ap.rearrange("p (a b) -> p a b", a=4), op=ALU.add), reads=[snw, pb], writes=[snw])
                            dma("sp", rso_d[half * 8:(half + 1) * 8, h].rearrange("s k v -> k s v"), snw.ap, reads=[snw], writes=[Buf()])

            on = f128.next()
            ob = b128.next()
            finish_o(po_ap, po, on, ob, zs, j, h)
            pt2 = rPTq.next()
            S.op("pe", lambda e: e.transpose(out=pt2.ap, in_=ob.ap, identity=identb), reads=[ob, cbf], writes=[pt2])
            S.op("act", lambda e: e.activation(out=onT.ap[:, h, cols], in_=pt2.ap, func=AF.Copy), reads=[pt2], writes=[onT])

        def chain(c, h, qT, kT, ktok, vtok, zs):
            cb = chains[c]
            QB = [T(banks[c][:, q * 128:(q + 1) * 128], PF[c].b) for q in range(4)]
            F0 = T(banks[c][:, 256:384], PF[c].b)
            F1 = T(banks[c][:, 384:512], PF[c].b)
            F01 = T(banks[c][:, 256:512].rearrange("p (a b) -> p a b", a=2), PF[c].b)
            qkT, Rb, kg, kd, qd, vn, nw, ob = cb["b"]
            E, egb, dS, on = cb["f"]
            for j, (kind, pi) in zip(tiles, kinds):
                if kind == "s":
                    continue
                cols = slice(j * 128, (j + 1) * 128)
                G = lambda q: gs.ap[:, j, q, h:h + 1]
                mm(QB[0], QB[0].ap, kT.ap[:, cols], qT.ap[:, cols], [kT, qT])
                mm(QB[1], QB[1].ap, kT.ap[:, cols], kT.ap[:, cols], [kT])
                mm(F0, F0.ap, gs.ap[:, j, 0, h:h + 1].to_broadcast([128, 128]), cU, [gs, c32])
                yield
                S.op("dve", lambda e: e.scalar_tensor_tensor(out=E.ap, in0=F0.ap, scalar=G(3), in1=cNM, op0=ALU.subtract, op1=ALU.add),
                     reads=[F0, gs, c32], writes=[E])
                S.op("act", lambda e: e.activation(out=egb.ap, in_=F0.ap, func=AF.Exp), reads=[F0], writes=[egb])
                yield
                S.op("act", lambda e: e.activation(out=E.ap, in_=E.ap, func=AF.Exp), reads=[E], writes=[E])
                yield
                S.op("dve", lambda e: e.tensor_tensor(out=qkT.ap, in0=QB[0].ap, in1=E.ap, op=ALU.mult), reads=[QB[0], E], writes=[qkT])
                S.op("dve", lambda e: e.tensor_tensor(out=dS.ap, in0=E.ap, in1=cST, op=ALU.mult), reads=[E, c32], writes=[dS])
                X = cb["x"][0]
                S.op("dve", lambda e: e.scalar_tensor_tensor(out=X.ap[:, 0, :], in0=QB[1].ap, scalar=G(2), in1=dS.ap, op0=ALU.mult, op1=ALU.mult),
                     reads=[QB[1], gs, dS], writes=[X])
                S.op("dve", lambda e: e.tensor_tensor(out=qd.ap, in0=qT.ap[:, cols], in1=egb.ap, op=ALU.mult), reads=[qT, egb], writes=[qd])
                yield
                mm(F1, F1.ap, X.ap[:, 0, :], cID, [X, c32])
                Rm = cb["r"][0]
                S.op("dve", lambda e: e.tensor_tensor(out=Rm.ap, in0=X.ap[:, 0, :], in1=cID, op=ALU.add), reads=[X, c32], writes=[Rm])
                S.op("act", lambda e: e.mul(out=kg.ap, in_=ktok.ap[:, j, :], mul=G(4)), reads=[ktok, gs], writes=[kg])
                S.op("act", lambda e: e.mul(out=kd.ap, in_=ktok.ap[:, j, :], mul=G(5)), reads=[ktok, gs], writes=[kd])
                yield
                S.op("act", lambda e: e.activation(out=X.ap[:, 1, :], in_=F1.ap, func=AF.Copy), reads=[F1], writes=[X])
                yield
                for lev in range(6):
                    last = lev == 5
                    Xn = cb["x"][(lev + 1) % 2]
                    Rn = cb["r"][(lev + 1) % 2]
                    mm(F1, F1.ap, X.ap[:, 0, :], X.ap[:, 1, :], [X])
                    if not last:
                        mm(F0, F0.ap, X.ap[:, 1, :], X.ap[:, 0, :], [X])
                    yield
                    if last:
                        S.op("act", lambda e: e.activation(out=Xn.ap[:, 1, :], in_=F1.ap, func=AF.Copy), reads=[F1], writes=[Xn])
                    else:
                        S.op("act", lambda e: e.activation(out=Xn.ap, in_=F01.ap, func=AF.Copy), reads=[F01], writes=[Xn])
                    yield
                    pr = F0
                    mm(pr, pr.ap, Xn.ap[:, 1, :], Rm.ap, [Xn, Rm])
                    yield
                    S.op("dve", lambda e: e.tensor_tensor(out=Rn.ap, in0=pr.ap, in1=Rm.ap, op=ALU.add), reads=[pr, Rm], writes=[Rn])
                    yield
                    X, Rm = Xn, Rn
                S.op("act", lambda e: e.activation(out=Rb.ap, in_=Rm.ap, func=AF.Copy), reads=[Rm], writes=[Rb])
                yield
                mm(QB[2], QB[2].ap, kg.ap, Rb.ap, [kg, Rb])
                yield
                S.op("dve", lambda e: e.tensor_scalar(out=nw.ap, in0=QB[2].ap, scalar1=-1.0, scalar2=None, op0=ALU.mult), reads=[QB[2]], writes=[nw])
                yield
                mm(QB[3], QB[3].ap, Rb.ap, vtok.ap[:, j, :], [Rb, vtok], start=True, stop=False)
                mm(QB[3], QB[3].ap, nw.ap, Sbf[h].ap, [nw, Sbf[h]], start=False, stop=True)
                yield
                S.op("act", lambda e: e.mul(out=vn.ap, in_=QB[3].ap, mul=G(1)), reads=[QB[3], gs], writes=[vn])
                yield
                mm(QB[0], QB[0].ap, qd.ap, Sbf[h].ap, [qd, Sbf[h]], start=True, stop=False)
                mm(QB[0], QB[0].ap, qkT.ap, vn.ap, [qkT, vn], start=False, stop=True)
                mm(QB[1], QB[1].ap, kd.ap, vn.ap, [kd, vn])
                yield
                S.op("dve", lambda e: e.scalar_tensor_tensor(out=S32[h].ap, in0=S32[h].ap, scalar=glb.ap[:, j, h:h + 1], in1=QB[1].ap,
                                                             op0=ALU.mult, op1=ALU.add), reads=[S32[h], glb, QB[1]], writes=[S32[h]])
                st = rstd_of(QB[0].ap, QB[0], 128, 128)
                yield
                S.op("act", lambda e: e.activation(out=Sbf[h].ap, in_=S32[h].ap, func=AF.Copy), reads=[S32[h]], writes=[Sbf[h]])
                S.op("dve", lambda e: e.scalar_tensor_tensor(out=on.ap, in0=QB[0].ap, scalar=st.ap[:, 0:1], in1=onb.ap, op0=ALU.mult, op1=ALU.mult),
                     reads=[QB[0], st, onb], writes=[on])
                S.op("dve", lambda e: e.tensor_tensor(out=ob.ap, in0=on.ap, in1=zs.ap[:, j, :], op=ALU.mult), reads=[on, zs], writes=[ob])
                if pi == 15:
                    dma("sp", rp_d[h], S32[h].ap, reads=[S32[h]], writes=[Buf()])
                yield
                pt2 = rPTq.next()
                S.op("pe", lambda e: e.transpose(out=pt2.ap, in_=ob.ap, identity=identb), reads=[ob, cbf], writes=[pt2])
                yield
                S.op("act", lambda e: e.activation(out=onT.ap[:, h, cols], in_=pt2.ap, func=AF.Copy), reads=[pt2], writes=[onT])
                yield

        for grp in range(4):
            heads = [4 * grp + i for i in range(4)]
            for kk in range(2):
                kh = 2 * grp + kk
                wq = wload(kh * 128)
                wk = wload(1024 + kh * 128)
                co = proj_conv(wq, kh)
                l2n(co, qTs[kk], 128.0 ** -0.5)
                co = proj_conv(wk, 8 + kh)
                l2n(co, kTs[kk], 1.0)
                to_tok(kTs[kk], ktoks[kk])
            for c, h in enumerate(heads):
                wv = wload(2048 + h * 128)
                wz = wload(4096 + h * 128)
                co = proj_conv(wv, 16 + h)
                vT = vTring.next()
                S.op("act", lambda e: e.activation(out=vT.ap, in_=co.ap, func=AF.Copy), reads=[co], writes=[vT])
                to_tok(vT, vtoks[c])
                zproj(wz, zss[c])
                if sbi == 0:
                    S.op("dve", lambda e: e.memset(S32[h].ap, 0.0), writes=[S32[h]])
                    S.op("dve", lambda e: e.memset(Sbf[h].ap, 0.0), writes=[Sbf[h]])
            if has_s:
                S.op("dve", lambda e: e.memset(padw.ap, 0.0), writes=[padw])
                S.op("dve", lambda e: e.memset(padq.ap, 0.0), writes=[padq])
                for c, h in enumerate(heads):
                    sample_state(h, qTs[c // 2], kTs[c // 2], ktoks[c // 2], vtoks[c], zss[c])
            allg = [chain(c, h, qTs[c // 2], kTs[c // 2], ktoks[c // 2], vtoks[c], zss[c]) for c, h in enumerate(heads)]
            gw = 4
            for g0 in range(0, 4, gw):
              gens = allg[g0:g0 + gw]
              while gens:
                alive = []
                for g_ in gens:
                    try:
                        next(g_)
                        alive.append(g_)
                    except StopIteration:
                        pass
                gens = alive

        if dbg_d is not None and sbi == 0:
            dma("sp", dbg_d, ybig, reads=[onT], writes=[Buf()])
        areset()
        gpost = aalloc([128, D], F32)
        dma("sp", gpost.ap, nmo_d[1:2, :].partition_broadcast(128), writes=[gpost])
        wo_all = [aalloc([128, 4, D], BF16) for _ in range(4)]
        for hg in range(4):
            dma("pool", wo_all[hg].ap, gwo_d[hg * 512:(hg + 1) * 512, :].rearrange("(c p) n -> p c n", p=128), writes=[wo_all[hg]])
        for j in tiles:
            pys = [rPF.next(), rPF.next()]
            for hg in range(4):
                wo = wo_all[hg]
                for hh in range(4):
                    h = hg * 4 + hh
                    for dh in range(2):
                        mm(pys[dh], pys[dh].ap, onT.ap[:, h, j * 128:(j + 1) * 128], wo.ap[:, hh, dh * 512:(dh + 1) * 512], [onT, wo],
                           start=(h == 0), stop=(h == 15))
            m32 = rTmp.next()
            for dh in range(2):
                S.op("act", lambda e, dh=dh: e.activation(out=m32.ap[:, dh * 512:(dh + 1) * 512], in_=pys[dh].ap, func=AF.Copy), reads=[pys[dh]], writes=[m32])
            post_norm_add(j, m32.ap, m32, gpost)

    for sbi in range(2):
        if sbi == 0:
            kinds = [("s", None)] + [("p", i) for i in range(8)]
            blocks = [(0, 128), (128, 512), (640, 512)]
        else:
            kinds = [("p", 8 + i) for i in range(8)]
            blocks = [(0, 512), (512, 512)]
        tiles = list(range(len(kinds)))
        for j, (kind, pi) in zip(tiles, kinds):
            src = xs_d if kind == "s" else xp_d[pi * 128:(pi + 1) * 128, :]
            dma("sp", x_t[j].ap, src, writes=[x_t[j]])
        pool_phase(sbi, tiles, kinds)
        if stop_after not in ("pool", "nod2d"):
            ffn(0, tiles, blocks)
        if stop_after not in ("pool", "nod2d", "l0"):
            gdn_phase(sbi, tiles, kinds, blocks)
            if stop_after != "gdn":
                ffn(1, tiles, blocks)
        for j, (kind, pi) in zip(tiles, kinds):
            dst = ys_d if kind == "s" else yp_d[pi * 128:(pi + 1) * 128, :]
            dma("sp", dst, x_t[j].ap, reads=[x_t[j]], writes=[Buf()])

    with nc.allow_low_precision("bf16 matmul operands, fp32 accumulation"):
        S.emit()
    return nc


_NC_CACHE = {}


def make_in_maps(inp):
    c32, pm = _consts()
    f = lambda a: np.ascontiguousarray(np.asarray(a, dtype=np.float32))
    shared = {
        "nmp": f(inp["norm_mix_pre"]), "nmo": f(inp["norm_mix_post"]),
        "nfp": f(inp["norm_ffn_pre"]), "nfo": f(inp["norm_ffn_post"]),
        "pw": f(inp["pool_w"][0]), "psc": f(inp["pool_scale"]),
        "gwi": f(np.asarray(inp["gdn_w_in"][0])[:, :6144].reshape(8, 128, 48, 128).transpose(2, 1, 0, 3).reshape(48, 128, 1024)),
        "gwba": f(np.asarray(inp["gdn_w_in"][0])[:, 6144:6176]), "gcw": f(np.asarray(inp["gdn_conv_w"][0]).T),
        "galog": f(inp["gdn_a_log"]), "gdtb": f(inp["gdn_dt_bias"]), "gon": f(inp["gdn_o_norm"]),
        "gwo": f(inp["gdn_w_out"][0]), "fwi": f(inp["ffn_w_in"]), "fwo": f(inp["ffn_w_out"]),
        "c32": c32, "pm": pm,
    }
    maps = []
    for c in range(NCORE):
        sl = slice(16 * c, 16 * (c + 1))
        m = dict(shared)
        m["xp"] = f(inp["x_prompt"][c])
        m["xs"] = f(np.asarray(inp["x_sample"][sl]).reshape(128, D))
        m["pst"] = f(np.asarray(inp["state_pool"][0, sl]).reshape(240, D))
        m["cst"] = f(np.asarray(inp["state_gdn_conv"][0, sl]).transpose(2, 0, 1))
        m["rst"] = f(inp["state_gdn_rec"][0, sl])
        maps.append(m)
    return maps


def kernel(**inp):
    if "nc" not in _NC_CACHE:
        _NC_CACHE["nc"] = build()
    nc = _NC_CACHE["nc"]
    maps = make_in_maps(inp)
    res = run_bass_kernel_spmd(nc, maps, core_ids=list(range(NCORE)))
    R = res.results
    y_p = np.stack([R[c]["yp"] for c in range(NCORE)]).reshape(8, 2048, D)
    y_s = np.concatenate([R[c]["ys"].reshape(16, 8, D) for c in range(NCORE)])
    pool_p = np.stack([R[c]["pp"] for c in range(NCORE)])[None]
    pool_s = np.concatenate([R[c]["pso"] for c in range(NCORE)])[None]
    conv_p = np.stack([R[c]["cp"].T for c in range(NCORE)])[None]
    conv_s = np.concatenate([R[c]["cso"].transpose(1, 2, 0) for c in range(NCORE)])[None]
    rec_p = np.stack([R[c]["rp"] for c in range(NCORE)])[None]
    rec_s = np.concatenate([R[c]["rso"] for c in range(NCORE)])[None]
    return (y_p, y_s, pool_p, pool_s, conv_p, conv_s, rec_p, rec_s)
```

```python
import numpy as np
import concourse.bass as bass
import concourse.mybir as mybir
from concourse.bass_utils import run_bass_kernel_spmd

F32 = mybir.dt.float32
BF16 = mybir.dt.bfloat16
AF = mybir.ActivationFunctionType
ALU = mybir.AluOpType

D = 1024
DFF = 2816
NCORE = 8
NEG = -30000.0


class Buf:
    __slots__ = ("w", "r", "excl")

    def __init__(self):
        self.w = None
        self.r = []
        self.excl = False


class T:
    def __init__(self, ap, bufs=None):
        self.ap = ap
        self.b = bufs if bufs is not None else [Buf()]


class Ring:
    def __init__(self, items):
        self.items = items
        self.i = 0

    def next(self):
        t = self.items[self.i % len(self.items)]
        self.i += 1
        return t


def _bufs(lst):
    out = []
    for t in lst:
        if isinstance(t, Buf):
            out.append(t)
        else:
            out.extend(t.b)
    return out


class _Rec:
    def __getattr__(self, name):
        def f(*a, **k):
            self.__dict__["call"] = (name, a, k)
            return self
        return f


class Sched:
    ENG = ["pe", "act", "dve", "pool", "sp"]
    NDSEM = 12

    def __init__(self, nc):
        self.nc = nc
        self.ops = []
        self.by_eng = {e: [] for e in self.ENG}
        self.ndma = {e: 0 for e in self.ENG}

    def op(self, eng, fn, reads=(), writes=(), dma=False):
        import os
        if len(self.ops) >= int(os.environ.get("MAXOPS", "100000000")):
            return None
        rec_ = _Rec()
        fn(rec_)
        call = rec_.call
        fn = lambda e, call=call: getattr(e, call[0])(*call[1], **call[2])
        reads = _bufs(reads)
        writes = _bufs(writes)
        writes = writes + [b for b in reads if b.excl and b not in writes]
        reads = [b for b in reads if not b.excl]
        oid = len(self.ops)
        deps = {}
        for b in reads:
            if b.w is not None:
                deps[(b.w, "raw")] = 1
        for b in writes:
            if b.w is not None:
                deps[(b.w, "waw")] = 1
            for r in b.r:
                deps[(r, "war")] = 1
        for b in reads:
            b.r.append(oid)
        for b in writes:
            b.w = oid
            b.r = []
        real = {}
        for (p, kind) in deps:
            if p == oid:
                continue
            po = self.ops[p]
            if not po[3] and po[0] == eng and eng == "pe" and not dma:
                continue
            real[p] = 1
        rec = [eng, fn, list(real.keys()), dma, False, None, None]
        if dma:
            rec[5] = self.ndma[eng]
            self.ndma[eng] += 1
        self.ops.append(rec)
        self.by_eng[eng].append(oid)
        for p in real:
            self.ops[p][4] = True
        return oid

    def emit(self):
        nc = self.nc
        engsem = {e: nc.alloc_semaphore(name=f"es_{e}") for e in self.ENG}
        dsem = {e: [nc.alloc_semaphore(name=f"ds_{e}{i}") for i in range(self.NDSEM)]
                for e in self.ENG if self.ndma[e] > 0}
        for e in self.ENG:
            c = 0
            for oid in self.by_eng[e]:
                o = self.ops[oid]
                if not o[3] and o[4]:
                    c += 1
                    o[6] = c
        K = self.NDSEM
        ops = self.ops
        allsems = list(engsem.values()) + [x for v in dsem.values() for x in v]
        for sm in allsems:
            nc.gpsimd.sem_clear(sm)
        nc.all_engine_barrier()

        def run(e, eng):
            waited = {}

            def wait(sem, val):
                key = id(sem)
                if waited.get(key, 0) >= val:
                    return
                waited[key] = val
                eng.wait_ge(sem, val)

            for oid in self.by_eng[e]:
                o = ops[oid]
                for p in o[2]:
                    po = ops[p]
                    if po[3]:
                        wait(dsem[po[0]][po[5] % K], 16 * (po[5] // K + 1))
                    else:
                        wait(engsem[po[0]], po[6])
                if o[3]:
                    j = o[5]
                    if j >= K:
                        wait(dsem[e][j % K], 16 * (j // K))
                inst = o[1](eng)
                if o[3]:
                    inst.then_inc(dsem[e][o[5] % K], 16)
                elif o[4]:
                    inst.then_inc(engsem[e], 1)
            if e == "sp":
                for q in dsem:
                    n = self.ndma[q]
                    for i in range(min(K, n)):
                        wait(dsem[q][i], 16 * ((n - 1 - i) // K + 1))

        with nc.Block() as block:
            @block.tensor
            def _(eng):
                run("pe", eng)

            @block.scalar
            def _(eng):
                run("act", eng)

            @block.vector
            def _(eng):
                run("dve", eng)

            @block.gpsimd
            def _(eng):
                run("pool", eng)

            @block.sync
            def _(eng):
                run("sp", eng)

        nc.all_engine_barrier()
        nc.clear_and_free_semaphores(allsems)
        nc.all_engine_barrier()


def _consts():
    idx = np.arange(128)
    j = idx[:, None]
    i = idx[None, :]
    same = (j // 8) == (i // 8)
    U = (j <= i).astype(np.float32)
    Ub = ((j <= i) & same).astype(np.float32)
    NM = np.where(i >= j, 0.0, NEG).astype(np.float32)
    NMb = np.where((i >= j) & same, 0.0, NEG).astype(np.float32)
    ST = (i > j).astype(np.float32)
    STb = ((i > j) & same).astype(np.float32)
    BO = same.astype(np.float32)
    ONES = np.ones((128, 128), np.float32)
    ID = np.eye(128, dtype=np.float32)
    BM = ((idx[:, None] // 8) == np.arange(16)[None, :]).astype(np.float32)
    c32 = np.concatenate([U, Ub, NM, NMb, ST, STb, BO, ONES, ID, BM], axis=1)
    wins = (2, 4, 8, 16)
    pm = np.zeros((128, 6, 4, 128), np.float32)
    s = idx[:, None]
    t = idx[None, :]
    for g, w in enumerate(wins):
        pm[:, 0, g] = ((s > t - w) & (s <= t)) / w - (s == t)
        cnt = np.minimum(w, t + 1)
        pm[:, 1, g] = ((s > t - w) & (s <= t)) / cnt - (s == t)
        pm[:, 2, g] = ((s - 128) > (t - w)) / w
        sq, sp_ = s // 8, s % 8
        tq, tp = t // 8, t % 8
        pm[:, 3, g] = ((sq == tq) & (sp_ > tp - w) & (sp_ <= tp)) / w - (s == t)
        r = np.arange(120)[:, None]
        rq, rb = r // 15, r % 15
        for half in range(2):
            pm[:120, 4 + half, g] = ((rq + 8 * half == tq) & ((rb - 15) > (tp - w))) / w
    return c32, pm.reshape(128, 6 * 512)


def build(stop_after=None):
    nc = bass.Bass("TRN2", target_bir_lowering=False)

    def din(name, shape):
        return nc.dram_tensor(name, list(shape), F32, kind="ExternalInput").ap()

    def dout(name, shape):
        return nc.dram_tensor(name, list(shape), F32, kind="ExternalOutput").ap()

    xp_d = din("xp", [2048, D])
    xs_d = din("xs", [128, D])
    pst_d = din("pst", [240, D])
    cst_d = din("cst", [4096, 16, 3])
    rst_d = din("rst", [16, 16, 128, 128])
    nmp_d = din("nmp", [2, D])
    nmo_d = din("nmo", [2, D])
    nfp_d = din("nfp", [2, D])
    nfo_d = din("nfo", [2, D])
    pw_d = din("pw", [4, 256, 256])
    psc_d = din("psc", [1, D])
    gwi_d = din("gwi", [48, 128, 8 * 128])
    gwba_d = din("gwba", [D, 32])
    gcw_d = din("gcw", [4096, 4])
    galog_d = din("galog", [1, 16])
    gdtb_d = din("gdtb", [1, 16])
    gon_d = din("gon", [1, 128])
    gwo_d = din("gwo", [2048, D])
    fwi_d = din("fwi", [2, D, 2 * DFF])
    fwo_d = din("fwo", [2, DFF, D])
    c32_d = din("c32", [128, 9 * 128 + 16])
    pm_d = din("pm", [128, 6 * 512])

    yp_d = dout("yp", [2048, D])
    ys_d = dout("ys", [128, D])
    pp_d = dout("pp", [15, D])
    pso_d = dout("pso", [16, 15, D])
    cp_d = dout("cp", [4096, 3])
    cso_d = dout("cso", [4096, 16, 3])
    rp_d = dout("rp", [16, 128, 128])
    rso_d = dout("rso", [16, 16, 128, 128])
    dbg_d = dout("dbg", [128, 9 * D]) if stop_after == "gdn" else None
    dbg2_d = dout("dbg2", [128, 9 * D]) if stop_after == "gdn" else None
    dbg3_d = dout("dbg3", [16, 128, 128]) if stop_after == "gdn" else None
    dbgn = [0]

    def dump(t, ap):
        if dbg3_d is None or dbgn[0] >= 16:
            return
        dma("pool", dbg3_d[dbgn[0]][:, 0:ap.shape[-1]] if len(ap.shape) == 2 else dbg3_d[dbgn[0]], ap, reads=[t], writes=[Buf()])
        dbgn[0] += 1

    S = Sched(nc)
    cnt = [0]

    def sb(shape, dt, name=None):
        cnt[0] += 1
        return nc.alloc_sbuf_tensor(f"s_{name}" if name else f"sb{cnt[0]}", list(shape), dt).ap()

    def tsb(shape, dt, name=None):
        return T(sb(shape, dt, name))

    dram_out = Buf()

    def dma(q, out, in_, reads=(), writes=()):
        S.op(q, lambda e: e.dma_start(out=out, in_=in_), reads=reads, writes=writes, dma=True)

    ARENA = 82 * 1024
    arena = sb([128, ARENA // 4], F32, "arena")
    ablk = [Buf() for _ in range(ARENA // 512)]
    apos = [0]

    def areset():
        apos[0] = 0

    def aalloc(shape, dt):
        esz = 4 if dt == F32 else 2
        n = esz
        for d_ in shape[1:]:
            n *= d_
        off = apos[0]
        nb = (n + 511) // 512 * 512
        assert off + nb <= ARENA, ("arena overflow", off, nb)
        apos[0] = off + nb
        v = arena[:, off // 4:(off + nb) // 4]
        if dt != F32:
            v = v.bitcast(dt)
        v = v[:, :n // esz]
        if len(shape) == 3:
            v = v.rearrange("p (a b) -> p a b", a=shape[1])
        elif len(shape) == 4:
            v = v.rearrange("p (a b c) -> p a b c", a=shape[1], b=shape[2])
        if shape[0] < 128:
            v = v[:shape[0]]
        return T(v, ablk[off // 512:(off + nb) // 512])

    c32 = tsb([128, 9 * 128 + 16], F32, "c32")
    dma("sp", c32.ap, c32_d, writes=[c32])
    cU, cUb, cNM, cNMb, cST, cSTb, cBO, cONES, cID = [c32.ap[:, k * 128:(k + 1) * 128] for k in range(9)]
    cBM = c32.ap[:, 9 * 128:9 * 128 + 16]
    cbf = tsb([128, 2, 128], BF16, "cbf")
    S.op("dve", lambda e: e.tensor_copy(out=cbf.ap[:, 0, :], in_=cID), reads=[c32], writes=[cbf])
    S.op("dve", lambda e: e.tensor_copy(out=cbf.ap[:, 1, :], in_=cONES), reads=[c32], writes=[cbf])
    identb = cbf.ap[:, 0, :]
    onesb = cbf.ap[:, 1, :]

    banks = [nc.alloc_psum_tensor(f"pb{k}", [128, 512], F32).ap() for k in range(6)]
    PF = [T(banks[k]) for k in range(6)]
    PH = [T(banks[3 + k // 2][:, (k % 2) * 256:(k % 2) * 256 + 256], PF[3 + k // 2].b) for k in range(6)]
    ptr_ap = nc.alloc_psum_tensor("ptr", [128, 8, 128], BF16).ap()
    ptrB = [Buf()]
    PTq = [T(ptr_ap[:, k, :], ptrB) for k in range(8)]
    PTh = [T(ptr_ap[:, 4 * k:4 * k + 4, :], ptrB) for k in range(2)]
    pm_ap = nc.alloc_psum_tensor("pmisc", [128, 512], F32).ap()
    pmB = [Buf()]
    PM = [T(pm_ap[:, k * 128:(k + 1) * 128], pmB) for k in range(4)]
    for t_ in PF:
        t_.b[0].excl = True
    ptrB[0].excl = True
    pmB[0].excl = True
    rPF = Ring(PF)
    rPB = Ring(PF[:3])
    rPH = Ring(PH)
    rPTq = Ring(PTq)
    rPTh = Ring(PTh)
    rPM = Ring(PM)

    NTM = 9
    xbig = sb([128, NTM, D], F32, "x")
    x_t = [T(xbig[:, j, :]) for j in range(NTM)]
    hTbig = sb([128, 8, NTM * 128], BF16, "hT")
    hT_t = [T(hTbig[:, :, j * 128:(j + 1) * 128]) for j in range(NTM)]
    ybig = sb([128, NTM * D], F32, "yacc")
    y_t = [T(ybig[:, j * D:(j + 1) * D]) for j in range(NTM)]
    junk = tsb([128, D], BF16, "junk")
    rSmall = Ring([tsb([128, 2], F32) for _ in range(6)])
    rHb = Ring([tsb([128, D], BF16) for _ in range(2)])
    rHbPool = Ring([tsb([128, D], BF16) for _ in range(2)])
    rTmp = Ring([tsb([128, D], F32) for _ in range(2)])

    def load_gain(src_row):
        g = aalloc([128, D], F32)
        dma("sp", g.ap, src_row.partition_broadcast(128), writes=[g])
        return g

    def rstd_of(src_ap, srcT, rows, n):
        st = rSmall.next()
        sc = float(n) ** -0.5
        S.op("act", lambda e: e.activation(out=junk.ap[:rows, :n], in_=src_ap, func=AF.Square, scale=sc,
                                           accum_out=st.ap[:rows, 0:1]), reads=[srcT], writes=[junk, st])
        S.op("act", lambda e: e.activation(out=st.ap[:rows, 1:2], in_=st.ap[:rows, 0:1], func=AF.Ln,
                                           bias=1e-6, scale=1.0), reads=[st], writes=[st])
        S.op("act", lambda e: e.activation(out=st.ap[:rows, 0:1], in_=st.ap[:rows, 1:2], func=AF.Exp,
                                           scale=-0.5), reads=[st], writes=[st])
        return st

    def pre_norm(j, gain, ring, out32=None):
        st = rstd_of(x_t[j].ap, x_t[j], 128, D)
        hb = ring.next()
        if out32 is not None:
            S.op("dve", lambda e: e.scalar_tensor_tensor(out=out32.ap, in0=x_t[j].ap, scalar=st.ap[:, 0:1],
                                                         in1=gain.ap, op0=ALU.mult, op1=ALU.mult),
                 reads=[x_t[j], st, gain], writes=[out32])
            S.op("act", lambda e: e.activation(out=hb.ap, in_=out32.ap, func=AF.Copy), reads=[out32], writes=[hb])
        else:
            S.op("dve", lambda e: e.scalar_tensor_tensor(out=hb.ap, in0=x_t[j].ap, scalar=st.ap[:, 0:1],
                                                         in1=gain.ap, op0=ALU.mult, op1=ALU.mult),
                 reads=[x_t[j], st, gain], writes=[hb])
        return hb

    def to_hT(j, hb):
        for half in range(2):
            pt = rPTh.next()
            for k in range(4):
                kk = half * 4 + k
                S.op("pe", lambda e, kk=kk, k=k, pt=pt: e.transpose(out=pt.ap[:, k, :], in_=hb.ap[:, kk * 128:(kk + 1) * 128],
                                                                     identity=identb), reads=[hb, cbf], writes=[pt])
            S.op("act", lambda e, pt=pt, half=half: e.activation(out=hT_t[j].ap[:, half * 4:half * 4 + 4, :], in_=pt.ap, func=AF.Copy),
                 reads=[pt], writes=[hT_t[j]])

    def post_norm_add(j, src_ap, srcT, gain):
        st = rstd_of(src_ap, srcT, 128, D)
        tmp = rTmp.next()
        S.op("dve", lambda e: e.scalar_tensor_tensor(out=tmp.ap, in0=src_ap, scalar=st.ap[:, 0:1], in1=gain.ap,
                                                     op0=ALU.mult, op1=ALU.mult), reads=[srcT, st, gain], writes=[tmp])
        S.op("dve", lambda e: e.tensor_tensor(out=x_t[j].ap, in0=x_t[j].ap, in1=tmp.ap, op=ALU.add),
             reads=[x_t[j], tmp], writes=[x_t[j]])

    def ffn(layer, tiles, blocks):
        areset()
        rWg = Ring([aalloc([128, 8, 512], BF16) for _ in range(2)])
        rWu = Ring([aalloc([128, 8, 512], BF16) for _ in range(2)])
        rWo = Ring([aalloc([128, 4, D], BF16) for _ in range(2)])
        rAct = Ring([aalloc([128, 4, 512], BF16) for _ in range(2)])
        rSg = Ring([aalloc([128, 512], F32) for _ in range(2)])
        gpre = load_gain(nfp_d[layer:layer + 1, :])
        gpost = load_gain(nfo_d[layer:layer + 1, :])
        groups = [(f0, min(4, 22 - f0)) for f0 in range(0, 22, 4)]

        def load_group(gi):
            f0, nf = groups[gi]
            wg, wu, wo = rWg.next(), rWu.next(), rWo.next()
            dma("pool", wg.ap[:, :, :nf * 128],
                fwi_d[layer, :, f0 * 128:(f0 + nf) * 128].rearrange("(k p) n -> p k n", p=128), writes=[wg])
            dma("pool", wu.ap[:, :, :nf * 128],
                fwi_d[layer, :, DFF + f0 * 128:DFF + (f0 + nf) * 128].rearrange("(k p) n -> p k n", p=128), writes=[wu])
            dma("pool", wo.ap[:, :nf, :],
                fwo_d[layer, f0 * 128:(f0 + nf) * 128, :].rearrange("(c p) n -> p c n", p=128), writes=[wo])
            return wg, wu, wo

        nxt = load_group(0)
        for j in tiles:
            hb = pre_norm(j, gpre, rHb)
            to_hT(j, hb)
        for gi, (f0, nf) in enumerate(groups):
            wg, wu, wo = nxt
            if gi + 1 < len(groups):
                nxt = load_group(gi + 1)
            for (c0, wd) in blocks:
                tl = list(range(c0 // 128, (c0 + wd) // 128))
                hts = [hT_t[j] for j in tl]
                actT = rAct.next()
                for fc in range(nf):
                    pg = rPF.next()
                    for k in range(8):
                        S.op("pe", lambda e, pg=pg, k=k, fc=fc: e.matmul(pg.ap[:, :wd], lhsT=wg.ap[:, k, fc * 128:(fc + 1) * 128],
                                                                         rhs=hTbig[:, k, c0:c0 + wd], start=(k == 0), stop=(k == 7)),
                             reads=[wg] + hts, writes=[pg])
                    pu = rPF.next()
                    for k in range(8):
                        S.op("pe", lambda e, pu=pu, k=k, fc=fc: e.matmul(pu.ap[:, :wd], lhsT=wu.ap[:, k, fc * 128:(fc + 1) * 128],
                                                                         rhs=hTbig[:, k, c0:c0 + wd], start=(k == 0), stop=(k == 7)),
                             reads=[wu] + hts, writes=[pu])
                    sg = rSg.next()
                    S.op("act", lambda e, pg=pg, sg=sg: e.activation(out=sg.ap[:, :wd], in_=pg.ap[:, :wd], func=AF.Silu),
                         reads=[pg], writes=[sg])
                    S.op("dve", lambda e, pu=pu, sg=sg, fc=fc: e.tensor_tensor(out=actT.ap[:, fc, :wd], in0=pu.ap[:, :wd], in1=sg.ap[:, :wd],
                                                                               op=ALU.mult), reads=[pu, sg], writes=[actT])
                for ti, j in enumerate(tl):
                    for dh in range(2):
                        py = rPF.next()
                        for fc in range(nf):
                            S.op("pe", lambda e, py=py, fc=fc, ti=ti, dh=dh: e.matmul(py.ap, lhsT=actT.ap[:, fc, ti * 128:(ti + 1) * 128],
                                                                                      rhs=wo.ap[:, fc, dh * 512:(dh + 1) * 512],
                                                                                      start=(fc == 0), stop=(fc == nf - 1)),
                                 reads=[actT, wo], writes=[py])
                        ysl = y_t[j].ap[:, dh * 512:(dh + 1) * 512]
                        if gi == 0:
                            S.op("act", lambda e, py=py, ysl=ysl: e.activation(out=ysl, in_=py.ap, func=AF.Copy), reads=[py], writes=[y_t[j]])
                        else:
                            S.op("dve", lambda e, py=py, ysl=ysl: e.tensor_tensor(out=ysl, in0=ysl, in1=py.ap, op=ALU.add),
                                 reads=[py, y_t[j]], writes=[y_t[j]])
        for j in tiles:
            post_norm_add(j, y_t[j].ap, y_t[j], gpost)

    pool_prev = [None]

    def pool_phase(sbi, tiles, kinds):
        areset()
        pmb = aalloc([128, 6, 4, 128], BF16)
        dma("pool", pmb.ap, pm_d.rearrange("p (a g t) -> p a g t", a=6, g=4), writes=[pmb])
        pwb = aalloc([128, 4, 2, 256], BF16)
        pstb = aalloc([128, 2, D], BF16)
        dT = Ring([aalloc([128, 8, 128], BF16) for _ in range(2)])
        gpre = load_gain(nmp_d[0:1, :])
        gpost = load_gain(nmo_d[0:1, :])
        gsc = load_gain(psc_d[0:1, :])
        dma("pool", pwb.ap, pw_d.rearrange("g (c p) e -> p g c e", p=128), writes=[pwb])
        if sbi == 0:
            S.op("dve", lambda e: e.memset(pstb.ap, 0.0), writes=[pstb])
            dma("pool", pstb.ap[:120, :, :], pst_d.rearrange("(a r) d -> r a d", a=2), writes=[pstb])
            if stop_after != "nod2d":
                dma("sp", pso_d[:, 0:7, :], pst_d.rearrange("(s b) d -> s b d", b=15)[:, 8:15, :], writes=[Buf()])
        for j, (kind, pi) in zip(tiles, kinds):
            h32 = y_t[j]
            hb = pre_norm(j, gpre, rHbPool, out32=h32)
            if kind == "s":
                for sq in range(16):
                    dma("sp", pso_d[sq, 7:15, :], h32.ap[sq * 8:(sq + 1) * 8, :], reads=[h32], writes=[Buf()])
            elif pi == 15:
                dma("sp", pp_d, h32.ap[113:128, :], reads=[h32], writes=[Buf()])
            d = dT.next()
            phs = [rPH.next() for _ in range(4)]
            for cc in range(8):
                g = cc // 2
                ph = phs[cc // 2]
                o_ap = ph.ap[:, (cc % 2) * 128:(cc % 2) * 128 + 128]
                lhs_cur = hb.ap[:, cc * 128:(cc + 1) * 128]
                if kind == "s":
                    S.op("pe", lambda e, o_ap=o_ap, lhs_cur=lhs_cur, g=g: e.matmul(o_ap, lhsT=lhs_cur, rhs=pmb.ap[:, 3, g, :], start=True, stop=False),
                         reads=[hb, pmb], writes=[ph])
                    for half in range(2):
                        S.op("pe", lambda e, o_ap=o_ap, cc=cc, g=g, half=half: e.matmul(o_ap, lhsT=pstb.ap[:, half, cc * 128:(cc + 1) * 128],
                                                                                        rhs=pmb.ap[:, 4 + half, g, :], start=False, stop=(half == 1)),
                             reads=[pstb, pmb], writes=[ph])
                elif pi == 0:
                    S.op("pe", lambda e, o_ap=o_ap, lhs_cur=lhs_cur, g=g: e.matmul(o_ap, lhsT=lhs_cur, rhs=pmb.ap[:, 1, g, :], start=True, stop=True),
                         reads=[hb, pmb], writes=[ph])
                else:
                    prev = pool_prev[0]
                    S.op("pe", lambda e, o_ap=o_ap, lhs_cur=lhs_cur, g=g: e.matmul(o_ap, lhsT=lhs_cur, rhs=pmb.ap[:, 0, g, :], start=True, stop=False),
                         reads=[hb, pmb], writes=[ph])
                    S.op("pe", lambda e, o_ap=o_ap, cc=cc, g=g, prev=prev: e.matmul(o_ap, lhsT=prev.ap[:, cc * 128:(cc + 1) * 128], rhs=pmb.ap[:, 2, g, :],
                                                                                    start=False, stop=True), reads=[prev, pmb], writes=[ph])
            for q in range(4):
                S.op("act", lambda e, q=q: e.activation(out=d.ap[:, 2 * q:2 * q + 2, :], in_=phs[q].ap.rearrange("p (a t) -> p a t", a=2), func=AF.Copy),
                     reads=[phs[q]], writes=[d])
            if kind == "p":
                pool_prev[0] = hb
            pys = [rPB.next(), rPB.next()]
            for g in range(4):
                py = pys[g // 2]
                o_ap = py.ap[:, (g % 2) * 256:(g % 2) * 256 + 256]
                for c in range(2):
                    S.op("pe", lambda e, o_ap=o_ap, g=g, c=c: e.matmul(o_ap, lhsT=d.ap[:, 2 * g + c, :], rhs=pwb.ap[:, g, c, :],
                                                                       start=(c == 0), stop=(c == 1)), reads=[d, pwb], writes=[py])
            m32 = rTmp.next()
            for hh in range(2):
                S.op("dve", lambda e, hh=hh: e.tensor_tensor(out=m32.ap[:, hh * 512:(hh + 1) * 512], in0=pys[hh].ap,
                                                             in1=gsc.ap[:, hh * 512:(hh + 1) * 512], op=ALU.mult),
                     reads=[pys[hh], gsc], writes=[m32])
            post_norm_add(j, m32.ap, m32, gpost)


    S32 = [tsb([128, 128], F32, f"S32_{h}") for h in range(16)]
    Sbf = [tsb([128, 128], BF16, f"Sbf_{h}") for h in range(16)]
    ctail = tsb([128, 32, 3], F32, "ctail")

    def gdn_phase(sbi, tiles, kinds, blocks):
        NTl = len(tiles)
        Ncol = NTl * 128
        has_s = kinds[0][0] == "s"
        p0 = 128 if has_s else 0
        Np = Ncol - p0
        areset()
        if dbg2_d is not None and sbi == 0:
            dma("sp", dbg2_d.rearrange("p (a b) -> p a b", a=9), xbig, reads=x_t, writes=[Buf()])
        gpre = load_gain(nmp_d[1:2, :])
        for j in tiles:
            hb = pre_norm(j, gpre, rHb)
            to_hT(j, hb)
        areset()
        gs = aalloc([128, NTl, 6, 16], F32)
        glb = aalloc([128, NTl, 16], F32)
        gls = aalloc([128, 16, 16], F32)
        gblk = aalloc([128, 16, 16], F32)
        wba = aalloc([128, 8, 128], BF16)
        dtb = aalloc([128, 16], F32)
        nea = aalloc([128, 16], F32)
        cw = aalloc([128, 32, 4], F32)
        onb = aalloc([128, 128], F32)
        t16 = Ring([aalloc([128, 16], F32) for _ in range(3)])
        S.op("dve", lambda e: e.memset(wba.ap, 0.0), writes=[wba])
        S.op("dve", lambda e: e.memset(gs.ap, 0.0), writes=[gs])
        dma("pool", wba.ap[:, :, 0:32], gwba_d.rearrange("(k p) n -> p k n", p=128), writes=[wba])
        dma("sp", dtb.ap, gdtb_d.partition_broadcast(128), writes=[dtb])
        dma("sp", nea.ap, galog_d.partition_broadcast(128), writes=[nea])
        dma("sp", cw.ap, gcw_d.rearrange("(c p) t -> p c t", p=128), writes=[cw])
        dma("sp", onb.ap, gon_d.partition_broadcast(128), writes=[onb])
        S.op("act", lambda e: e.activation(out=nea.ap, in_=nea.ap, func=AF.Exp), reads=[nea], writes=[nea])
        S.op("dve", lambda e: e.tensor_scalar(out=nea.ap, in0=nea.ap, scalar1=-1.0, scalar2=None, op0=ALU.mult), reads=[nea], writes=[nea])
        for j, (kind, pi) in zip(tiles, kinds):
            cols = slice(j * 128, (j + 1) * 128)
            pm = rPM.next()
            for k in range(8):
                S.op("pe", lambda e, k=k: e.matmul(pm.ap, lhsT=hTbig[:, k, cols], rhs=wba.ap[:, k, :], start=(k == 0), stop=(k == 7)),
                     reads=[hT_t[j], wba], writes=[pm])
            G = lambda q: gs.ap[:, j, q, :]
            ta, tb = t16.next(), t16.next()
            S.op("act", lambda e: e.activation(out=ta.ap, in_=pm.ap[:, 0:16], func=AF.Exp, scale=-1.0), reads=[pm], writes=[ta])
            S.op("dve", lambda e: e.tensor_tensor(out=tb.ap, in0=pm.ap[:, 16:32], in1=dtb.ap, op=ALU.add), reads=[pm, dtb], writes=[tb])
            S.op("dve", lambda e: e.tensor_scalar(out=ta.ap, in0=ta.ap, scalar1=1.0, scalar2=None, op0=ALU.add), reads=[ta], writes=[ta])
            S.op("dve", lambda e: e.reciprocal(out=G(1), in_=ta.ap), reads=[ta], writes=[gs])
            S.op("dve", lambda e: e.tensor_scalar(out=G(2), in0=G(1), scalar1=-1.0, scalar2=None, op0=ALU.mult), reads=[gs], writes=[gs])
            S.op("act", lambda e: e.activation(out=tb.ap, in_=tb.ap, func=AF.Exp), reads=[tb], writes=[tb])
            S.op("act", lambda e: e.activation(out=tb.ap, in_=tb.ap, func=AF.Ln, bias=1.0, scale=1.0), reads=[tb], writes=[tb])
            S.op("dve", lambda e: e.tensor_tensor(out=G(0), in0=tb.ap, in1=nea.ap, op=ALU.mult), reads=[tb, nea], writes=[gs])
            pm2 = rPM.next()
            pm3 = rPM.next()
            grow = gs.ap[:, j, :, :].rearrange("p a b -> p (a b)")
            S.op("pe", lambda e: e.matmul(pm2.ap[:, 0:96], lhsT=(cUb if kind == "s" else cU), rhs=grow, start=True, stop=True), reads=[c32, gs], writes=[pm2])
            S.op("pe", lambda e: e.matmul(pm3.ap[:, 0:96], lhsT=(cBO if kind == "s" else cONES), rhs=grow, start=True, stop=True), reads=[c32, gs], writes=[pm3])
            S.op("dve", lambda e: e.tensor_copy(out=G(3), in_=pm2.ap[:, 0:16]), reads=[pm2], writes=[gs])
            S.op("act", lambda e: e.activation(out=G(4), in_=pm2.ap[:, 0:16], func=AF.Exp), reads=[pm2], writes=[gs])
            S.op("act", lambda e: e.activation(out=glb.ap[:, j, :], in_=pm3.ap[:, 0:16], func=AF.Exp), reads=[pm3], writes=[glb])
            tc_ = t16.next()
            S.op("dve", lambda e: e.tensor_tensor(out=tc_.ap, in0=pm3.ap[:, 0:16], in1=G(3), op=ALU.subtract), reads=[pm3, gs], writes=[tc_])
            S.op("act", lambda e: e.activation(out=G(5), in_=tc_.ap, func=AF.Exp), reads=[tc_], writes=[gs])
            if kind == "s":
                S.op("dve", lambda e: e.tensor_tensor(out=gblk.ap, in0=G(0).unsqueeze(1).to_broadcast([128, 16, 16]),
                                                      in1=cBM.unsqueeze(2).to_broadcast([128, 16, 16]), op=ALU.mult), reads=[gs, c32], writes=[gblk])
                pb = rPB.next()
                S.op("pe", lambda e: e.matmul(pb.ap[:, 0:256], lhsT=cONES, rhs=gblk.ap.rearrange("p a b -> p (a b)"), start=True, stop=True),
                     reads=[c32, gblk], writes=[pb])
                S.op("act", lambda e: e.activation(out=gls.ap.rearrange("p a b -> p (a b)"), in_=pb.ap[:, 0:256], func=AF.Exp), reads=[pb], writes=[gls])

        onT = T(ybig.bitcast(BF16)[:, :16 * Ncol].rearrange("p (a b) -> p a b", a=16), [b for t in y_t for b in t.b])
        rPBg = Ring([PF[4], PF[5]])
        qTs = [aalloc([128, Ncol], BF16) for _ in range(2)]
        kTs = [aalloc([128, Ncol], BF16) for _ in range(2)]
        ktoks = [aalloc([128, NTl, 128], BF16) for _ in range(2)]
        vtoks = [aalloc([128, NTl, 128], BF16) for _ in range(4)]
        zss = [aalloc([128, NTl, 128], BF16) for _ in range(4)]
        umark = apos[0]
        wring = Ring([aalloc([128, 8, 128], BF16) for _ in range(4)])
        prering = Ring([aalloc([128, 3 + 1024], F32) for _ in range(2)])
        presring = Ring([aalloc([128, 16, 11], F32) for _ in range(2)])
        coring = Ring([aalloc([128, Ncol], F32) for _ in range(2)])
        sqring = Ring([aalloc([128, 512], BF16) for _ in range(1)])
        rnring = Ring([aalloc([128, 512], F32) for _ in range(1)])
        vTring = Ring([aalloc([128, Ncol], BF16) for _ in range(1)])
        uend = apos[0]
        apos[0] = umark
        chains = []
        cstart = []
        for c in range(4):
            cstart.append(apos[0])
            bb_ = aalloc([128, 8, 128], BF16)
            chains.append(dict(b=[T(bb_.ap[:, i_, :], bb_.b[i_ // 2:i_ // 2 + 1]) for i_ in range(8)],
                               f=[aalloc([128, 128], F32) for _ in range(4)],
                               x=[aalloc([128, 2, 128], F32) for _ in range(2)],
                               r=[aalloc([128, 128], F32) for _ in range(2)]))
        uend = max(uend, apos[0])
        if has_s:
            apos[0] = cstart[1]
            padw = aalloc([128, 16 * 136], BF16)
            padq = aalloc([128, 16 * 136], BF16)
            kpad = aalloc([128, 16, 128], BF16)
            s0f = aalloc([128, 8, 128], F32)
            s0b = aalloc([128, 8, 128], BF16)
            snw = aalloc([128, 8, 128], F32)
            uend = max(uend, apos[0])
        apos[0] = uend
        f128 = Ring(chains[0]["f"])
        b128 = Ring(chains[0]["b"])
        xx = Ring(chains[0]["x"])
        rR = Ring(chains[0]["r"])

        def proj_conv(w, cidx, silu=True):
            pre, pres, co = prering.next(), presring.next(), coring.next()
            for (c0, wd) in blocks:
                ps = rPBg.next()
                tl = [hT_t[t] for t in range(c0 // 128, (c0 + wd) // 128)]
                for k in range(8):
                    S.op("pe", lambda e, k=k: e.matmul(ps.ap[:, :wd], lhsT=w.ap[:, k, :], rhs=hTbig[:, k, c0:c0 + wd], start=(k == 0), stop=(k == 7)),
                         reads=[w] + tl, writes=[ps])
                if has_s and c0 == 0:
                    S.op("act", lambda e: e.activation(out=pres.ap[:, :, 3:11], in_=ps.ap[:, 0:128].rearrange("p (a b) -> p a b", a=16), func=AF.Copy),
                         reads=[ps], writes=[pres])
                else:
                    S.op("act", lambda e: e.activation(out=pre.ap[:, 3 + c0 - p0:3 + c0 - p0 + wd], in_=ps.ap[:, :wd], func=AF.Copy), reads=[ps], writes=[pre])
            if sbi == 0:
                S.op("dve", lambda e: e.memset(pre.ap[:, 0:3], 0.0), writes=[pre])
            else:
                S.op("dve", lambda e: e.tensor_copy(out=pre.ap[:, 0:3], in_=ctail.ap[:, cidx, :]), reads=[ctail], writes=[pre])
            if has_s:
                dma("sp", pres.ap[:, :, 0:3], cst_d[cidx * 128:(cidx + 1) * 128, :, :], writes=[pres])
            if sbi == 0:
                S.op("dve", lambda e: e.tensor_copy(out=ctail.ap[:, cidx, :], in_=pre.ap[:, Np:Np + 3]), reads=[pre], writes=[ctail])
            else:
                dma("sp", cp_d[cidx * 128:(cidx + 1) * 128, :], pre.ap[:, Np:Np + 3], reads=[pre], writes=[Buf()])
            if has_s:
                dma("sp", cso_d[cidx * 128:(cidx + 1) * 128, :, :], pres.ap[:, :, 8:11], reads=[pres], writes=[Buf()])
            views = [(co.ap[:, p0:Ncol], lambda tap: pre.ap[:, tap:tap + Np], pre)]
            if has_s:
                views.append((co.ap[:, 0:128].rearrange("p (a b) -> p a b", a=16), lambda tap: pres.ap[:, :, tap:tap + 8], pres))
            for (o_ap, src, srcT) in views:
                S.op("dve", lambda e: e.tensor_scalar(out=o_ap, in0=src(0), scalar1=cw.ap[:, cidx, 0:1], scalar2=None, op0=ALU.mult),
                     reads=[srcT, cw], writes=[co])
                for tap in range(1, 4):
                    S.op("dve", lambda e, tap=tap: e.scalar_tensor_tensor(out=o_ap, in0=src(tap), scalar=cw.ap[:, cidx, tap:tap + 1], in1=o_ap,
                                                                           op0=ALU.mult, op1=ALU.add), reads=[srcT, cw, co], writes=[co])
            S.op("act", lambda e: e.activation(out=co.ap, in_=co.ap, func=AF.Silu), reads=[co], writes=[co])
            return co, pre, pres

        def l2n(cpp, dst, scale):
            co, pre, pres = cpp
            sqf = vTring.items[0]
            S.op("dve", lambda e: e.tensor_tensor(out=sqf.ap, in0=co.ap, in1=co.ap, op=ALU.mult), reads=[co], writes=[sqf])
            for (c0, wd) in blocks:
                ps = rPBg.next()
                S.op("pe", lambda e: e.matmul(ps.ap[:, :wd], lhsT=onesb, rhs=sqf.ap[:, c0:c0 + wd], start=True, stop=True), reads=[cbf, sqf], writes=[ps])
                if has_s and c0 == 0:
                    rn_ap, rnT = pres.ap.rearrange("p a b -> p (a b)")[:, 0:128], pres
                else:
                    rn_ap, rnT = pre.ap[:, c0 - p0:c0 - p0 + wd], pre
                S.op("act", lambda e: e.activation(out=rn_ap, in_=ps.ap[:, :wd], func=AF.Ln, bias=1e-6, scale=1.0), reads=[ps], writes=[rnT])
                S.op("act", lambda e: e.activation(out=rn_ap, in_=rn_ap, func=AF.Exp, scale=-0.5), reads=[rnT], writes=[rnT])
                S.op("dve", lambda e: e.scalar_tensor_tensor(out=dst.ap[:, c0:c0 + wd], in0=co.ap[:, c0:c0 + wd], scalar=scale, in1=rn_ap,
                                                             op0=ALU.mult, op1=ALU.mult), reads=[co, rnT], writes=[dst])

        PTall = T(ptr_ap, ptrB)
        PMall = T(pm_ap, pmB)

        def to_tok(srcT, dst):
            j0 = 0
            while j0 < NTl:
                n = min(8, NTl - j0)
                for i in range(n):
                    S.op("pe", lambda e, i=i: e.transpose(out=ptr_ap[:, i, :], in_=srcT.ap[:, (j0 + i) * 128:(j0 + i + 1) * 128], identity=identb),
                         reads=[srcT, cbf], writes=[PTall])
                S.op("act", lambda e: e.activation(out=dst.ap[:, j0:j0 + n, :], in_=ptr_ap[:, 0:n, :], func=AF.Copy), reads=[PTall], writes=[dst])
                j0 += n

        def wload(col0):
            w = wring.next()
            dma("pool", w.ap, gwi_d[col0 // 128].rearrange("p (k n) -> p k n", k=8), writes=[w])
            return w

        def mm(out_t, out_ap, lhsT, rhs, R, start=True, stop=True):
            S.op("pe", lambda e: e.matmul(out_ap, lhsT=lhsT, rhs=rhs, start=start, stop=stop), reads=R, writes=[out_t])


        def zproj(w, dst):
            j0 = 0
            while j0 < NTl:
                n = min(4, NTl - j0)
                for i in range(n):
                    jj = j0 + i
                    for k in range(8):
                        mm(PMall, pm_ap[:, i * 128:(i + 1) * 128], hTbig[:, k, jj * 128:(jj + 1) * 128], w.ap[:, k, :], [hT_t[jj], w],
                           start=(k == 0), stop=(k == 7))
                S.op("act", lambda e: e.activation(out=dst.ap[:, j0:j0 + n, :], in_=pm_ap[:, 0:n * 128].rearrange("p (a b) -> p a b", a=n), func=AF.Silu),
                     reads=[PMall], writes=[dst])
                j0 += n

        def finish_o(po_ap, po, on, ob, zs, j, h):
            st = rstd_of(po_ap, po, 128, 128)
            S.op("dve", lambda e: e.scalar_tensor_tensor(out=on.ap, in0=po_ap, scalar=st.ap[:, 0:1], in1=onb.ap, op0=ALU.mult, op1=ALU.mult),
                 reads=[po, st, onb], writes=[on])
            S.op("dve", lambda e: e.tensor_tensor(out=ob.ap, in0=on.ap, in1=zs.ap[:, j, :], op=ALU.mult), reads=[on, zs], writes=[ob])

        def sample_state(h, qT, kT, ktok, vtok, zs):
            j = 0
            smp = True
            cols = slice(0, 128)
            G = lambda q: gs.ap[:, j, q, h:h + 1]
            pkq = rPH.next()
            mm(pkq, pkq.ap[:, 0:128], kT.ap[:, cols], qT.ap[:, cols], [kT, qT])
            mm(pkq, pkq.ap[:, 128:256], kT.ap[:, cols], kT.ap[:, cols], [kT])
            prow = rPM.next()
            mm(prow, prow.ap, gs.ap[:, j, 0, h:h + 1].to_broadcast([128, 128]), cUb, [gs, c32])
            E = f128.next()
            S.op("dve", lambda e: e.scalar_tensor_tensor(out=E.ap, in0=prow.ap, scalar=G(3), in1=cNMb, op0=ALU.subtract, op1=ALU.add),
                 reads=[prow, gs, c32], writes=[E])
            S.op("act", lambda e: e.activation(out=E.ap, in_=E.ap, func=AF.Exp), reads=[E], writes=[E])
            egb = f128.next()
            S.op("act", lambda e: e.activation(out=egb.ap, in_=prow.ap, func=AF.Exp), reads=[prow], writes=[egb])
            qkT = b128.next()
            S.op("dve", lambda e: e.tensor_tensor(out=qkT.ap, in0=pkq.ap[:, 0:128], in1=E.ap, op=ALU.mult), reads=[pkq, E], writes=[qkT])
            dS = f128.next()
            S.op("dve", lambda e: e.tensor_tensor(out=dS.ap, in0=E.ap, in1=cSTb, op=ALU.mult), reads=[E, c32], writes=[dS])
            X = xx.next()
            S.op("dve", lambda e: e.scalar_tensor_tensor(out=X.ap[:, 0, :], in0=pkq.ap[:, 128:256], scalar=G(2), in1=dS.ap, op0=ALU.mult, op1=ALU.mult),
                 reads=[pkq, gs, dS], writes=[X])
            pt = rPM.next()
            mm(pt, pt.ap, X.ap[:, 0, :], cID, [X, c32])
            S.op("act", lambda e: e.activation(out=X.ap[:, 1, :], in_=pt.ap, func=AF.Copy), reads=[pt], writes=[X])
            Rm = rR.next()
            S.op("dve", lambda e: e.tensor_tensor(out=Rm.ap, in0=X.ap[:, 0, :], in1=cID, op=ALU.add), reads=[X, c32], writes=[Rm])
            nlev = 2
            for lev in range(nlev):
                px = rPH.next()
                last = lev == nlev - 1
                mm(px, px.ap[:, 128:256], X.ap[:, 0, :], X.ap[:, 1, :], [X])
                if not last:
                    mm(px, px.ap[:, 0:128], X.ap[:, 1, :], X.ap[:, 0, :], [X])
                X2 = xx.next()
                if last:
                    S.op("act", lambda e: e.activation(out=X2.ap[:, 1, :], in_=px.ap[:, 128:256], func=AF.Copy), reads=[px], writes=[X2])
                else:
                    S.op("act", lambda e: e.activation(out=X2.ap, in_=px.ap.rearrange("p (a b) -> p a b", a=2), func=AF.Copy), reads=[px], writes=[X2])
                pr = rPM.next()
                mm(pr, pr.ap, X2.ap[:, 1, :], Rm.ap, [X2, Rm])
                R2 = rR.next()
                S.op("dve", lambda e: e.tensor_tensor(out=R2.ap, in0=pr.ap, in1=Rm.ap, op=ALU.add), reads=[pr, Rm], writes=[R2])
                Rm, X = R2, X2
            Rb = b128.next()
            S.op("act", lambda e: e.activation(out=Rb.ap, in_=Rm.ap, func=AF.Copy), reads=[Rm], writes=[Rb])
            Rm = Rb
            kg = b128.next()
            S.op("act", lambda e: e.mul(out=kg.ap, in_=ktok.ap[:, j, :], mul=G(4)), reads=[ktok, gs], writes=[kg])
            kd = b128.next()
            S.op("act", lambda e: e.mul(out=kd.ap, in_=ktok.ap[:, j, :], mul=G(5)), reads=[ktok, gs], writes=[kd])
            pw_ = rPM.next()
            mm(pw_, pw_.ap, kg.ap, Rm.ap, [kg, Rm])
            qd = b128.next()
            S.op("dve", lambda e: e.tensor_tensor(out=qd.ap, in0=qT.ap[:, cols], in1=egb.ap, op=ALU.mult), reads=[qT, egb], writes=[qd])
            vn = b128.next()
            if True:
                        pw3 = padw.ap[:, :].rearrange("p (a b) -> p a b", b=136)[:, :, 0:8]
                        pq3 = padq.ap[:, :].rearrange("p (a b) -> p a b", b=136)[:, :, 0:8]
                        S.op("dve", lambda e: e.tensor_scalar(out=pw3, in0=pw_.ap.rearrange("p (a b) -> p a b", a=16), scalar1=-1.0, scalar2=None, op0=ALU.mult),
                             reads=[pw_], writes=[padw])
                        S.op("dve", lambda e: e.tensor_copy(out=pq3, in_=qd.ap.rearrange("p (a b) -> p a b", a=16)), reads=[qd], writes=[padq])
                        S.op("dve", lambda e: e.tensor_tensor(out=kpad.ap, in0=kd.ap.unsqueeze(1).to_broadcast([128, 16, 128]),
                                                              in1=cBM.unsqueeze(2).to_broadcast([128, 16, 128]), op=ALU.mult), reads=[kd, c32], writes=[kpad])
                        pv = rPM.next()
                        po = rPH.next()
                        po_ap = po.ap[:, 0:128]
                        for half in range(2):
                            dma("sp", s0f.ap, rst_d[half * 8:(half + 1) * 8, h].rearrange("s k v -> k s v"), writes=[s0f])
                            dma("pool", s0b.ap, rst_d[half * 8:(half + 1) * 8, h].rearrange("s k v -> k s v"), writes=[s0b])
                            if half == 0:
                                mm(pv, pv.ap, Rm.ap, vtok.ap[:, j, :], [Rm, vtok], start=True, stop=False)
                            for sl in range(8):
                                sq_ = half * 8 + sl
                                mm(pv, pv.ap, padw.ap[:, sq_ * 128:(sq_ + 1) * 128], s0b.ap[:, sl, :], [padw, s0b], start=False, stop=(sq_ == 15))
                            for sl in range(8):
                                sq_ = half * 8 + sl
                                mm(po, po_ap, padq.ap[:, sq_ * 128:(sq_ + 1) * 128], s0b.ap[:, sl, :], [padq, s0b], start=(sq_ == 0), stop=False)
                            if half == 1:
                                S.op("act", lambda e: e.mul(out=vn.ap, in_=pv.ap, mul=G(1)), reads=[pv, gs], writes=[vn])
                                mm(po, po_ap, qkT.ap, vn.ap, [qkT, vn], start=False, stop=True)
                        for half in range(2):
                            dma("sp", s0f.ap, rst_d[half * 8:(half + 1) * 8, h].rearrange("s k v -> k s v"), writes=[s0f])
                            for qd4 in range(2):
                                pb = rPB.next()
                                for s4 in range(4):
                                    sq_ = half * 8 + qd4 * 4 + s4
                                    mm(pb, pb.ap[:, s4 * 128:(s4 + 1) * 128], kpad.ap[:, sq_, :], vn.ap, [kpad, vn])
                                sl0 = qd4 * 4
                                S.op("dve", lambda e: e.tensor_tensor(out=snw.ap[:, sl0:sl0 + 4, :], in0=s0f.ap[:, sl0:sl0 + 4, :],
                                                                      in1=gls.ap[:, half * 8 + sl0:half * 8 + sl0 + 4, h:h + 1].to_broadcast([128, 4, 128]), op=ALU.mult),
                                     reads=[s0f, gls], writes=[snw])
                                S.op("dve", lambda e: e.tensor_tensor(out=snw.ap[:, sl0:sl0 + 4, :], in0=snw.ap[:, sl0:sl0 + 4, :],
                                                                      in1=pb.ap.rearrange("p (a b) -> p a b", a=4), op=ALU.add), reads=[snw, pb], writes=[snw])
                            dma("sp", rso_d[half * 8:(half + 1) * 8, h].rearrange("s k v -> k s v"), snw.ap, reads=[snw], writes=[Buf()])

            on = f128.next()
            ob = b128.next()
            finish_o(po_ap, po, on, ob, zs, j, h)
            pt2 = rPTq.next()
            S.op("pe", lambda e: e.transpose(out=pt2.ap, in_=ob.ap, identity=identb), reads=[ob, cbf], writes=[pt2])
            S.op("act", lambda e: e.activation(out=onT.ap[:, h, cols], in_=pt2.ap, func=AF.Copy), reads=[pt2], writes=[onT])

        def chain(c, h, qT, kT, ktok, vtok, zs):
            cb = chains[c]
            QB = [T(banks[c][:, q * 128:(q + 1) * 128], PF[c].b) for q in range(4)]
            F0 = T(banks[c][:, 256:384], PF[c].b)
            F1 = T(banks[c][:, 384:512], PF[c].b)
            F01 = T(banks[c][:, 256:512].rearrange("p (a b) -> p a b", a=2), PF[c].b)
            qkT, Rb, kg, kd, qd, vn, nw, ob = cb["b"]
            E, egb, dS, on = cb["f"]
            for j, (kind, pi) in zip(tiles, kinds):
                if kind == "s":
                    continue
                cols = slice(j * 128, (j + 1) * 128)
                G = lambda q: gs.ap[:, j, q, h:h + 1]
                mm(QB[0], QB[0].ap, kT.ap[:, cols], qT.ap[:, cols], [kT, qT])
                mm(QB[1], QB[1].ap, kT.ap[:, cols], kT.ap[:, cols], [kT])
                mm(F0, F0.ap, gs.ap[:, j, 0, h:h + 1].to_broadcast([128, 128]), cU, [gs, c32])
                yield
                S.op("dve", lambda e: e.scalar_tensor_tensor(out=E.ap, in0=F0.ap, scalar=G(3), in1=cNM, op0=ALU.subtract, op1=ALU.add),
                     reads=[F0, gs, c32], writes=[E])
                S.op("act", lambda e: e.activation(out=egb.ap, in_=F0.ap, func=AF.Exp), reads=[F0], writes=[egb])
                yield
                S.op("act", lambda e: e.activation(out=E.ap, in_=E.ap, func=AF.Exp), reads=[E], writes=[E])
                yield
                S.op("dve", lambda e: e.tensor_tensor(out=qkT.ap, in0=QB[0].ap, in1=E.ap, op=ALU.mult), reads=[QB[0], E], writes=[qkT])
                S.op("dve", lambda e: e.tensor_tensor(out=dS.ap, in0=E.ap, in1=cST, op=ALU.mult), reads=[E, c32], writes=[dS])
                X = cb["x"][0]
                S.op("dve", lambda e: e.scalar_tensor_tensor(out=X.ap[:, 0, :], in0=QB[1].ap, scalar=G(2), in1=dS.ap, op0=ALU.mult, op1=ALU.mult),
                     reads=[QB[1], gs, dS], writes=[X])
                S.op("dve", lambda e: e.tensor_tensor(out=qd.ap, in0=qT.ap[:, cols], in1=egb.ap, op=ALU.mult), reads=[qT, egb], writes=[qd])
                yield
                mm(F1, F1.ap, X.ap[:, 0, :], cID, [X, c32])
                Rm = cb["r"][0]
                S.op("dve", lambda e: e.tensor_tensor(out=Rm.ap, in0=X.ap[:, 0, :], in1=cID, op=ALU.add), reads=[X, c32], writes=[Rm])
                S.op("act", lambda e: e.mul(out=kg.ap, in_=ktok.ap[:, j, :], mul=G(4)), reads=[ktok, gs], writes=[kg])
                S.op("act", lambda e: e.mul(out=kd.ap, in_=ktok.ap[:, j, :], mul=G(5)), reads=[ktok, gs], writes=[kd])
                yield
                S.op("act", lambda e: e.activation(out=X.ap[:, 1, :], in_=F1.ap, func=AF.Copy), reads=[F1], writes=[X])
                yield
                for lev in range(6):
                    last = lev == 5
                    Xn = cb["x"][(lev + 1) % 2]
                    Rn = cb["r"][(lev + 1) % 2]
                    mm(F1, F1.ap, X.ap[:, 0, :], X.ap[:, 1, :], [X])
                    if not last:
                        mm(F0, F0.ap, X.ap[:, 1, :], X.ap[:, 0, :], [X])
                    yield
                    if last:
                        S.op("act", lambda e: e.activation(out=Xn.ap[:, 1, :], in_=F1.ap, func=AF.Copy), reads=[F1], writes=[Xn])
                    else:
                        S.op("act", lambda e: e.activation(out=Xn.ap, in_=F01.ap, func=AF.Copy), reads=[F01], writes=[Xn])
                    yield
                    pr = F0
                    mm(pr, pr.ap, Xn.ap[:, 1, :], Rm.ap, [Xn, Rm])
                    yield
                    S.op("dve", lambda e: e.tensor_tensor(out=Rn.ap, in0=pr.ap, in1=Rm.ap, op=ALU.add), reads=[pr, Rm], writes=[Rn])
                    yield
                    X, Rm = Xn, Rn
                S.op("act", lambda e: e.activation(out=Rb.ap, in_=Rm.ap, func=AF.Copy), reads=[Rm], writes=[Rb])
                yield
                mm(QB[2], QB[2].ap, kg.ap, Rb.ap, [kg, Rb])
                yield
                S.op("dve", lambda e: e.tensor_scalar(out=nw.ap, in0=QB[2].ap, scalar1=-1.0, scalar2=None, op0=ALU.mult), reads=[QB[2]], writes=[nw])
                yield
                mm(QB[3], QB[3].ap, Rb.ap, vtok.ap[:, j, :], [Rb, vtok], start=True, stop=False)
                mm(QB[3], QB[3].ap, nw.ap, Sbf[h].ap, [nw, Sbf[h]], start=False, stop=True)
                yield
                S.op("act", lambda e: e.mul(out=vn.ap, in_=QB[3].ap, mul=G(1)), reads=[QB[3], gs], writes=[vn])
                yield
                mm(QB[0], QB[0].ap, qd.ap, Sbf[h].ap, [qd, Sbf[h]], start=True, stop=False)
                mm(QB[0], QB[0].ap, qkT.ap, vn.ap, [qkT, vn], start=False, stop=True)
                mm(QB[1], QB[1].ap, kd.ap, vn.ap, [kd, vn])
                yield
                S.op("dve", lambda e: e.scalar_tensor_tensor(out=S32[h].ap, in0=S32[h].ap, scalar=glb.ap[:, j, h:h + 1], in1=QB[1].ap,
                                                             op0=ALU.mult, op1=ALU.add), reads=[S32[h], glb, QB[1]], writes=[S32[h]])
                st = rstd_of(QB[0].ap, QB[0], 128, 128)
                yield
                S.op("act", lambda e: e.activation(out=Sbf[h].ap, in_=S32[h].ap, func=AF.Copy), reads=[S32[h]], writes=[Sbf[h]])
                S.op("dve", lambda e: e.scalar_tensor_tensor(out=on.ap, in0=QB[0].ap, scalar=st.ap[:, 0:1], in1=onb.ap, op0=ALU.mult, op1=ALU.mult),
                     reads=[QB[0], st, onb], writes=[on])
                S.op("dve", lambda e: e.tensor_tensor(out=ob.ap, in0=on.ap, in1=zs.ap[:, j, :], op=ALU.mult), reads=[on, zs], writes=[ob])
                if pi == 15:
                    dma("sp", rp_d[h], S32[h].ap, reads=[S32[h]], writes=[Buf()])
                yield
                pt2 = rPTq.next()
                S.op("pe", lambda e: e.transpose(out=pt2.ap, in_=ob.ap, identity=identb), reads=[ob, cbf], writes=[pt2])
                yield
                S.op("act", lambda e: e.activation(out=onT.ap[:, h, cols], in_=pt2.ap, func=AF.Copy), reads=[pt2], writes=[onT])
                yield

        for grp in range(4):
            heads = [4 * grp + i for i in range(4)]
            for kk in range(2):
                kh = 2 * grp + kk
                wq = wload(kh * 128)
                wk = wload(1024 + kh * 128)
                l2n(proj_conv(wq, kh), qTs[kk], 128.0 ** -0.5)
                l2n(proj_conv(wk, 8 + kh), kTs[kk], 1.0)
                to_tok(kTs[kk], ktoks[kk])
            for c, h in enumerate(heads):
                wv = wload(2048 + h * 128)
                wz = wload(4096 + h * 128)
                co = proj_conv(wv, 16 + h)[0]
                vT = vTring.next()
                S.op("act", lambda e: e.activation(out=vT.ap, in_=co.ap, func=AF.Copy), reads=[co], writes=[vT])
                to_tok(vT, vtoks[c])
                zproj(wz, zss[c])
                if sbi == 0:
                    S.op("dve", lambda e: e.memset(S32[h].ap, 0.0), writes=[S32[h]])
                    S.op("dve", lambda e: e.memset(Sbf[h].ap, 0.0), writes=[Sbf[h]])
            if has_s:
                S.op("dve", lambda e: e.memset(padw.ap, 0.0), writes=[padw])
                S.op("dve", lambda e: e.memset(padq.ap, 0.0), writes=[padq])
                for c, h in enumerate(heads):
                    sample_state(h, qTs[c // 2], kTs[c // 2], ktoks[c // 2], vtoks[c], zss[c])
            allg = [chain(c, h, qTs[c // 2], kTs[c // 2], ktoks[c // 2], vtoks[c], zss[c]) for c, h in enumerate(heads)]
            gw = 4
            for g0 in range(0, 4, gw):
              gens = allg[g0:g0 + gw]
              while gens:
                alive = []
                for g_ in gens:
                    try:
                        next(g_)
                        alive.append(g_)
                    except StopIteration:
                        pass
                gens = alive

        if dbg_d is not None and sbi == 0:
            dma("sp", dbg_d, ybig, reads=[onT], writes=[Buf()])
        areset()
        gpost = aalloc([128, D], F32)
        dma("sp", gpost.ap, nmo_d[1:2, :].partition_broadcast(128), writes=[gpost])
        wo_all = [aalloc([128, 4, D], BF16) for _ in range(4)]
        for hg in range(4):
            dma("pool", wo_all[hg].ap, gwo_d[hg * 512:(hg + 1) * 512, :].rearrange("(c p) n -> p c n", p=128), writes=[wo_all[hg]])
        for j in tiles:
            pys = [rPF.next(), rPF.next()]
            for hg in range(4):
                wo = wo_all[hg]
                for hh in range(4):
                    h = hg * 4 + hh
                    for dh in range(2):
                        mm(pys[dh], pys[dh].ap, onT.ap[:, h, j * 128:(j + 1) * 128], wo.ap[:, hh, dh * 512:(dh + 1) * 512], [onT, wo],
                           start=(h == 0), stop=(h == 15))
            m32 = rTmp.next()
            for dh in range(2):
                S.op("act", lambda e, dh=dh: e.activation(out=m32.ap[:, dh * 512:(dh + 1) * 512], in_=pys[dh].ap, func=AF.Copy), reads=[pys[dh]], writes=[m32])
            post_norm_add(j, m32.ap, m32, gpost)

    for sbi in range(2):
        if sbi == 0:
            kinds = [("s", None)] + [("p", i) for i in range(8)]
            blocks = [(0, 128), (128, 512), (640, 512)]
        else:
            kinds = [("p", 8 + i) for i in range(8)]
            blocks = [(0, 512), (512, 512)]
        tiles = list(range(len(kinds)))
        for j, (kind, pi) in zip(tiles, kinds):
            src = xs_d if kind == "s" else xp_d[pi * 128:(pi + 1) * 128, :]
            dma("sp", x_t[j].ap, src, writes=[x_t[j]])
        pool_phase(sbi, tiles, kinds)
        if stop_after not in ("pool", "nod2d"):
            ffn(0, tiles, blocks)
        if stop_after not in ("pool", "nod2d", "l0"):
            gdn_phase(sbi, tiles, kinds, blocks)
            if stop_after != "gdn":
                ffn(1, tiles, blocks)
        for j, (kind, pi) in zip(tiles, kinds):
            dst = ys_d if kind == "s" else yp_d[pi * 128:(pi + 1) * 128, :]
            dma("sp", dst, x_t[j].ap, reads=[x_t[j]], writes=[Buf()])

    with nc.allow_low_precision("bf16 matmul operands, fp32 accumulation"):
        S.emit()
    return nc


_NC_CACHE = {}


def make_in_maps(inp):
    c32, pm = _consts()
    f = lambda a: np.ascontiguousarray(np.asarray(a, dtype=np.float32))
    shared = {
        "nmp": f(inp["norm_mix_pre"]), "nmo": f(inp["norm_mix_post"]),
        "nfp": f(inp["norm_ffn_pre"]), "nfo": f(inp["norm_ffn_post"]),
        "pw": f(inp["pool_w"][0]), "psc": f(inp["pool_scale"]),
        "gwi": f(np.asarray(inp["gdn_w_in"][0])[:, :6144].reshape(8, 128, 48, 128).transpose(2, 1, 0, 3).reshape(48, 128, 1024)),
        "gwba": f(np.asarray(inp["gdn_w_in"][0])[:, 6144:6176]), "gcw": f(np.asarray(inp["gdn_conv_w"][0]).T),
        "galog": f(inp["gdn_a_log"]), "gdtb": f(inp["gdn_dt_bias"]), "gon": f(inp["gdn_o_norm"]),
        "gwo": f(inp["gdn_w_out"][0]), "fwi": f(inp["ffn_w_in"]), "fwo": f(inp["ffn_w_out"]),
        "c32": c32, "pm": pm,
    }
    maps = []
    for c in range(NCORE):
        sl = slice(16 * c, 16 * (c + 1))
        m = dict(shared)
        m["xp"] = f(inp["x_prompt"][c])
        m["xs"] = f(np.asarray(inp["x_sample"][sl]).reshape(128, D))
        m["pst"] = f(np.asarray(inp["state_pool"][0, sl]).reshape(240, D))
        m["cst"] = f(np.asarray(inp["state_gdn_conv"][0, sl]).transpose(2, 0, 1))
        m["rst"] = f(inp["state_gdn_rec"][0, sl])
        maps.append(m)
    return maps


def kernel(**inp):
    if "nc" not in _NC_CACHE:
        _NC_CACHE["nc"] = build()
    nc = _NC_CACHE["nc"]
    maps = make_in_maps(inp)
    res = run_bass_kernel_spmd(nc, maps, core_ids=list(range(NCORE)))
    R = res.results
    y_p = np.stack([R[c]["yp"] for c in range(NCORE)]).reshape(8, 2048, D)
    y_s = np.concatenate([R[c]["ys"].reshape(16, 8, D) for c in range(NCORE)])
    pool_p = np.stack([R[c]["pp"] for c in range(NCORE)])[None]
    pool_s = np.concatenate([R[c]["pso"] for c in range(NCORE)])[None]
    conv_p = np.stack([R[c]["cp"].T for c in range(NCORE)])[None]
    conv_s = np.concatenate([R[c]["cso"].transpose(1, 2, 0) for c in range(NCORE)])[None]
    rec_p = np.stack([R[c]["rp"] for c in range(NCORE)])[None]
    rec_s = np.concatenate([R[c]["rso"] for c in range(NCORE)])[None]
    return (y_p, y_s, pool_p, pool_s, conv_p, conv_s, rec_p, rec_s)
```
